# Optimizing a Trainium2 kernel written in Bass

```python
import jax
import jax.numpy as jnp
from jax import lax
import numpy as np

D_MODEL = 1024
BATCH = 4
SEQ = 4096
DEPTH = 1

GRID_W = 64
CTX_LEN = 256
N_DIR = 2
GDN_HEADS = 8
GDN_DK = 128
GDN_DV = 128
GDN_CHUNK = 64
CONV_K = 3
HGRN_HEADS = 8
HGRN_DK = 128
HGRN_DV = 128
HGRN_CHUNK = 64
D_FF = 4 * D_MODEL
EPS = 1e-6
QKV_W = GDN_HEADS * (2 * GDN_DK + GDN_DV)
IN_WIDTHS = (
    QKV_W,
    GDN_HEADS * GDN_DV,
    N_DIR * GDN_HEADS,
    N_DIR * GDN_HEADS,
    N_DIR * HGRN_HEADS * HGRN_DK,
    HGRN_HEADS * HGRN_DK,
    HGRN_HEADS * HGRN_DV,
    HGRN_HEADS * HGRN_DV,
    2 * D_MODEL,
)
IN_TOTAL = sum(IN_WIDTHS)

kernel_name = "hybrid_gdn_hgrn2_dit_block"


def _rms(t, g):
    tf = t.astype(jnp.float32)
    tf = tf * lax.rsqrt(jnp.mean(tf * tf, axis=-1, keepdims=True) + EPS)
    return (tf * g.astype(jnp.float32)).astype(t.dtype)


def _l2n(t):
    return t * lax.rsqrt(jnp.sum(t * t, axis=-1, keepdims=True) + EPS)


def _modulation(cond, w_ada, b_ada):
    m = jax.nn.silu(cond) @ w_ada + b_ada
    return jnp.split(m[..., None, :], 6, axis=-1)


def _to_chunks(t, chunk):
    b, T, h = t.shape[:3]
    t = t.reshape(b, T // chunk, chunk, h, *t.shape[3:])
    return jnp.moveaxis(t, 3, 1)


def _from_chunks(o):
    n, b, h, c, d = o.shape
    return jnp.transpose(o, (1, 0, 3, 2, 4)).reshape(b, n * c, h, d)


def _gdn_scan(q, k, v, log_a, beta, s0):
    C = GDN_CHUNK
    dv = v.shape[-1]
    qc, kc, vc = _to_chunks(q, C), _to_chunks(k, C), _to_chunks(v, C)
    g = jnp.cumsum(_to_chunks(log_a, C), axis=-1)
    bc = _to_chunks(beta, C)
    incl = jnp.tril(jnp.ones((C, C), bool))
    strict = jnp.tril(jnp.ones((C, C), bool), -1)
    L = jnp.exp(jnp.where(incl, g[..., :, None] - g[..., None, :], -jnp.inf))
    A = jnp.where(strict, bc[..., :, None] * jnp.einsum('bhnik,bhnjk->bhnij', kc, kc) * L, 0.0)
    rhs = jnp.concatenate([vc * bc[..., None], kc * (bc * jnp.exp(g))[..., None]], axis=-1)
    sol = lax.linalg.triangular_solve(A + jnp.eye(C, dtype=A.dtype), rhs,
                                      left_side=True, lower=True, unit_diagonal=True)
    u, w = sol[..., :dv], sol[..., dv:]
    attn = jnp.where(incl, jnp.einsum('bhnik,bhnjk->bhnij', qc, kc) * L, 0.0)
    qg = qc * jnp.exp(g)[..., None]
    g_last = g[..., -1:]
    kdec = kc * jnp.exp(g_last - g)[..., None]
    dlast = jnp.exp(g_last[..., 0])

    def step(S, xs):
        u_n, w_n, attn_n, qg_n, kdec_n, d_n = xs
        v_new = u_n - jnp.einsum('bhck,bhkv->bhcv', w_n, S)
        o = jnp.einsum('bhck,bhkv->bhcv', qg_n, S) + jnp.einsum('bhij,bhjv->bhiv', attn_n, v_new)
        S = d_n[..., None, None] * S + jnp.einsum('bhck,bhcv->bhkv', kdec_n, v_new)
        return S, o

    xs = tuple(jnp.moveaxis(t, 2, 0) for t in (u, w, attn, qg, kdec, dlast))
    s_fin, o = lax.scan(step, s0, xs)
    return _from_chunks(o), s_fin


def _hgrn2_scan(q, k, v, log_f, s0):
    C = HGRN_CHUNK
    qc, kc, vc = _to_chunks(q, C), _to_chunks(k, C), _to_chunks(v, C)
    b = jnp.cumsum(_to_chunks(log_f, C), axis=3)
    incl = jnp.tril(jnp.ones((C, C), bool))[:, :, None]

    def step(S, xs):
        q_n, k_n, v_n, b_n = xs
        o_inter = jnp.einsum('bhck,bhkv->bhcv', q_n * jnp.exp(b_n), S)
        dec = jnp.exp(jnp.where(incl, b_n[:, :, :, None, :] - b_n[:, :, None, :, :], -jnp.inf))
        a = jnp.einsum('bhtk,bhsk,bhtsk->bhts', q_n, k_n, dec)
        o = o_inter + jnp.einsum('bhts,bhsv->bhtv', a, v_n)
        b_last = b_n[:, :, -1:]
        S = jnp.exp(b_last[:, :, 0])[..., None] * S + jnp.einsum(
            'bhsk,bhsv->bhkv', k_n * jnp.exp(b_last - b_n), v_n)
        return S, o

    xs = tuple(jnp.moveaxis(t, 2, 0) for t in (qc, kc, vc, b))
    s_fin, o = lax.scan(step, s0, xs)
    return _from_chunks(o), s_fin


def _bidir(scan_fn, fwd_args, bwd_args, s0):
    o_f, s_f = scan_fn(*fwd_args, s0[0])
    o_b, s_b = scan_fn(*(jnp.flip(t, 1) for t in bwd_args), s0[1])
    return o_f + jnp.flip(o_b, 1), jnp.stack([s_f, s_b])


def _conv_latent(t, w):
    B, T, Cc = t.shape
    rows = T // GRID_W
    y = lax.conv_general_dilated(
        t.reshape(B, rows, GRID_W, Cc), w[:, :, None, :].astype(t.dtype),
        window_strides=(1, 1), padding='SAME',
        dimension_numbers=('NHWC', 'HWIO', 'NHWC'), feature_group_count=Cc)
    return jax.nn.silu(y.reshape(B, T, Cc))


def _conv_context(t, w):
    taps = w[CONV_K // 2].astype(t.dtype)
    pad = CONV_K // 2
    T = t.shape[1]
    tp = jnp.pad(t, ((0, 0), (pad, pad), (0, 0)))
    y = sum(taps[j] * tp[:, j:j + T] for j in range(CONV_K))
    return jax.nn.silu(y)


def _mixer_core(h, conv_fn, s0_a, s0_b, w_in, conv_w, a_log, dt_bias, lb):
    B, T, _ = h.shape
    f32 = jnp.float32
    z = h @ w_in
    qkv, g_a, a_in, b_in, f_in, q_b, i_b, g_b, mg = jnp.split(
        z, np.cumsum(IN_WIDTHS)[:-1].tolist(), axis=-1)
    qkv = conv_fn(qkv, conv_w).astype(f32)
    q_a, k_a, v_a = jnp.split(qkv, [GDN_HEADS * GDN_DK, 2 * GDN_HEADS * GDN_DK], axis=-1)
    q_a = _l2n(q_a.reshape(B, T, GDN_HEADS, GDN_DK)) * GDN_DK ** -0.5
    k_a = _l2n(k_a.reshape(B, T, GDN_HEADS, GDN_DK))
    v_a = v_a.reshape(B, T, GDN_HEADS, GDN_DV)
    a_in = a_in.astype(f32).reshape(B, T, N_DIR, GDN_HEADS)
    log_a = -jnp.exp(a_log.astype(f32)) * jax.nn.softplus(a_in + dt_bias.astype(f32))
    beta = jax.nn.sigmoid(b_in.astype(f32).reshape(B, T, N_DIR, GDN_HEADS))
    o_a, s_a = _bidir(_gdn_scan,
                      (q_a, k_a, v_a, log_a[:, :, 0], beta[:, :, 0]),
                      (q_a, k_a, v_a, log_a[:, :, 1], beta[:, :, 1]), s0_a)
    f_z = f_in.astype(f32).reshape(B, T, N_DIR, HGRN_HEADS, HGRN_DK)
    lb = lb.reshape(N_DIR, HGRN_HEADS, HGRN_DK)
    log_f = jnp.log(lb + (1.0 - lb) * jax.nn.sigmoid(f_z))
    k_b = (1.0 - lb) * jax.nn.sigmoid(-f_z)
    q_bh = jax.nn.silu(q_b.astype(f32)).reshape(B, T, HGRN_HEADS, HGRN_DK) * HGRN_DK ** -0.5
    v_bh = i_b.astype(f32).reshape(B, T, HGRN_HEADS, HGRN_DV)
    o_b, s_b = _bidir(_hgrn2_scan,
                      (q_bh, k_b[:, :, 0], v_bh, log_f[:, :, 0]),
                      (q_bh, k_b[:, :, 1], v_bh, log_f[:, :, 1]), s0_b)
    return o_a.astype(h.dtype), o_b.astype(h.dtype), g_a, g_b, mg, s_a, s_b


def _mixer_out(o_a, o_b, g_a, g_b, mg, gdn_norm_w, hgrn_norm_w, w_branch_a, w_branch_b, w_out):
    B, T = o_a.shape[:2]
    y_a = (_rms(o_a, gdn_norm_w) * jax.nn.silu(g_a.reshape(B, T, GDN_HEADS, GDN_DV))).reshape(B, T, -1)
    y_b = (_rms(o_b, hgrn_norm_w) * jax.nn.silu(g_b.reshape(B, T, HGRN_HEADS, HGRN_DV))).reshape(B, T, -1)
    gate_a, gate_b = jnp.split(jax.nn.sigmoid(mg), 2, axis=-1)
    return (gate_a * (y_a @ w_branch_a) + gate_b * (y_b @ w_branch_b)) @ w_out


def _sq_relu_mlp(h, w1, w2):
    return jnp.square(jax.nn.relu(h @ w1)) @ w2


def setup_inputs(seed: int = 0) -> dict:
    key = jax.random.key(seed)
    ks = jax.random.split(key, 20)
    f32 = jnp.float32

    def nrm(k, shape, scale):
        return scale * jax.random.normal(k, shape, f32)

    dt = jnp.exp(jax.random.uniform(ks[10], (DEPTH, N_DIR, GDN_HEADS), f32,
                                    float(np.log(1e-3)), float(np.log(1e-1))))
    return {
        "x": nrm(ks[0], (BATCH, SEQ, D_MODEL), 1.0),
        "c": nrm(ks[1], (BATCH, D_MODEL), 1.0),
        "ctx": nrm(ks[2], (BATCH, CTX_LEN, D_MODEL), 1.0),
        "c_ctx": nrm(ks[3], (D_MODEL,), 1.0),
        "w_ada": nrm(ks[4], (DEPTH, D_MODEL, 6 * D_MODEL), 0.5 * D_MODEL ** -0.5),
        "b_ada": nrm(ks[5], (DEPTH, 6 * D_MODEL), 0.02),
        "norm_g": 1.0 + nrm(ks[6], (DEPTH, 4, D_MODEL), 0.02),
        "w_in": nrm(ks[7], (DEPTH, D_MODEL, IN_TOTAL), D_MODEL ** -0.5),
        "conv_w": nrm(ks[8], (DEPTH, CONV_K, CONV_K, QKV_W), 1.0 / CONV_K),
        "gdn_a_log": jnp.log(jax.random.uniform(ks[9], (DEPTH, N_DIR, GDN_HEADS), f32, 1.0, 16.0)),
        "gdn_dt_bias": dt + jnp.log(-jnp.expm1(-dt)),
        "gdn_norm_w": 1.0 + nrm(ks[11], (DEPTH, GDN_DV), 0.02),
        "hgrn_lb_logits": nrm(ks[12], (DEPTH + 1, N_DIR, HGRN_HEADS * HGRN_DK), 0.5),
        "hgrn_norm_w": 1.0 + nrm(ks[13], (DEPTH, HGRN_DV), 0.02),
        "w_branch_a": nrm(ks[14], (DEPTH, GDN_HEADS * GDN_DV, D_MODEL), (GDN_HEADS * GDN_DV) ** -0.5),
        "w_branch_b": nrm(ks[15], (DEPTH, HGRN_HEADS * HGRN_DV, D_MODEL), (HGRN_HEADS * HGRN_DV) ** -0.5),
        "w_out": nrm(ks[16], (DEPTH, D_MODEL, D_MODEL), D_MODEL ** -0.5),
        "w_mlp_in": nrm(ks[17], (DEPTH, D_MODEL, D_FF), D_MODEL ** -0.5),
        "w_mlp_out": nrm(ks[18], (DEPTH, D_FF, D_MODEL), D_FF ** -0.5),
    }


def reference(x, c, ctx, c_ctx, w_ada, b_ada, norm_g, w_in, conv_w, gdn_a_log, gdn_dt_bias,
              gdn_norm_w, hgrn_lb_logits, hgrn_norm_w, w_branch_a, w_branch_b, w_out,
              w_mlp_in, w_mlp_out):
    B = x.shape[0]
    f32 = jnp.float32
    lb_all = jnp.cumsum(jax.nn.softmax(hgrn_lb_logits.astype(f32), axis=0), axis=0)
    zeros_a = jnp.zeros((N_DIR, B, GDN_HEADS, GDN_DK, GDN_DV), f32)
    zeros_b = jnp.zeros((N_DIR, B, HGRN_HEADS, HGRN_DK, HGRN_DV), f32)
    for l in range(DEPTH):
        sh_mx, sc_mx, gt_mx, sh_ff, sc_ff, gt_ff = _modulation(c, w_ada[l], b_ada[l])
        csh_mx, csc_mx, cgt_mx, csh_ff, csc_ff, cgt_ff = _modulation(c_ctx, w_ada[l], b_ada[l])
        mix_p = (w_in[l], conv_w[l], gdn_a_log[l], gdn_dt_bias[l], lb_all[l])
        out_p = (gdn_norm_w[l], hgrn_norm_w[l], w_branch_a[l], w_branch_b[l], w_out[l])
        hc = _rms(ctx, norm_g[l, 0]) * (1.0 + csc_mx) + csh_mx
        ca, cb, cga, cgb, cmg, s_a, s_b = _mixer_core(hc, _conv_context, zeros_a, zeros_b, *mix_p)
        hx = _rms(x, norm_g[l, 0]) * (1.0 + sc_mx) + sh_mx
        xa, xb, xga, xgb, xmg, _, _ = _mixer_core(hx, _conv_latent, s_a, s_b, *mix_p)
        x = x + gt_mx * _rms(_mixer_out(xa, xb, xga, xgb, xmg, *out_p), norm_g[l, 1])
        hx = _rms(x, norm_g[l, 2]) * (1.0 + sc_ff) + sh_ff
        x = x + gt_ff * _rms(_sq_relu_mlp(hx, w_mlp_in[l], w_mlp_out[l]), norm_g[l, 3])
        if l < DEPTH - 1:
            ctx = ctx + cgt_mx * _rms(_mixer_out(ca, cb, cga, cgb, cmg, *out_p), norm_g[l, 1])
            hc = _rms(ctx, norm_g[l, 2]) * (1.0 + csc_ff) + csh_ff
            ctx = ctx + cgt_ff * _rms(_sq_relu_mlp(hc, w_mlp_in[l], w_mlp_out[l]), norm_g[l, 3])
    return x
```

```python
import numpy as np
from contextlib import ExitStack
import concourse.bass as bass
import concourse.mybir as mybir
from concourse.bass_utils import run_bass_kernel_spmd

F32 = mybir.dt.float32
BF16 = mybir.dt.bfloat16
AF = mybir.ActivationFunctionType
ALU = mybir.AluOpType
AX = mybir.AxisListType

D = 1024
NH = 8
EPS = 1e-6
NOWN = 16
NOTH = 16
NLOC = NOWN + NOTH
NCTX = 2
NT = NCTX + NLOC
TOK0 = NCTX * 128
NTOK = NT * 128
NEG = -30000.0
IN_W = 11296
O_Q, O_K, O_V, O_GA, O_A, O_B, O_F, O_QB, O_IB, O_GB, O_MG = (
    0, 1024, 2048, 3072, 4096, 4112, 4128, 6176, 7200, 8224, 9248)

C_I, C_ONE, C_UF, C_UB, C_NSF, C_NSB, C_NIF, C_NIB, C_HMF, C_HMB, C_RST, C_SUB = range(12)
NCONST = 12


class R:
    __slots__ = ("w", "r")

    def __init__(self):
        self.w = None
        self.r = {}


class V:
    def __init__(self, ap, res, psum=False):
        self.ap = ap
        self.res = res
        self.psum = psum

    def __getitem__(self, k):
        return V(self.ap[k], self.res, self.psum)

    def bc(self, shape):
        return V(self.ap.to_broadcast(list(shape)), self.res, self.psum)

    def re(self, s, **kw):
        return V(self.ap.rearrange(s, **kw), self.res, self.psum)


def _flat(vs):
    out = []
    for v in vs:
        if isinstance(v, V):
            if isinstance(v.res, tuple):
                out.extend(v.res)
            else:
                out.append(v.res)
    return out


class KB:
    def __init__(self, nc, es):
        self.nc = nc
        self.es = es
        self.eng = {"pe": nc.tensor, "act": nc.scalar, "dve": nc.vector,
                    "pool": nc.gpsimd, "sp": nc.sync}
        self.sem = {n: es.enter_context(nc.semaphore("s_" + n)) for n in self.eng}
        self.cnt = {n: 0 for n in self.eng}
        self.known = {n: {} for n in self.eng}
        self.nslots = {"sp": 24, "pool": 12, "act": 4}
        self.dsem = {}
        self.dval = {}
        self.dnext = {q: 0 for q in self.nslots}
        for q, n in self.nslots.items():
            for i in range(n):
                key = ("d", q, i)
                self.dsem[key] = es.enter_context(nc.semaphore("d_%s_%d" % (q, i)))
                self.dval[key] = 0
        self.out_toks = []
        self.uid = 0
        self.ninst = 0

    def sb(self, shape, dt=F32, name=None):
        self.uid += 1
        t = self.es.enter_context(self.nc.sbuf_tensor("sb_" + (name or ("t%d" % self.uid)), list(shape), dt))
        return t

    def sbv(self, shape, dt=F32, name=None):
        t = self.sb(shape, dt, name)
        return V(t[tuple(slice(None) for _ in shape)], R())

    def ps(self, shape, dt=F32, name=None):
        self.uid += 1
        t = self.es.enter_context(self.nc.psum_tensor("ps_" + (name or ("p%d" % self.uid)), list(shape), dt))
        return V(t[tuple(slice(None) for _ in shape)], R(), True)

    def ring(self, n, shape, dt=F32, psum=False):
        items = [(self.ps(shape, dt) if psum else self.sbv(shape, dt)) for _ in range(n)]
        return Ring(items)

    def _semof(self, key):
        return self.sem[key] if isinstance(key, str) else self.dsem[key]

    def _wait(self, e, toks):
        kn = self.known[e]
        for tok in toks:
            if tok is None:
                continue
            key, val, snap = tok
            if kn.get(key, 0) >= val:
                continue
            self.eng[e].wait_ge(self._semof(key), val)
            self.ninst += 1
            kn[key] = val
            for k2, v2 in snap.items():
                if kn.get(k2, 0) < v2:
                    kn[k2] = v2

    @staticmethod
    def _toks(rd, wr):
        toks = []
        for r in rd:
            toks.append(r.w)
        for r in wr:
            toks.append(r.w)
            toks.extend(r.r.values())
        return toks

    @staticmethod
    def _commit(tok, rd, wr):
        for r in rd:
            r.r[tok[0]] = tok
        for r in wr:
            r.w = tok
            r.r = {}

    def op(self, e, fn, rd, wr):
        wr = list(wr) + [v for v in rd if isinstance(v, V) and v.psum]
        rd = _flat(rd)
        wr = _flat(wr)
        self._wait(e, self._toks(rd, wr))
        inst = fn(self.eng[e])
        self.cnt[e] += 1
        self.ninst += 1
        inst.then_inc(self.sem[e], 1)
        tok = (e, self.cnt[e], dict(self.known[e]))
        self._commit(tok, rd, wr)

    def dma(self, q, out, in_, is_out=False):
        rd = _flat([in_])
        wr = _flat([out])
        i = self.dnext[q]
        self.dnext[q] = (i + 1) % self.nslots[q]
        key = ("d", q, i)
        prev = self.dval[key]
        toks = self._toks(rd, wr)
        if prev > 0:
            toks.append((key, prev, {}))
        self._wait(q, toks)
        oap = out.ap if isinstance(out, V) else out
        iap = in_.ap if isinstance(in_, V) else in_
        inst = self.eng[q].dma_start(out=oap, in_=iap)
        self.ninst += 1
        val = prev + 16
        inst.then_inc(self.dsem[key], 16)
        self.dval[key] = val
        tok = (key, val, dict(self.known[q]))
        self._commit(tok, rd, wr)
        if is_out:
            self.out_toks.append(tok)

    def barrier(self):
        toks = [(e, c, {}) for e, c in self.cnt.items() if c > 0]
        toks += [(k_, v, {}) for k_, v in self.dval.items() if v > 0]
        for e in self.eng:
            self._wait(e, toks)

    def finish(self):
        self._wait("sp", self.out_toks)
        toks = [(k, v, {}) for k, v in self.dval.items() if v > 0]
        self._wait("sp", toks)

    def mm(self, out, lhsT, rhs, start=True, stop=True):
        self.op("pe", lambda e: e.matmul(out.ap, lhsT=lhsT.ap, rhs=rhs.ap, start=start, stop=stop),
                [lhsT, rhs] + ([] if start else [out]), [out])

    def tr(self, out, in_, ident):
        self.op("pe", lambda e: e.transpose(out.ap, in_.ap, ident.ap), [in_, ident], [out])

    def act(self, e, out, in_, func, bias=None, scale=1.0):
        kw = {}
        if bias is not None:
            kw["bias"] = bias.ap if isinstance(bias, V) else bias
        kw["scale"] = scale.ap if isinstance(scale, V) else scale
        assert e == "act"
        self.op("act", lambda en: en.activation(out=out.ap, in_=in_.ap, func=func, **kw),
                [in_, bias, scale], [out])

    def tt(self, e, out, a, b, op):
        self.op(e, lambda en: en.tensor_tensor(out=out.ap, in0=a.ap, in1=b.ap, op=op), [a, b], [out])

    def ts(self, e, out, a, s1, op0, s2=None, op1=None):
        s1a = s1.ap if isinstance(s1, V) else s1
        s2a = s2.ap if isinstance(s2, V) else s2
        if op1 is None:
            fn = lambda en: en.tensor_scalar(out=out.ap, in0=a.ap, scalar1=s1a, scalar2=None, op0=op0)
        else:
            fn = lambda en: en.tensor_scalar(out=out.ap, in0=a.ap, scalar1=s1a, scalar2=s2a, op0=op0, op1=op1)
        self.op(e, fn, [a, s1, s2], [out])

    def stt(self, e, out, a, s, b, op0, op1):
        sa = s.ap if isinstance(s, V) else s
        self.op(e, lambda en: en.scalar_tensor_tensor(out=out.ap, in0=a.ap, scalar=sa, in1=b.ap, op0=op0, op1=op1),
                [a, s, b], [out])

    def cp(self, e, out, in_):
        if e == "act":
            self.op("act", lambda en: en.copy(out=out.ap, in_=in_.ap), [in_], [out])
        else:
            self.op(e, lambda en: en.tensor_copy(out=out.ap, in_=in_.ap), [in_], [out])

    def memset(self, e, out, val):
        self.op(e, lambda en: en.memset(out.ap, val), [], [out])

    def red(self, e, out, in_, op=ALU.add):
        self.op(e, lambda en: en.tensor_reduce(out=out.ap, in_=in_.ap, axis=AX.X, op=op), [in_], [out])

    def scan(self, out, d0, d1, init, op0, op1):
        self.op("dve", lambda en: en.tensor_tensor_scan(out=out.ap, data0=d0.ap, data1=d1.ap, initial=init,
                                                        op0=op0, op1=op1), [d0, d1], [out])


class Ring:
    def __init__(self, items):
        self.items = items
        self.i = 0

    def get(self):
        it = self.items[self.i % len(self.items)]
        self.i += 1
        return it


class Cut(Exception):
    pass


class Done(Exception):
    pass


def build(stage=99, dbg_names=(), cutn=0):
    nc = bass.Bass("TRN2", target_bir_lowering=False)

    def cut(n):
        if cutn == n:
            raise Cut()

    def din(name, shape, dt=F32):
        return nc.dram_tensor(name, list(shape), dt, kind="ExternalInput").ap()

    x_d = din("x", [NLOC * 128, D])
    ctx_d = din("ctx", [NCTX * 128, D])
    cfm_d = din("cfm", [128, 8, 2])
    wada_d = din("w_ada", [D, 6 * D])
    badafm_d = din("bada_fm", [128, 48])
    bgt_d = din("bgt", [128, 2, D])
    g02_d = din("g02", [128, 2, 8])
    g13_d = din("g13", [128, 2, D])
    win_d = din("w_in", [D, IN_W])
    convw_d = din("convw", [128, 24, 9])
    ab_d = din("ab", [128, 2, 16])
    lbl_d = din("lbl", [128, 2, 16])
    nw_d = din("nw", [128, 2])
    wa_d = din("w_a", [D, D])
    wb_d = din("w_b", [D, D])
    wo_d = din("w_o", [D, D])
    w1_d = din("w_1", [D, 4 * D])
    w2_d = din("w_2", [4 * D, D])
    cst_d = din("consts", [128, NCONST * 128])
    y_d = nc.dram_tensor("y", [NOWN * 128, D], F32, kind="ExternalOutput").ap()
    dbg_d = {}

    es = ExitStack()
    with es:
        k = KB(nc, es)

        def dbg(name, v, shape):
            if name in dbg_names:
                d = nc.dram_tensor("dbg_" + name, list(shape), v.ap.dtype, kind="ExternalOutput").ap()
                dbg_d[name] = d
                k.dma("sp", d, v, is_out=True)

        PBIG = k.ring(3, [128, 512], F32, psum=True)
        _psm = [k.ps([128, 512], F32) for _ in range(4)]
        PSM = Ring([V(b.ap[:, j * 128:(j + 1) * 128], b.res, True) for j in range(4) for b in _psm])
        _pbf = k.ps([128, 512], BF16)
        PBF = Ring([V(_pbf.ap[:, j * 128:(j + 1) * 128], _pbf.res, True) for j in range(4)])

        class Scope:
            def __enter__(s_):
                s_.es = ExitStack()
                s_.es.__enter__()
                s_.save = k.es
                k.es = s_.es
                return s_

            def __exit__(s_, *a):
                k.barrier()
                k.es = s_.save
                s_.es.__exit__(None, None, None)
                return False

        cst = k.sbv([128, NCONST * 128], F32, "cst")
        k.dma("sp", cst, cst_d)

        def C(i):
            return cst[:, i * 128:(i + 1) * 128]

        cbf = k.sbv([128, 2 * 128], BF16, "cbf")
        k.cp("dve", cbf, cst[:, 0:256])
        I_bf = cbf[:, 0:128]
        ONE_bf = cbf[:, 128:256]
        I_f = C(C_I)
        ONE_f = C(C_ONE)

        cfm = k.sbv([128, 8, 2], F32, "cfm")
        k.dma("sp", cfm, cfm_d)
        csl = k.sbv([128, 8, 2], F32, "csl")
        k.act("act", csl, cfm, AF.Silu)
        badafm = k.sbv([128, 48], F32, "badafm")
        k.dma("sp", badafm, badafm_d)
        g02 = k.sbv([128, 2, 8], F32, "g02")
        k.dma("sp", g02, g02_d)
        ab = k.sbv([128, 2, 16], F32, "ab")
        k.dma("sp", ab, ab_d)
        lbl = k.sbv([128, 2, 16], F32, "lbl")
        k.dma("sp", lbl, lbl_d)
        nw = k.sbv([128, 2], F32, "nw")
        k.dma("sp", nw, nw_d)
        convw = k.sbv([128, 24, 9], F32, "convw")
        k.dma("sp", convw, convw_d)

        nA = k.sbv([128, 16], F32, "nA")
        k.act("act", nA, ab[:, 0, :], AF.Exp)
        k.ts("dve", nA, nA, -1.0, ALU.mult)
        dtb = ab[:, 1, :]
        lbd = k.sbv([128, 16], F32, "lbd")
        k.tt("dve", lbd, lbl[:, 0, :], lbl[:, 1, :], ALU.subtract)
        lb = k.sbv([128, 16], F32, "lb")
        oml = k.sbv([128, 16], F32, "oml")
        k.act("act", lb, lbd, AF.Sigmoid)
        k.act("act", oml, lbd, AF.Sigmoid, scale=-1.0)
        noml = k.sbv([128, 16], F32, "noml")
        k.ts("dve", noml, oml, -1.0, ALU.mult)

        modfm = k.sbv([128, 48, 2], F32, "modfm")
        grow = k.sbv([128, 2, D], F32, "grow")
        wada_v = wada_d.rearrange("(kt p) n -> p kt n", p=128)
        pp_mod = PBIG
        with ExitStack() as es2:
            es_save = k.es
            k.es = es2
            wring = k.ring(2, [128, 8, 512], F32)
            cbc = k.sbv([128, 8, 128], F32, "cbc")
            k.cp("dve", cbc, csl[:, :, 0:1].bc([128, 8, 128]))
            bgt = k.sbv([128, 2, D], F32, "bgt")
            k.dma("sp", bgt, bgt_d)
            g13 = k.sbv([128, 2, D], F32, "g13")
            k.dma("sp", g13, g13_d)
            for ch in (2, 3, 0, 1, 6, 7, 8, 9, 4, 5, 10, 11):
                wt = wring.get()
                k.dma("sp", wt[:, 0:4, :], wada_v[:, 0:4, ch * 512:(ch + 1) * 512])
                k.dma("sp", wt[:, 4:8, :], wada_v[:, 4:8, ch * 512:(ch + 1) * 512])
                pm = pp_mod.get()
                for j in range(4):
                    ct = ch * 4 + j
                    for kt in range(8):
                        k.mm(pm[:, 2 * j:2 * j + 2], wt[:, kt, j * 128:(j + 1) * 128], csl[:, kt, :],
                             start=(kt == 0), stop=(kt == 7))
                for j in range(4):
                    ct = ch * 4 + j
                    k.ts("dve", modfm[:, ct, :], pm[:, 2 * j:2 * j + 2], badafm[:, ct:ct + 1], ALU.add)
                if ch in (4, 5, 10, 11):
                    gi = 0 if ch < 6 else 1
                    c0 = (ch % 2) * 512
                    pg = pp_mod.get()
                    for kt in range(8):
                        k.mm(pg, cbc[:, kt, :], wt[:, kt, :], start=(kt == 0), stop=(kt == 7))
                    k.tt("dve", grow[:, gi, c0:c0 + 512], pg, bgt[:, gi, c0:c0 + 512], ALU.add)
            k.tt("dve", grow[:, 0, :], grow[:, 0, :], g13[:, 0, :], ALU.mult)
            k.tt("dve", grow[:, 1, :], grow[:, 1, :], g13[:, 1, :], ALU.mult)
            k.barrier()
            k.es = es_save
        A0 = k.sbv([128, 8, 2], F32, "A0")
        A2 = k.sbv([128, 8, 2], F32, "A2")
        for c_ in range(2):
            k.stt("dve", A0[:, :, c_], modfm[:, 8:16, c_], 1.0, g02[:, 0, :], ALU.add, ALU.mult)
            k.stt("dve", A2[:, :, c_], modfm[:, 32:40, c_], 1.0, g02[:, 1, :], ALU.add, ALU.mult)
        B0 = modfm[:, 0:8, :]
        B2 = modfm[:, 24:32, :]
        dbg("modfm", modfm, [128, 48, 2])
        dbg("grow", grow, [128, 2, D])

        hxT_t = k.sb([128, 8, NTOK], BF16, "hxT")
        hx = [V(hxT_t[:, :, t * 128:(t + 1) * 128], R()) for t in range(NT)]
        epsc = k.sbv([128, 1], F32, "epsc")
        k.memset("dve", epsc, EPS)
        onec = k.sbv([128, 1], F32, "onec")
        k.memset("dve", onec, 1.0)

        def hxs(c0, n):
            ts_ = tuple(hx[t].res for t in range(c0 // 128, (c0 + n + 127) // 128))
            return lambda kt: V(hxT_t[:, kt, c0:c0 + n], ts_)

        def rsqrt(out, in_, scale):
            k.act("act", out, in_, AF.Sqrt, bias=epsc[0:out.ap.shape[0], :], scale=scale)
            k.op("dve", lambda en: en.reciprocal(out=out.ap, in_=out.ap), [out], [out])

        def norm_T(xt, sq, st, dst, A, Bm, col, eng_flip):
            k.tt("dve", sq, xt, xt, ALU.mult)
            k.red("dve", st[:, 0:1], sq)
            rsqrt(st[:, 1:2], st[:, 0:1], 1.0 / D)
            k.act("act", sq, xt, AF.Copy, scale=st[:, 1:2])
            for half in range(2):
                p = PBIG.get()
                for j in range(4):
                    kt = half * 4 + j
                    k.tr(p[:, j * 128:(j + 1) * 128], sq[:, kt * 128:(kt + 1) * 128], I_f)
                for j in range(4):
                    kt = half * 4 + j
                    if (kt + eng_flip) % 2 == 0:
                        k.act("act", dst[:, kt, :], p[:, j * 128:(j + 1) * 128], AF.Identity,
                              bias=Bm[:, kt, col:col + 1], scale=A[:, kt, col:col + 1])
                    else:
                        k.ts("dve", dst[:, kt, :], p[:, j * 128:(j + 1) * 128], A[:, kt, col:col + 1], ALU.mult,
                             Bm[:, kt, col:col + 1], ALU.add)

        with Scope():
            xring = k.ring(3, [128, D], F32)
            sqring = k.ring(2, [128, D], F32)
            stat = k.ring(4, [128, 2], F32)
            for t in range(NT):
                xt = xring.get()
                src = ctx_d[t * 128:(t + 1) * 128, :] if t < NCTX else x_d[(t - NCTX) * 128:(t - NCTX + 1) * 128, :]
                k.dma("sp", xt, src)
                norm_T(xt, sqring.get(), stat.get(), hx[t], A0, B0, 1 if t < NCTX else 0, t)
            dbg("hxT", V(hxT_t[:, :, 0:512], hx[0].res), [128, 8, 512])

        win_v = win_d.rearrange("(kt p) n -> p kt n", p=128)
        def _phase2plus():
            ydr = nc.dram_tensor("ydr", [2, NH, 128, NOWN * 128], BF16, kind="Internal").ap()
            ydr_res = [[R() for _ in range(NH)] for _ in range(2)]

            wring = k.ring(5, [128, 8, 128], BF16)

            wst = k.ring(2, [128, 8, 128], F32)

            def load_w(col0):
                st_ = wst.get()
                k.dma("sp", st_, win_v[:, :, col0:col0 + 128])
                w = wring.get()
                k.cp("dve", w, st_)
                return w

            def project_fm(w, c0, n, dst_fn):
                hv = hxs(c0, n)
                p = PBIG.get()
                for kt in range(8):
                    k.mm(p[:, 0:n], w[:, kt, :], hv(kt), start=(kt == 0), stop=(kt == 7))
                dst_fn(p[:, 0:n])

            with Scope():
                names = ("la", "lnb", "negg", "bsc", "bg", "ed", "dl")
                sm = {n: k.sbv([128, NT, 16], F32, "sm_" + n) for n in names}
                sm["la"].res = tuple(R() for _ in range(NT))
                smres = [R() for _ in range(NT)]
                with Scope():
                    wab = k.sbv([128, 8, 32], BF16, "wab")
                    wab32 = k.sbv([128, 8, 32], F32, "wab32")
                    k.dma("sp", wab32, win_v[:, :, O_A:O_A + 32])
                    k.cp("dve", wab, wab32)
                    tmp = k.ring(6, [128, 32], F32)
                    for t in range(NT):
                        hv = hxs(t * 128, 128)
                        pab = PSM.get()
                        for kt in range(8):
                            k.mm(pab[:, 0:32], hv(kt), wab[:, kt, :], start=(kt == 0), stop=(kt == 7))
                        xx = tmp.get()
                        k.tt("dve", xx[:, 0:16], pab[:, 0:16], dtb, ALU.add)
                        k.ts("dve", xx[:, 16:32], pab[:, 16:32], -1.0, ALU.mult)
                        if t == 0:
                            dbg("d_wab", wab, [128, 8, 32])
                            dbg("d_wab32", wab32, [128, 8, 32])
                            dbg("d_hx0", hx[0], [128, 8, 128])
                            dbg("d_xx", xx, [128, 32])
                            dbg("d_nA", nA, [128, 16])
                            dbg("d_ab", ab, [128, 2, 16])
                        aa = tmp.get()
                        k.ts("dve", aa, xx, -1.0, ALU.mult)
                        k.tt("dve", aa, aa, xx, ALU.max)
                        if t == 0:
                            dbg("d_abs", aa, [128, 32])
                        k.act("act", aa, aa, AF.Exp, scale=-1.0)
                        if t == 0:
                            dbg("d_exp", aa, [128, 32])
                        k.act("act", aa, aa, AF.Ln, bias=onec)
                        if t == 0:
                            dbg("d_ln", aa, [128, 32])
                        k.ts("dve", xx, xx, 0.0, ALU.max)
                        k.tt("dve", xx, xx, aa, ALU.add)
                        sres = smres[t]

                        def S_(n):
                            return V(sm[n].ap[:, t, :], sres)
                        la_t = V(sm["la"].ap[:, t, :], sm["la"].res[t])
                        k.tt("dve", la_t, xx[:, 0:16], nA, ALU.mult)
                        k.ts("dve", S_("lnb"), xx[:, 16:32], -1.0, ALU.mult)
                        pg = PSM.get()
                        k.mm(pg[:, 0:8], C(C_UF), la_t[:, 0:8])
                        k.mm(pg[:, 8:16], C(C_UB), la_t[:, 8:16])
                        k.mm(pg[:, 16:32], ONE_f, la_t)
                        gg = tmp.get()
                        k.cp("dve", gg, pg[:, 0:32])
                        k.ts("dve", S_("negg"), gg[:, 0:16], -1.0, ALU.mult)
                        k.act("act", S_("bsc"), S_("lnb"), AF.Exp)
                        t2 = tmp.get()
                        k.tt("dve", t2[:, 0:16], S_("lnb"), gg[:, 0:16], ALU.add)
                        k.act("act", S_("bg"), t2[:, 0:16], AF.Exp)
                        k.tt("dve", t2[:, 16:32], gg[:, 16:32], gg[:, 0:16], ALU.subtract)
                        k.act("act", S_("ed"), t2[:, 16:32], AF.Exp)
                        k.act("act", S_("dl"), gg[:, 16:32], AF.Exp)

                if cutn == 1:
                    for n_ in names:
                        dbg("sm_" + n_, V(sm[n_].ap, tuple(smres) + tuple(sm["la"].res)), [128, NT, 16])
                cut(1)

                def SM(n, t, hd):
                    r = sm["la"].res[t] if n == "la" else smres[t]
                    return V(sm[n].ap[:, t, hd:hd + 1], r)

                NSLOT = 4
                t128s = [k.ring(8, [128, 128], F32) for _ in range(NSLOT)]
                b128s = [k.ring(8, [128, 128], BF16) for _ in range(NSLOT)]

                def run_units(unit_fn, arglist, nslots):
                    pending = list(enumerate(arglist))
                    active = []
                    free = list(range(nslots))
                    next_chain = 0
                    while pending or active:
                        while pending and free:
                            idx, a_ = pending.pop(0)
                            sl = free.pop(0)
                            active.append([idx, unit_fn(*a_, slot=sl), sl, "prep"])
                        for item in list(active):
                            idx, g, sl, st = item
                            if st == "wait" and idx != next_chain:
                                continue
                            try:
                                r_ = next(g)
                                if r_ == "chain":
                                    item[3] = "wait"
                            except StopIteration:
                                active.remove(item)
                                free.append(sl)
                                next_chain += 1
                with Scope():
                    pre_loc = k.sbv([128, 66, 66], BF16, "pre_loc")
                    pre_ctx = k.sbv([128, 258], BF16, "pre_ctx")
                    k.memset("pool", pre_loc, 0.0)
                    k.memset("pool", pre_ctx, 0.0)
                    dgr = k.ring(1, [128, 9, 128], BF16)
                    KT = k.sbv([128, NTOK], BF16, "KT")
                    QT = k.sbv([128, NOWN * 128], BF16, "QT")
                    ktok = k.sbv([128, NT, 128], BF16, "ktok")
                    vtok = k.sbv([128, NT, 128], BF16, "vtok")
                    gaT = k.sbv([128, NOWN * 128], BF16, "gaT")
                    oBT = k.sbv([128, NOWN * 128], BF16, "oBT")
                    yT = k.ring(1, [128, NOWN * 128], BF16)
                    f512 = k.ring(2, [128, 512], F32)
                    g512 = k.ring(2, [128, 512], F32)
                    vt512 = k.ring(2, [128, 512], BF16)
                    S32 = [k.sbv([128, 128], F32, "S32_%d" % d) for d in range(2)]
                    Sbf = [k.ring(2, [128, 128], BF16) for d in range(2)]

                    def conv_head(w, ct, which):
                        dg = dgr.get()
                        for tap in range(9):
                            k.ts("dve", dg[:, tap, :], I_bf, convw[:, ct, tap:tap + 1], ALU.mult)
                        cut(21)
                        if which != "q":
                            project_fm(w, 0, 256, lambda p: k.cp("act", pre_ctx[:, 1:257], p))
                        for g in range(8 if which != "q" else 5):
                            project_fm(w, TOK0 + 512 * g, 512,
                                       lambda p: k.cp("act", pre_loc[:, 1 + 8 * g:9 + 8 * g, 1:65],
                                                      p.re("p (a b) -> p a b", b=64)))
                        cut(22)
                        for g in (range(-1, 8) if which != "q" else range(0, 4)):
                            n = 256 if g < 0 else 512
                            c0 = 0 if g < 0 else TOK0 + 512 * g
                            pc = PBIG.get()
                            if g < 0:
                                for kw in range(3):
                                    k.mm(pc[:, 0:256], dg[:, 3 + kw, :], pre_ctx[:, kw:kw + 256], start=(kw == 0), stop=(kw == 2))
                            else:
                                pc3 = pc.re("p (a b) -> p a b", b=64)
                                for tap in range(9):
                                    kh, kw = tap // 3, tap % 3
                                    k.mm(pc3, dg[:, tap, :], pre_loc[:, 8 * g + kh:8 * g + kh + 8, kw:kw + 64],
                                         start=(tap == 0), stop=(tap == 8))
                            if g == 0:
                                cut(26)
                            if which == "v":
                                vt = vt512.get()
                                k.act("act", vt[:, 0:n], pc[:, 0:n], AF.Silu)
                                for j in range(n // 128):
                                    pb = PBF.get()
                                    k.tr(pb, vt[:, j * 128:(j + 1) * 128], I_bf)
                                    k.cp("dve", vtok[:, c0 // 128 + j, :], pb)
                            else:
                                qs = f512.get()
                                k.act("act", qs[:, 0:n], pc[:, 0:n], AF.Silu)
                                sq = g512.get()
                                k.tt("dve", sq[:, 0:n], qs[:, 0:n], qs[:, 0:n], ALU.mult)
                                pss = PBIG.get()
                                k.mm(pss[:, 0:n], ONE_f, sq[:, 0:n])
                                k.act("act", sq[:, 0:n], pss[:, 0:n], AF.Sqrt, bias=epsc)
                                k.op("dve", lambda en: en.reciprocal(out=sq.ap[:, 0:n], in_=sq.ap[:, 0:n]), [sq], [sq])
                                if g == 0:
                                    cut(27)
                                if which == "k":
                                    k.tt("dve", KT[:, c0:c0 + n], qs[:, 0:n], sq[:, 0:n], ALU.mult)
                                    if g == 0:
                                        cut(28)
                                    for j in range(n // 128):
                                        pb = PBF.get()
                                        k.tr(pb, KT[:, c0 + j * 128:c0 + (j + 1) * 128], I_bf)
                                        k.cp("dve", ktok[:, c0 // 128 + j, :], pb)
                                        if g == 0:
                                            cut(30 + j)
                                else:
                                    k.stt("dve", QT[:, c0 - TOK0:c0 - TOK0 + n], qs[:, 0:n], 128.0 ** -0.5, sq[:, 0:n],
                                      ALU.mult, ALU.mult)

                    def gdn_unit(h, d, t, full, own_idx, slot=0):
                        T1, B1 = t128s[slot], b128s[slot]
                        hd = d * 8 + h
                        UM = C(C_UF if d == 0 else C_UB)
                        NS = C(C_NSF if d == 0 else C_NSB)
                        NI = C(C_NIF if d == 0 else C_NIB)
                        la_bc = SM("la", t, hd).bc([128, 128])
                        lnb_bc = SM("lnb", t, hd).bc([128, 128])
                        negg = SM("negg", t, hd)
                        cs = slice(t * 128, (t + 1) * 128)
                        qs_ = slice(own_idx * 128, (own_idx + 1) * 128)
                        pA = PSM.get()
                        k.mm(pA, la_bc, UM, start=True, stop=False)
                        k.mm(pA, lnb_bc, I_f, start=False, stop=False)
                        k.mm(pA, I_f, NS, start=False, stop=True)
                        LA = T1.get()
                        k.act("act", LA, pA, AF.Exp, bias=negg)
                        yield
                        pK = PSM.get()
                        k.mm(pK, KT[:, cs], KT[:, cs])
                        Xt = T1.get()
                        k.stt("dve", Xt, pK, -1.0, LA, ALU.mult, ALU.mult)
                        yield
                        pb = PSM.get()
                        k.tr(pb, Xt, I_f)
                        Xn = T1.get()
                        k.cp("act", Xn, pb)
                        yield
                        Rk = T1.get()
                        k.tt("pool", Rk, Xt, I_f, ALU.add)
                        yield
                        for lvl in range(1, 7):
                            pn = PSM.get()
                            k.mm(pn, Xt, Xn)
                            Xn2 = T1.get()
                            k.cp("act", Xn2, pn)
                            yield
                            if lvl < 6:
                                pt_ = PSM.get()
                                k.mm(pt_, Xn, Xt)
                                Xt2 = T1.get()
                                k.cp("dve", Xt2, pt_)
                                yield
                            pr = PSM.get()
                            k.mm(pr, Xn2, Rk)
                            Rk2 = T1.get()
                            k.tt("dve", Rk2, pr, Rk, ALU.add)
                            yield
                            Xn, Rk = Xn2, Rk2
                            if lvl < 6:
                                Xt = Xt2
                        Tb = B1.get()
                        k.act("act", Tb, Rk, AF.Copy, scale=SM("bsc", t, hd))
                        yield
                        Tbg = B1.get()
                        k.act("act", Tbg, Rk, AF.Copy, scale=SM("bg", t, hd))
                        yield
                        pu = PSM.get()
                        k.mm(pu, Tb, vtok[:, t, :])
                        u32 = T1.get()
                        k.cp("act", u32, pu)
                        yield
                        pw = PSM.get()
                        k.mm(pw, ktok[:, t, :], Tbg)
                        wT = B1.get()
                        k.cp("dve", wT, pw)
                        yield
                        kdec = B1.get()
                        k.ts("pool", kdec, ktok[:, t, :], SM("ed", t, hd), ALU.mult)
                        yield
                        if full:
                            pL = PSM.get()
                            k.mm(pL, la_bc, UM, start=True, stop=False)
                            k.mm(pL, I_f, NI, start=False, stop=True)
                            Lt = T1.get()
                            k.act("act", Lt, pL, AF.Exp, bias=negg)
                            yield
                            pQ = PSM.get()
                            k.mm(pQ, KT[:, cs], QT[:, qs_])
                            attnT = B1.get()
                            k.tt("dve", attnT, pQ, Lt, ALU.mult)
                            yield
                            pE = PSM.get()
                            k.mm(pE, la_bc, UM)
                            Er = B1.get()
                            k.act("act", Er, pE, AF.Exp)
                            yield
                            qgT = B1.get()
                            k.tt("pool", qgT, QT[:, qs_], Er, ALU.mult)
                            yield
                        if h == 0 and d == 1 and t == 1:
                            dbg("LA", LA, [128, 128])
                            dbg("Rk", Rk, [128, 128])
                            dbg("Tb", Tb, [128, 128])
                            dbg("u32", u32, [128, 128])
                            dbg("wT", wT, [128, 128])
                            dbg("kdec", kdec, [128, 128])
                            dbg("smla", sm["la"][:, 1, :], [128, 16])
                            dbg("smnegg", V(sm["negg"].ap[:, 1, :], smres[1]), [128, 16])
                            dbg("smbg", V(sm["bg"].ap[:, 1, :], smres[1]), [128, 16])
                            dbg("smed", V(sm["ed"].ap[:, 1, :], smres[1]), [128, 16])
                        yield "chain"
                        Sb = Sbf[d].items[(Sbf[d].i - 1) % 2]
                        pws = PSM.get()
                        k.mm(pws, wT, Sb)
                        vnew = B1.get()
                        k.tt("dve", vnew, u32, pws, ALU.subtract)
                        yield
                        if full:
                            po = PSM.get()
                            k.mm(po, Sb, qgT, start=True, stop=False)
                            k.mm(po, vnew, attnT, start=False, stop=True)
                            oc = slice(own_idx * 128, (own_idx + 1) * 128)
                            if d == 1:
                                k.cp("act", oBT[:, oc], po)
                                yield
                            else:
                                o32 = T1.get()
                                k.tt("dve", o32, po, oBT[:, oc], ALU.add)
                                yield
                                sq = T1.get()
                                k.tt("dve", sq, o32, o32, ALU.mult)
                                yield
                                pss = PSM.get()
                                k.mm(pss, ONE_f, sq)
                                rn = T1.get()
                                rsqrt(rn, pss, 1.0 / 128)
                                k.tt("dve", o32, o32, rn, ALU.mult)
                                yield
                                k.stt("dve", ycur[:, oc], o32, nw[:, 0:1], gaT[:, oc], ALU.mult, ALU.mult)
                                yield
                        pS = PSM.get()
                        k.mm(pS, kdec, vnew)
                        k.stt("dve", S32[d], S32[d], SM("dl", t, hd), pS, ALU.mult, ALU.add)
                        yield
                        Sn = Sbf[d].get()
                        k.cp("dve", Sn, S32[d])
                        yield

                    for h in range(NH if stage >= 3 else 1):
                        wq, wk, wv, wga = (load_w(O_Q + h * 128), load_w(O_K + h * 128),
                                           load_w(O_V + h * 128), load_w(O_GA + h * 128))
                        conv_head(wk, 8 + h, "k")
                        if h == 0:
                            dbg("wk", wk, [128, 8, 128])
                            dbg("wst", wst.items[1], [128, 8, 128])
                            dbg("prectx", pre_ctx, [128, 258])
                            dbg("preloc", pre_loc[:, 0:4, :], [128, 4, 66])
                            dbg("dg", dgr.items[0], [128, 9, 128])
                        cut(2)
                        conv_head(wv, 16 + h, "v")
                        conv_head(wq, h, "q")
                        cut(3)
                        for g in range(4):
                            project_fm(wga, TOK0 + 512 * g, 512,
                                       lambda p: k.act("act", gaT[:, 512 * g:512 * (g + 1)], p, AF.Silu))
                        ycur = yT.get()
                        for d in (1, 0):
                            k.memset("pool", S32[d], 0.0)
                            k.memset("pool", Sbf[d].get(), 0.0)
                            seq = []
                            if d == 0:
                                seq += [(t, False, -1) for t in range(NCTX)]
                                seq += [(NCTX + t, True, t) for t in range(NOWN)]
                            else:
                                seq += [(t, False, -1) for t in reversed(range(NCTX))]
                                seq += [(NCTX + t, False, -1) for t in reversed(range(NOWN, NLOC))]
                                seq += [(NCTX + t, True, t) for t in reversed(range(NOWN))]
                            run_units(gdn_unit, [(h, d, t, full, oi) for (t, full, oi) in seq], 4)
                            cut(6)
                        if h == 0:
                            dbg("KT", KT[:, 0:1024], [128, 1024])
                            dbg("vtok", vtok[:, 0:4, :], [128, 4, 128])
                            dbg("ya0", ycur, [128, NOWN * 128])
                            dbg("S32", S32[0], [128, 128])
                        k.dma("sp", V(ydr[0, h], ydr_res[0][h]), ycur)
                cut(7)

                with Scope():
                    vtokH = k.sbv([128, NT, 128], BF16, "vtokH")
                    qbT = k.sbv([128, NOWN * 128], BF16, "qbT")
                    gbT = k.sbv([128, NOWN * 128], BF16, "gbT")
                    oBT = k.sbv([128, NOWN * 128], BF16, "oBTh")
                    yT = k.ring(2, [128, NOWN * 128], BF16)
                    S32 = [k.sbv([128, 128], F32, "HS32_%d" % d) for d in range(2)]
                    Sbf = [k.ring(2, [128, 128], BF16) for d in range(2)]
                    vmall = k.sbv([128, NT, 4, 128], BF16, "vmall")
                    ebrs = [k.ring(1, [128, 4], F32) for _ in range(3)]
                    RST = C(C_RST)
                    PM = C(C_SUB)

                    def hgrn_unit(h, d, t, full, own_idx, wf, slot=0):
                        T1, B1 = t128s[slot], b128s[slot]
                        hd = d * 8 + h
                        HM = C(C_HMF if d == 0 else C_HMB)
                        lbc, omlc, nomlc = lb[:, hd:hd + 1], oml[:, hd:hd + 1], noml[:, hd:hd + 1]
                        hv = hxs(t * 128, 128)
                        pf = PSM.get()
                        for kt in range(8):
                            k.mm(pf, wf[:, kt, :], hv(kt), start=(kt == 0), stop=(kt == 7))
                        sig = T1.get()
                        k.act("act", sig, pf, AF.Sigmoid)
                        yield
                        lf = T1.get()
                        k.act("act", lf, sig, AF.Ln, bias=lbc, scale=omlc)
                        yield
                        kb = T1.get()
                        k.ts("dve", kb, sig, nomlc, ALU.mult, omlc, ALU.add)
                        yield
                        bb = T1.get()
                        k.scan(bb, RST, lf, 0.0, ALU.mult, ALU.add)
                        yield
                        if d == 1:
                            b2 = T1.get()
                            k.stt("dve", b2, bb, -1.0, lf, ALU.mult, ALU.add)
                            yield
                            for c in range(4):
                                k.ts("dve", b2[:, 32 * c:32 * c + 32], b2[:, 32 * c:32 * c + 32],
                                     bb[:, 32 * c + 31:32 * c + 32], ALU.add)
                            bb = b2
                        lastcol = [(32 * c + 31) if d == 0 else (32 * c) for c in range(4)]
                        Dm = T1.get()
                        for c in range(4):
                            k.ts("dve", Dm[:, 32 * c:32 * c + 32], bb[:, 32 * c:32 * c + 32],
                                 bb[:, 32 * c + 15:32 * c + 16], ALU.subtract)
                        EK = T1.get()
                        k.act("act", EK, Dm, AF.Exp, scale=-1.0)
                        yield
                        Kp = B1.get()
                        k.tt("pool", Kp, kb, EK, ALU.mult)
                        yield
                        ED = T1.get()
                        eb = ebrs[slot].get()
                        for c in range(4):
                            lc = lastcol[c]
                            k.act("act", ED[:, 32 * c:32 * c + 32], bb[:, 32 * c:32 * c + 32], AF.Exp,
                                  bias=bb[:, lc:lc + 1], scale=-1.0)
                            yield
                        k.act("act", eb, bb[:, lastcol[0]:128:32], AF.Exp)
                        yield
                        kdT = B1.get()
                        k.tt("pool", kdT, kb, ED, ALU.mult)
                        yield
                        pb = PBF.get()
                        k.tr(pb, kdT, I_bf)
                        kdtok = B1.get()
                        k.cp("dve", kdtok, pb)
                        yield
                        pG = PBIG.get()
                        for c in range(4):
                            k.mm(pG[:, 128 * c:128 * c + 128], kdtok, vmall[:, t, c, :])
                        if full:
                            oc = slice(own_idx * 128, (own_idx + 1) * 128)
                            EQ = T1.get()
                            k.act("act", EQ, Dm, AF.Exp)
                            yield
                            Qp = B1.get()
                            k.tt("pool", Qp, qbT[:, oc], EQ, ALU.mult)
                            yield
                            pa = PSM.get()
                            k.mm(pa, Kp, Qp)
                            aT = B1.get()
                            k.tt("dve", aT, pa, HM, ALU.mult)
                            yield
                            EB = T1.get()
                            k.act("act", EB, bb, AF.Exp)
                            yield
                            Qb = B1.get()
                            k.tt("pool", Qb, qbT[:, oc], EB, ALU.mult)
                            yield
                            po = PSM.get()
                        yield "chain"
                        for c in (range(4) if d == 0 else reversed(range(4))):
                            Sb = Sbf[d].items[(Sbf[d].i - 1) % 2]
                            cs = slice(32 * c, 32 * c + 32)
                            if full:
                                k.mm(po[:, cs], Sb, Qb[:, cs], start=True, stop=False)
                                k.mm(po[:, cs], vtokH[:, t, :], aT[:, cs], start=False, stop=True)
                            k.stt("dve", S32[d], S32[d], eb[:, c:c + 1], pG[:, 128 * c:128 * c + 128], ALU.mult, ALU.add)
                            yield
                            Sn = Sbf[d].get()
                            k.cp("act", Sn, S32[d])
                            yield
                        if full:
                            if d == 1:
                                k.cp("act", oBT[:, oc], po)
                                yield
                            else:
                                o32 = T1.get()
                                k.tt("dve", o32, po, oBT[:, oc], ALU.add)
                                yield
                                sq = T1.get()
                                k.tt("dve", sq, o32, o32, ALU.mult)
                                yield
                                pss = PSM.get()
                                k.mm(pss, ONE_f, sq)
                                rn = T1.get()
                                rsqrt(rn, pss, 1.0 / 128)
                                k.tt("dve", o32, o32, rn, ALU.mult)
                                yield
                                k.stt("dve", ycurH[:, oc], o32, nw[:, 1:2], gbT[:, oc], ALU.mult, ALU.mult)
                                yield

                    for h in range(NH if stage >= 3 else 1):
                        wfF, wfB, wqb, wib, wgb = (load_w(O_F + h * 128), load_w(O_F + 1024 + h * 128),
                                                   load_w(O_QB + h * 128), load_w(O_IB + h * 128),
                                                   load_w(O_GB + h * 128))
                        for t in range(NT):
                            hv = hxs(t * 128, 128)
                            pv = PSM.get()
                            for kt in range(8):
                                k.mm(pv, hv(kt), wib[:, kt, :], start=(kt == 0), stop=(kt == 7))
                            k.cp("act", vtokH[:, t, :], pv)
                            for c in range(4):
                                k.ts("dve", vmall[:, t, c, :], vtokH[:, t, :], PM[:, c:c + 1], ALU.mult)
                        for g in range(4):
                            project_fm(wqb, TOK0 + 512 * g, 512,
                                       lambda p: k.act("act", qbT[:, 512 * g:512 * (g + 1)], p, AF.Silu))
                            project_fm(wgb, TOK0 + 512 * g, 512,
                                       lambda p: k.act("act", gbT[:, 512 * g:512 * (g + 1)], p, AF.Silu))
                        k.ts("dve", qbT, qbT, 128.0 ** -0.5, ALU.mult)
                        ycurH = yT.get()
                        for d in (1, 0):
                            k.memset("dve", S32[d], 0.0)
                            k.memset("dve", Sbf[d].get(), 0.0)
                            seq = []
                            if d == 0:
                                seq += [(t, False, -1) for t in range(NCTX)]
                                seq += [(NCTX + t, True, t) for t in range(NOWN)]
                            else:
                                seq += [(t, False, -1) for t in reversed(range(NCTX))]
                                seq += [(NCTX + t, False, -1) for t in reversed(range(NOWN, NLOC))]
                                seq += [(NCTX + t, True, t) for t in reversed(range(NOWN))]
                            run_units(hgrn_unit, [(h, d, t, full, oi, wfF if d == 0 else wfB) for (t, full, oi) in seq], 3)
                        if h == 0:
                            dbg("yb0", ycurH, [128, NOWN * 128])
                            dbg("HS", S32[0], [128, 128])
                        k.dma("sp", V(ydr[1, h], ydr_res[1][h]), ycurH)
            cut(8)

            yres = [R() for _ in range(NOWN)]
            h2 = [V(hxT_t[:, :, TOK0 + (NOWN + t) * 128:TOK0 + (NOWN + t + 1) * 128], hx[NCTX + NOWN + t].res)
                  for t in range(NOWN)]

            def load_cols(dst, src_d, ncol_tiles):
                sv = src_d.rearrange("(kt p) n -> p kt n", p=128)
                for ct in range(ncol_tiles):
                    st_ = wst.get()
                    k.dma("sp", st_, sv[:, :, ct * 128:(ct + 1) * 128])
                    k.cp("dve", dst[:, :, ct * 128:(ct + 1) * 128], st_)

            with Scope():
                Wa = k.sbv([128, 8, D], BF16, "Wa")
                Wb = k.sbv([128, 8, D], BF16, "Wb")
                Wo = k.sbv([128, 8, D], BF16, "Wo")
                load_cols(Wa, wa_d, 8)
                load_cols(Wb, wb_d, 8)
                load_cols(Wo, wo_d, 8)
                yA = k.sbv([128, NH, 512], BF16, "yA")
                yB = k.sbv([128, NH, 512], BF16, "yB")
                mgd = k.sbv([128, 8, 512], BF16, "mgd")
                xr = k.ring(1, [128, D], F32)
                o2r = k.ring(1, [128, D], F32)
                sqr = k.ring(1, [128, D], F32)
                str_ = k.ring(4, [128, 2], F32)
                sg = k.ring(1, [128, 512], F32)
                m1r = k.ring(2, [128, 512], F32)
                for G in range(4):
                    gc = slice(512 * G, 512 * (G + 1))
                    for h in range(NH):
                        k.dma("sp", yA[:, h, :], V(ydr[0, h][:, gc], ydr_res[0][h]))
                        k.dma("sp", yB[:, h, :], V(ydr[1, h][:, gc], ydr_res[1][h]))
                    hvG = hxs(TOK0 + 512 * G, 512)
                    for dt in range(8):
                        wga_ = load_w(O_MG + dt * 128)
                        wgb_ = load_w(O_MG + 1024 + dt * 128)
                        dc = slice(dt * 128, (dt + 1) * 128)
                        pga = PBIG.get()
                        for kt in range(8):
                            k.mm(pga, wga_[:, kt, :], hvG(kt), start=(kt == 0), stop=(kt == 7))
                        sga = sg.get()
                        k.act("act", sga, pga, AF.Sigmoid)
                        pa_ = PBIG.get()
                        for h in range(NH):
                            k.mm(pa_, Wa[:, h, dc], yA[:, h, :], start=(h == 0), stop=(h == NH - 1))
                        m1 = m1r.get()
                        k.tt("dve", m1, pa_, sga, ALU.mult)
                        pgb = PBIG.get()
                        for kt in range(8):
                            k.mm(pgb, wgb_[:, kt, :], hvG(kt), start=(kt == 0), stop=(kt == 7))
                        sgb = sg.get()
                        k.act("act", sgb, pgb, AF.Sigmoid)
                        pb_ = PBIG.get()
                        for h in range(NH):
                            k.mm(pb_, Wb[:, h, dc], yB[:, h, :], start=(h == 0), stop=(h == NH - 1))
                        m2 = m1r.get()
                        k.tt("dve", m2, pb_, sgb, ALU.mult)
                        k.tt("dve", mgd[:, dt, :], m1, m2, ALU.add)
                    if G == 0:
                        dbg("yA", yA, [128, NH, 512])
                        dbg("yB", yB, [128, NH, 512])
                        dbg("mgd", mgd, [128, 8, 512])
                    for j in range(4):
                        t = 4 * G + j
                        o2 = o2r.get()
                        for half in range(2):
                            po_ = PBIG.get()
                            for dt in range(8):
                                k.mm(po_, mgd[:, dt, j * 128:(j + 1) * 128], Wo[:, dt, half * 512:(half + 1) * 512],
                                     start=(dt == 0), stop=(dt == 7))
                            k.cp("act", o2[:, half * 512:(half + 1) * 512], po_)
                        sq = sqr.get()
                        st = str_.get()
                        if t == 0:
                            dbg("o2raw", o2, [128, D])
                        k.tt("dve", sq, o2, o2, ALU.mult)
                        k.red("dve", st[:, 0:1], sq)
                        rsqrt(st[:, 1:2], st[:, 0:1], 1.0 / D)
                        if t == 0:
                            dbg("o2st", st, [128, 2])
                        k.act("act", o2, o2, AF.Copy, scale=st[:, 1:2])
                        k.tt("dve", o2, o2, grow[:, 0, :], ALU.mult)
                        if t == 0:
                            dbg("o2sc", o2, [128, D])
                        xt = xr.get()
                        k.dma("sp", xt, x_d[t * 128:(t + 1) * 128, :])
                        k.tt("dve", xt, xt, o2, ALU.add)
                        k.dma("sp", V(y_d[t * 128:(t + 1) * 128, :], yres[t]), xt)
                        if t == 0:
                            dbg("x1t0", xt, [128, D])
                        norm_T(xt, sqr.get(), str_.get(), h2[t], A2, B2, 0, t)
            cut(9)

            w1_v = w1_d.rearrange("(kt p) n -> p kt n", p=128)
            w2_v = w2_d.rearrange("(ft p) n -> p ft n", p=128)
            with Scope():
                hid = k.sbv([128, 32, 512], BF16, "hid")
                w2r = k.ring(3, [128, 512], BF16)
                w2s = k.ring(3, [128, 512], F32)
                rl = k.ring(2, [128, 512], F32)
                xr = k.ring(1, [128, D], F32)
                o3r = k.ring(4, [128, D], F32)
                sqr = k.ring(1, [128, D], F32)
                str_ = k.ring(4, [128, 2], F32)
                PACC = [V(b.ap, b.res, True) for b in _psm]
                for G in range(4):
                    hvG = lambda kt: V(hxT_t[:, kt, TOK0 + (NOWN + 4 * G) * 128:TOK0 + (NOWN + 4 * G + 4) * 128],
                                       tuple(h2[4 * G + j].res for j in range(4)))
                    for ft in range(32):
                        st_ = wst.get()
                        k.dma("sp", st_, w1_v[:, :, ft * 128:(ft + 1) * 128])
                        w1 = wring.get()
                        k.cp("dve", w1, st_)
                        ph = PBIG.get()
                        for kt in range(8):
                            k.mm(ph, w1[:, kt, :], hvG(kt), start=(kt == 0), stop=(kt == 7))
                        r_ = rl.get()
                        k.act("act", r_, ph, AF.Relu)
                        k.tt("dve", hid[:, ft, :], r_, r_, ALU.mult)
                    o3s = [o3r.get() for _ in range(4)]
                    for half in range(2):
                        for ft in range(32):
                            s2 = w2s.get()
                            k.dma("sp", s2, w2_v[:, ft, half * 512:(half + 1) * 512])
                            w2 = w2r.get()
                            k.cp("dve", w2, s2)
                            for j in range(4):
                                k.mm(PACC[j], hid[:, ft, j * 128:(j + 1) * 128], w2, start=(ft == 0), stop=(ft == 31))
                        for j in range(4):
                            k.cp("act", o3s[j][:, half * 512:(half + 1) * 512], PACC[j])
                    for j in range(4):
                        t = 4 * G + j
                        o3 = o3s[j]
                        sq = sqr.get()
                        st = str_.get()
                        k.tt("dve", sq, o3, o3, ALU.mult)
                        k.red("dve", st[:, 0:1], sq)
                        rsqrt(st[:, 1:2], st[:, 0:1], 1.0 / D)
                        k.act("act", o3, o3, AF.Copy, scale=st[:, 1:2])
                        k.tt("dve", o3, o3, grow[:, 1, :], ALU.mult)
                        xt = xr.get()
                        yv = V(y_d[t * 128:(t + 1) * 128, :], yres[t])
                        k.dma("sp", xt, yv)
                        k.tt("dve", xt, xt, o3, ALU.add)
                        k.dma("sp", yv, xt, is_out=True)
            k.finish()
            raise Done()


        try:
            _phase2plus()
            raise Cut()
        except Done:
            return nc, dbg_d
        except Cut:
            z = V(hxT_t[:, 0, 0:2048].bitcast(F32), tuple(h_.res for h_ in hx))
            k.memset("dve", z, 0.0)
            for t in range(NOWN):
                k.dma("sp", y_d[t * 128:(t + 1) * 128, :], z, is_out=True)
        k.finish()
    return nc, dbg_d


def host_prep(inputs):
    f = np.float32
    x = np.asarray(inputs["x"], f)
    c = np.asarray(inputs["c"], f)
    ctx = np.asarray(inputs["ctx"], f)
    c_ctx = np.asarray(inputs["c_ctx"], f)
    w_ada = np.ascontiguousarray(np.asarray(inputs["w_ada"], f)[0])
    b_ada = np.asarray(inputs["b_ada"], f)[0]
    norm_g = np.asarray(inputs["norm_g"], f)[0]
    w_in = np.asarray(inputs["w_in"], f)[0]
    conv_w = np.asarray(inputs["conv_w"], f)[0]
    a_log = np.asarray(inputs["gdn_a_log"], f)[0]
    dt_b = np.asarray(inputs["gdn_dt_bias"], f)[0]
    gnw = np.asarray(inputs["gdn_norm_w"], f)[0]
    lbl = np.asarray(inputs["hgrn_lb_logits"], f)
    hnw = np.asarray(inputs["hgrn_norm_w"], f)[0]

    i = np.arange(128)
    consts = np.zeros((NCONST, 128, 128), f)
    consts[C_I] = np.eye(128)
    consts[C_ONE] = 1.0
    consts[C_UF] = (i[:, None] <= i[None, :])
    consts[C_UB] = (i[:, None] >= i[None, :])
    consts[C_NSF] = np.where(i[None, :] > i[:, None], 0.0, NEG)
    consts[C_NSB] = np.where(i[None, :] < i[:, None], 0.0, NEG)
    consts[C_NIF] = np.where(i[None, :] >= i[:, None], 0.0, NEG)
    consts[C_NIB] = np.where(i[None, :] <= i[:, None], 0.0, NEG)
    same = (i[:, None] // 32) == (i[None, :] // 32)
    consts[C_HMF] = same & (i[:, None] <= i[None, :])
    consts[C_HMB] = same & (i[:, None] >= i[None, :])
    consts[C_RST] = np.broadcast_to((i % 32 != 0).astype(f)[None, :], (128, 128))
    consts[C_SUB][:, 0:4] = ((i[:, None] // 32) == np.arange(4)[None, :])
    consts = np.ascontiguousarray(consts.transpose(1, 0, 2).reshape(128, NCONST * 128))

    bada_fm = np.ascontiguousarray(b_ada.reshape(48, 128).T)
    bgt = np.ascontiguousarray(np.broadcast_to(
        np.stack([b_ada[2 * D:3 * D], b_ada[5 * D:6 * D]])[None], (128, 2, D)))
    g02 = np.ascontiguousarray(np.stack([norm_g[0].reshape(8, 128).T, norm_g[2].reshape(8, 128).T], axis=1))
    g13 = np.ascontiguousarray(np.broadcast_to(np.stack([norm_g[1], norm_g[3]])[None], (128, 2, D)))
    nwv = np.ascontiguousarray(np.stack([gnw, hnw], axis=1))

    def perm_cols(flip):
        cols = []
        cols.append(np.arange(0, 4096))
        for base in (4096, 4112):
            idx = base + np.arange(16).reshape(2, 8)
            cols.append((idx[::-1] if flip else idx).reshape(-1))
        idx = 4128 + np.arange(2048).reshape(2, 1024)
        cols.append((idx[::-1] if flip else idx).reshape(-1))
        cols.append(np.arange(6176, IN_W))
        return np.concatenate(cols)

    maps = []
    shared = dict(w_ada=w_ada, bada_fm=bada_fm, bgt=bgt, g02=g02, g13=g13, nw=nwv,
                  w_a=np.ascontiguousarray(np.asarray(inputs["w_branch_a"], f)[0]),
                  w_b=np.ascontiguousarray(np.asarray(inputs["w_branch_b"], f)[0]),
                  w_o=np.ascontiguousarray(np.asarray(inputs["w_out"], f)[0]),
                  w_1=np.ascontiguousarray(np.asarray(inputs["w_mlp_in"], f)[0]),
                  w_2=np.ascontiguousarray(np.asarray(inputs["w_mlp_out"], f)[0]),
                  consts=consts)
    per_flip = {}
    for flip in (0, 1):
        win_l = np.ascontiguousarray(w_in[:, perm_cols(flip)])
        cw = conv_w[::-1, ::-1, :] if flip else conv_w
        convw = np.ascontiguousarray(cw.reshape(9, 24, 128).transpose(2, 1, 0))
        al = a_log[::-1] if flip else a_log
        db = dt_b[::-1] if flip else dt_b
        abv = np.ascontiguousarray(np.broadcast_to(np.stack([al.reshape(-1), db.reshape(-1)])[None], (128, 2, 16)))
        ll = lbl[:, ::-1, :] if flip else lbl
        lblv = np.ascontiguousarray(ll.reshape(2, 2, 8, 128).transpose(3, 0, 1, 2).reshape(128, 2, 16))
        per_flip[flip] = dict(w_in=win_l, convw=convw, ab=abv, lbl=lblv)
    for core in range(8):
        b, s = core // 2, core % 2
        xb = x[b][::-1] if s else x[b]
        cb = ctx[b][::-1] if s else ctx[b]
        cfm = np.ascontiguousarray(np.stack([c[b], c_ctx], axis=1).reshape(8, 128, 2).transpose(1, 0, 2))
        m = dict(shared)
        m.update(per_flip[s])
        m.update(x=np.ascontiguousarray(xb), ctx=np.ascontiguousarray(cb), cfm=cfm)
        maps.append(m)
    return maps


_NC_CACHE = {}


def kernel(**inputs):
    maps = host_prep(inputs)
    if "nc" not in _NC_CACHE:
        import os
        _NC_CACHE["nc"] = build(cutn=int(os.environ.get("KCUT", "0")))[0]
    nc = _NC_CACHE["nc"]
    res = run_bass_kernel_spmd(nc, maps, core_ids=list(range(8)))
    out = np.zeros((4, 4096, D), np.float32)
    for core in range(8):
        b, s = core // 2, core % 2
        y = np.asarray(res.results[core]["y"])
        if s:
            out[b, 2048:] = y[::-1]
        else:
            out[b, :2048] = y
    return out
```

```python
import numpy as np
from contextlib import ExitStack
import concourse.bass as bass
import concourse.mybir as mybir
from concourse.bass_utils import run_bass_kernel_spmd

F32 = mybir.dt.float32
BF16 = mybir.dt.bfloat16
AF = mybir.ActivationFunctionType
ALU = mybir.AluOpType
AX = mybir.AxisListType

D = 1024
NH = 8
EPS = 1e-6
NOWN = 16
NOTH = 16
NLOC = NOWN + NOTH
NCTX = 2
NT = NCTX + NLOC
TOK0 = NCTX * 128
NTOK = NT * 128
NEG = -30000.0
IN_W = 11296
O_Q, O_K, O_V, O_GA, O_A, O_B, O_F, O_QB, O_IB, O_GB, O_MG = (
    0, 1024, 2048, 3072, 4096, 4112, 4128, 6176, 7200, 8224, 9248)

C_I, C_ONE, C_UF, C_UB, C_NSF, C_NSB, C_NIF, C_NIB, C_HMF, C_HMB, C_RST, C_SUB = range(12)
NCONST = 12


class R:
    __slots__ = ("w", "r")

    def __init__(self):
        self.w = None
        self.r = {}


class V:
    def __init__(self, ap, res, psum=False):
        self.ap = ap
        self.res = res
        self.psum = psum

    def __getitem__(self, k):
        return V(self.ap[k], self.res, self.psum)

    def bc(self, shape):
        return V(self.ap.to_broadcast(list(shape)), self.res, self.psum)

    def re(self, s, **kw):
        return V(self.ap.rearrange(s, **kw), self.res, self.psum)


def _flat(vs):
    out = []
    for v in vs:
        if isinstance(v, V):
            if isinstance(v.res, tuple):
                out.extend(v.res)
            else:
                out.append(v.res)
    return out


class KB:
    def __init__(self, nc, es):
        self.nc = nc
        self.es = es
        self.eng = {"pe": nc.tensor, "act": nc.scalar, "dve": nc.vector,
                    "pool": nc.gpsimd, "sp": nc.sync}
        self.sem = {n: es.enter_context(nc.semaphore("s_" + n)) for n in self.eng}
        self.cnt = {n: 0 for n in self.eng}
        self.known = {n: {} for n in self.eng}
        self.nslots = {"sp": 24, "pool": 12, "act": 4}
        self.dsem = {}
        self.dval = {}
        self.dnext = {q: 0 for q in self.nslots}
        for q, n in self.nslots.items():
            for i in range(n):
                key = ("d", q, i)
                self.dsem[key] = es.enter_context(nc.semaphore("d_%s_%d" % (q, i)))
                self.dval[key] = 0
        self.out_toks = []
        self.uid = 0
        self.ninst = 0

    def sb(self, shape, dt=F32, name=None):
        self.uid += 1
        t = self.es.enter_context(self.nc.sbuf_tensor("sb_" + (name or ("t%d" % self.uid)), list(shape), dt))
        return t

    def sbv(self, shape, dt=F32, name=None):
        t = self.sb(shape, dt, name)
        return V(t[tuple(slice(None) for _ in shape)], R())

    def ps(self, shape, dt=F32, name=None):
        self.uid += 1
        t = self.es.enter_context(self.nc.psum_tensor("ps_" + (name or ("p%d" % self.uid)), list(shape), dt))
        return V(t[tuple(slice(None) for _ in shape)], R(), True)

    def ring(self, n, shape, dt=F32, psum=False):
        items = [(self.ps(shape, dt) if psum else self.sbv(shape, dt)) for _ in range(n)]
        return Ring(items)

    def _semof(self, key):
        return self.sem[key] if isinstance(key, str) else self.dsem[key]

    def _wait(self, e, toks):
        kn = self.known[e]
        for tok in toks:
            if tok is None:
                continue
            key, val, snap = tok
            if kn.get(key, 0) >= val:
                continue
            self.eng[e].wait_ge(self._semof(key), val)
            self.ninst += 1
            kn[key] = val
            for k2, v2 in snap.items():
                if kn.get(k2, 0) < v2:
                    kn[k2] = v2

    @staticmethod
    def _toks(rd, wr):
        toks = []
        for r in rd:
            toks.append(r.w)
        for r in wr:
            toks.append(r.w)
            toks.extend(r.r.values())
        return toks

    @staticmethod
    def _commit(tok, rd, wr):
        for r in rd:
            r.r[tok[0]] = tok
        for r in wr:
            r.w = tok
            r.r = {}

    def op(self, e, fn, rd, wr):
        wr = list(wr) + [v for v in rd if isinstance(v, V) and v.psum]
        rd = _flat(rd)
        wr = _flat(wr)
        self._wait(e, self._toks(rd, wr))
        inst = fn(self.eng[e])
        self.cnt[e] += 1
        self.ninst += 1
        inst.then_inc(self.sem[e], 1)
        tok = (e, self.cnt[e], dict(self.known[e]))
        self._commit(tok, rd, wr)

    def dma(self, q, out, in_, is_out=False):
        rd = _flat([in_])
        wr = _flat([out])
        i = self.dnext[q]
        self.dnext[q] = (i + 1) % self.nslots[q]
        key = ("d", q, i)
        prev = self.dval[key]
        toks = self._toks(rd, wr)
        if prev > 0:
            toks.append((key, prev, {}))
        self._wait(q, toks)
        oap = out.ap if isinstance(out, V) else out
        iap = in_.ap if isinstance(in_, V) else in_
        inst = self.eng[q].dma_start(out=oap, in_=iap)
        self.ninst += 1
        val = prev + 16
        inst.then_inc(self.dsem[key], 16)
        self.dval[key] = val
        tok = (key, val, dict(self.known[q]))
        self._commit(tok, rd, wr)
        if is_out:
            self.out_toks.append(tok)

    def barrier(self):
        toks = [(e, c, {}) for e, c in self.cnt.items() if c > 0]
        toks += [(k_, v, {}) for k_, v in self.dval.items() if v > 0]
        for e in self.eng:
            self._wait(e, toks)

    def finish(self):
        self._wait("sp", self.out_toks)
        toks = [(k, v, {}) for k, v in self.dval.items() if v > 0]
        self._wait("sp", toks)

    def mm(self, out, lhsT, rhs, start=True, stop=True):
        self.op("pe", lambda e: e.matmul(out.ap, lhsT=lhsT.ap, rhs=rhs.ap, start=start, stop=stop),
                [lhsT, rhs] + ([] if start else [out]), [out])

    def tr(self, out, in_, ident):
        self.op("pe", lambda e: e.transpose(out.ap, in_.ap, ident.ap), [in_, ident], [out])

    def act(self, e, out, in_, func, bias=None, scale=1.0):
        kw = {}
        if bias is not None:
            kw["bias"] = bias.ap if isinstance(bias, V) else bias
        kw["scale"] = scale.ap if isinstance(scale, V) else scale
        assert e == "act"
        self.op("act", lambda en: en.activation(out=out.ap, in_=in_.ap, func=func, **kw),
                [in_, bias, scale], [out])

    def tt(self, e, out, a, b, op):
        self.op(e, lambda en: en.tensor_tensor(out=out.ap, in0=a.ap, in1=b.ap, op=op), [a, b], [out])

    def ts(self, e, out, a, s1, op0, s2=None, op1=None):
        s1a = s1.ap if isinstance(s1, V) else s1
        s2a = s2.ap if isinstance(s2, V) else s2
        if op1 is None:
            fn = lambda en: en.tensor_scalar(out=out.ap, in0=a.ap, scalar1=s1a, scalar2=None, op0=op0)
        else:
            fn = lambda en: en.tensor_scalar(out=out.ap, in0=a.ap, scalar1=s1a, scalar2=s2a, op0=op0, op1=op1)
        self.op(e, fn, [a, s1, s2], [out])

    def stt(self, e, out, a, s, b, op0, op1):
        sa = s.ap if isinstance(s, V) else s
        self.op(e, lambda en: en.scalar_tensor_tensor(out=out.ap, in0=a.ap, scalar=sa, in1=b.ap, op0=op0, op1=op1),
                [a, s, b], [out])

    def cp(self, e, out, in_):
        if e == "act":
            self.op("act", lambda en: en.copy(out=out.ap, in_=in_.ap), [in_], [out])
        else:
            self.op(e, lambda en: en.tensor_copy(out=out.ap, in_=in_.ap), [in_], [out])

    def memset(self, e, out, val):
        self.op(e, lambda en: en.memset(out.ap, val), [], [out])

    def red(self, e, out, in_, op=ALU.add):
        self.op(e, lambda en: en.tensor_reduce(out=out.ap, in_=in_.ap, axis=AX.X, op=op), [in_], [out])

    def scan(self, out, d0, d1, init, op0, op1):
        self.op("dve", lambda en: en.tensor_tensor_scan(out=out.ap, data0=d0.ap, data1=d1.ap, initial=init,
                                                        op0=op0, op1=op1), [d0, d1], [out])


class Ring:
    def __init__(self, items):
        self.items = items
        self.i = 0

    def get(self):
        it = self.items[self.i % len(self.items)]
        self.i += 1
        return it


class Cut(Exception):
    pass


class Done(Exception):
    pass


def build(stage=99, dbg_names=(), cutn=0):
    nc = bass.Bass("TRN2", target_bir_lowering=False)

    def cut(n):
        if cutn == n:
            raise Cut()

    def din(name, shape, dt=F32):
        return nc.dram_tensor(name, list(shape), dt, kind="ExternalInput").ap()

    x_d = din("x", [NLOC * 128, D])
    ctx_d = din("ctx", [NCTX * 128, D])
    cfm_d = din("cfm", [128, 8, 2])
    wada_d = din("w_ada", [D, 6 * D])
    badafm_d = din("bada_fm", [128, 48])
    bgt_d = din("bgt", [128, 2, D])
    g02_d = din("g02", [128, 2, 8])
    g13_d = din("g13", [128, 2, D])
    win_d = din("w_in", [D, IN_W])
    convw_d = din("convw", [128, 24, 9])
    ab_d = din("ab", [128, 2, 16])
    lbl_d = din("lbl", [128, 2, 16])
    nw_d = din("nw", [128, 2])
    wa_d = din("w_a", [D, D])
    wb_d = din("w_b", [D, D])
    wo_d = din("w_o", [D, D])
    w1_d = din("w_1", [D, 4 * D])
    w2_d = din("w_2", [4 * D, D])
    cst_d = din("consts", [128, NCONST * 128])
    y_d = nc.dram_tensor("y", [NOWN * 128, D], F32, kind="ExternalOutput").ap()
    dbg_d = {}

    es = ExitStack()
    with es:
        k = KB(nc, es)

        def dbg(name, v, shape):
            if name in dbg_names:
                d = nc.dram_tensor("dbg_" + name, list(shape), v.ap.dtype, kind="ExternalOutput").ap()
                dbg_d[name] = d
                k.dma("sp", d, v, is_out=True)

        PBIG = k.ring(3, [128, 512], F32, psum=True)
        _psm = [k.ps([128, 512], F32) for _ in range(4)]
        PSM = Ring([V(b.ap[:, j * 128:(j + 1) * 128], b.res, True) for j in range(4) for b in _psm])
        _pbf = k.ps([128, 512], BF16)
        PBF = Ring([V(_pbf.ap[:, j * 128:(j + 1) * 128], _pbf.res, True) for j in range(4)])

        class Scope:
            def __enter__(s_):
                s_.es = ExitStack()
                s_.es.__enter__()
                s_.save = k.es
                k.es = s_.es
                return s_

            def __exit__(s_, *a):
                k.barrier()
                k.es = s_.save
                s_.es.__exit__(None, None, None)
                return False

        cst = k.sbv([128, NCONST * 128], F32, "cst")
        k.dma("sp", cst, cst_d)

        def C(i):
            return cst[:, i * 128:(i + 1) * 128]

        cbf = k.sbv([128, 2 * 128], BF16, "cbf")
        k.cp("dve", cbf, cst[:, 0:256])
        I_bf = cbf[:, 0:128]
        ONE_bf = cbf[:, 128:256]
        I_f = C(C_I)
        ONE_f = C(C_ONE)

        cfm = k.sbv([128, 8, 2], F32, "cfm")
        k.dma("sp", cfm, cfm_d)
        csl = k.sbv([128, 8, 2], F32, "csl")
        k.act("act", csl, cfm, AF.Silu)
        badafm = k.sbv([128, 48], F32, "badafm")
        k.dma("sp", badafm, badafm_d)
        g02 = k.sbv([128, 2, 8], F32, "g02")
        k.dma("sp", g02, g02_d)
        ab = k.sbv([128, 2, 16], F32, "ab")
        k.dma("sp", ab, ab_d)
        lbl = k.sbv([128, 2, 16], F32, "lbl")
        k.dma("sp", lbl, lbl_d)
        nw = k.sbv([128, 2], F32, "nw")
        k.dma("sp", nw, nw_d)
        convw = k.sbv([128, 24, 9], F32, "convw")
        k.dma("sp", convw, convw_d)

        nA = k.sbv([128, 16], F32, "nA")
        k.act("act", nA, ab[:, 0, :], AF.Exp)
        k.ts("dve", nA, nA, -1.0, ALU.mult)
        dtb = ab[:, 1, :]
        lbd = k.sbv([128, 16], F32, "lbd")
        k.tt("dve", lbd, lbl[:, 0, :], lbl[:, 1, :], ALU.subtract)
        lb = k.sbv([128, 16], F32, "lb")
        oml = k.sbv([128, 16], F32, "oml")
        k.act("act", lb, lbd, AF.Sigmoid)
        k.act("act", oml, lbd, AF.Sigmoid, scale=-1.0)
        noml = k.sbv([128, 16], F32, "noml")
        k.ts("dve", noml, oml, -1.0, ALU.mult)

        modfm = k.sbv([128, 48, 2], F32, "modfm")
        grow = k.sbv([128, 2, D], F32, "grow")
        wada_v = wada_d.rearrange("(kt p) n -> p kt n", p=128)
        pp_mod = PBIG
        with ExitStack() as es2:
            es_save = k.es
            k.es = es2
            wring = k.ring(2, [128, 8, 512], F32)
            cbc = k.sbv([128, 8, 128], F32, "cbc")
            k.cp("dve", cbc, csl[:, :, 0:1].bc([128, 8, 128]))
            bgt = k.sbv([128, 2, D], F32, "bgt")
            k.dma("sp", bgt, bgt_d)
            g13 = k.sbv([128, 2, D], F32, "g13")
            k.dma("sp", g13, g13_d)
            for ch in (2, 3, 0, 1, 6, 7, 8, 9, 4, 5, 10, 11):
                wt = wring.get()
                k.dma("sp", wt[:, 0:4, :], wada_v[:, 0:4, ch * 512:(ch + 1) * 512])
                k.dma("sp", wt[:, 4:8, :], wada_v[:, 4:8, ch * 512:(ch + 1) * 512])
                pm = pp_mod.get()
                for j in range(4):
                    ct = ch * 4 + j
                    for kt in range(8):
                        k.mm(pm[:, 2 * j:2 * j + 2], wt[:, kt, j * 128:(j + 1) * 128], csl[:, kt, :],
                             start=(kt == 0), stop=(kt == 7))
                for j in range(4):
                    ct = ch * 4 + j
                    k.ts("dve", modfm[:, ct, :], pm[:, 2 * j:2 * j + 2], badafm[:, ct:ct + 1], ALU.add)
                if ch in (4, 5, 10, 11):
                    gi = 0 if ch < 6 else 1
                    c0 = (ch % 2) * 512
                    pg = pp_mod.get()
                    for kt in range(8):
                        k.mm(pg, cbc[:, kt, :], wt[:, kt, :], start=(kt == 0), stop=(kt == 7))
                    k.tt("dve", grow[:, gi, c0:c0 + 512], pg, bgt[:, gi, c0:c0 + 512], ALU.add)
            k.tt("dve", grow[:, 0, :], grow[:, 0, :], g13[:, 0, :], ALU.mult)
            k.tt("dve", grow[:, 1, :], grow[:, 1, :], g13[:, 1, :], ALU.mult)
            k.barrier()
            k.es = es_save
        A0 = k.sbv([128, 8, 2], F32, "A0")
        A2 = k.sbv([128, 8, 2], F32, "A2")
        for c_ in range(2):
            k.stt("dve", A0[:, :, c_], modfm[:, 8:16, c_], 1.0, g02[:, 0, :], ALU.add, ALU.mult)
            k.stt("dve", A2[:, :, c_], modfm[:, 32:40, c_], 1.0, g02[:, 1, :], ALU.add, ALU.mult)
        B0 = modfm[:, 0:8, :]
        B2 = modfm[:, 24:32, :]
        dbg("modfm", modfm, [128, 48, 2])
        dbg("grow", grow, [128, 2, D])

        hxT_t = k.sb([128, 8, NTOK], BF16, "hxT")
        hx = [V(hxT_t[:, :, t * 128:(t + 1) * 128], R()) for t in range(NT)]
        epsc = k.sbv([128, 1], F32, "epsc")
        k.memset("dve", epsc, EPS)
        onec = k.sbv([128, 1], F32, "onec")
        k.memset("dve", onec, 1.0)

        def hxs(c0, n):
            ts_ = tuple(hx[t].res for t in range(c0 // 128, (c0 + n + 127) // 128))
            return lambda kt: V(hxT_t[:, kt, c0:c0 + n], ts_)

        def rsqrt(out, in_, scale):
            k.act("act", out, in_, AF.Sqrt, bias=epsc[0:out.ap.shape[0], :], scale=scale)
            k.op("dve", lambda en: en.reciprocal(out=out.ap, in_=out.ap), [out], [out])

        def norm_T(xt, sq, st, dst, A, Bm, col, eng_flip):
            k.tt("dve", sq, xt, xt, ALU.mult)
            k.red("dve", st[:, 0:1], sq)
            rsqrt(st[:, 1:2], st[:, 0:1], 1.0 / D)
            k.act("act", sq, xt, AF.Copy, scale=st[:, 1:2])
            for half in range(2):
                p = PBIG.get()
                for j in range(4):
                    kt = half * 4 + j
                    k.tr(p[:, j * 128:(j + 1) * 128], sq[:, kt * 128:(kt + 1) * 128], I_f)
                for j in range(4):
                    kt = half * 4 + j
                    if (kt + eng_flip) % 2 == 0:
                        k.act("act", dst[:, kt, :], p[:, j * 128:(j + 1) * 128], AF.Identity,
                              bias=Bm[:, kt, col:col + 1], scale=A[:, kt, col:col + 1])
                    else:
                        k.ts("dve", dst[:, kt, :], p[:, j * 128:(j + 1) * 128], A[:, kt, col:col + 1], ALU.mult,
                             Bm[:, kt, col:col + 1], ALU.add)

        with Scope():
            xring = k.ring(3, [128, D], F32)
            sqring = k.ring(2, [128, D], F32)
            stat = k.ring(4, [128, 2], F32)
            for t in range(NT):
                xt = xring.get()
                src = ctx_d[t * 128:(t + 1) * 128, :] if t < NCTX else x_d[(t - NCTX) * 128:(t - NCTX + 1) * 128, :]
                k.dma("sp", xt, src)
                norm_T(xt, sqring.get(), stat.get(), hx[t], A0, B0, 1 if t < NCTX else 0, t)
            dbg("hxT", V(hxT_t[:, :, 0:512], hx[0].res), [128, 8, 512])

        win_v = win_d.rearrange("(kt p) n -> p kt n", p=128)
        def _phase2plus():
            ydr = nc.dram_tensor("ydr", [2, NH, 128, NOWN * 128], BF16, kind="Internal").ap()
            ydr_res = [[R() for _ in range(NH)] for _ in range(2)]

            wring = k.ring(5, [128, 8, 128], BF16)

            wst = k.ring(2, [128, 8, 128], F32)

            def load_w(col0):
                st_ = wst.get()
                k.dma("sp", st_, win_v[:, :, col0:col0 + 128])
                w = wring.get()
                k.cp("dve", w, st_)
                return w

            def project_fm(w, c0, n, dst_fn):
                hv = hxs(c0, n)
                p = PBIG.get()
                for kt in range(8):
                    k.mm(p[:, 0:n], w[:, kt, :], hv(kt), start=(kt == 0), stop=(kt == 7))
                dst_fn(p[:, 0:n])

            with Scope():
                names = ("la", "lnb", "negg", "bsc", "bg", "ed", "dl")
                sm = {n: k.sbv([128, NT, 16], F32, "sm_" + n) for n in names}
                sm["la"].res = tuple(R() for _ in range(NT))
                smres = [R() for _ in range(NT)]
                with Scope():
                    wab = k.sbv([128, 8, 32], BF16, "wab")
                    wab32 = k.sbv([128, 8, 32], F32, "wab32")
                    k.dma("sp", wab32, win_v[:, :, O_A:O_A + 32])
                    k.cp("dve", wab, wab32)
                    tmp = k.ring(6, [128, 32], F32)
                    for t in range(NT):
                        hv = hxs(t * 128, 128)
                        pab = PSM.get()
                        for kt in range(8):
                            k.mm(pab[:, 0:32], hv(kt), wab[:, kt, :], start=(kt == 0), stop=(kt == 7))
                        xx = tmp.get()
                        k.tt("dve", xx[:, 0:16], pab[:, 0:16], dtb, ALU.add)
                        k.ts("dve", xx[:, 16:32], pab[:, 16:32], -1.0, ALU.mult)
                        if t == 0:
                            dbg("d_wab", wab, [128, 8, 32])
                            dbg("d_wab32", wab32, [128, 8, 32])
                            dbg("d_hx0", hx[0], [128, 8, 128])
                            dbg("d_xx", xx, [128, 32])
                            dbg("d_nA", nA, [128, 16])
                            dbg("d_ab", ab, [128, 2, 16])
                        aa = tmp.get()
                        k.ts("dve", aa, xx, -1.0, ALU.mult)
                        k.tt("dve", aa, aa, xx, ALU.max)
                        if t == 0:
                            dbg("d_abs", aa, [128, 32])
                        k.act("act", aa, aa, AF.Exp, scale=-1.0)
                        if t == 0:
                            dbg("d_exp", aa, [128, 32])
                        k.act("act", aa, aa, AF.Ln, bias=onec)
                        if t == 0:
                            dbg("d_ln", aa, [128, 32])
                        k.ts("dve", xx, xx, 0.0, ALU.max)
                        k.tt("dve", xx, xx, aa, ALU.add)
                        sres = smres[t]

                        def S_(n):
                            return V(sm[n].ap[:, t, :], sres)
                        la_t = V(sm["la"].ap[:, t, :], sm["la"].res[t])
                        k.tt("dve", la_t, xx[:, 0:16], nA, ALU.mult)
                        k.ts("dve", S_("lnb"), xx[:, 16:32], -1.0, ALU.mult)
                        pg = PSM.get()
                        k.mm(pg[:, 0:8], C(C_UF), la_t[:, 0:8])
                        k.mm(pg[:, 8:16], C(C_UB), la_t[:, 8:16])
                        k.mm(pg[:, 16:32], ONE_f, la_t)
                        gg = tmp.get()
                        k.cp("dve", gg, pg[:, 0:32])
                        k.ts("dve", S_("negg"), gg[:, 0:16], -1.0, ALU.mult)
                        k.act("act", S_("bsc"), S_("lnb"), AF.Exp)
                        t2 = tmp.get()
                        k.tt("dve", t2[:, 0:16], S_("lnb"), gg[:, 0:16], ALU.add)
                        k.act("act", S_("bg"), t2[:, 0:16], AF.Exp)
                        k.tt("dve", t2[:, 16:32], gg[:, 16:32], gg[:, 0:16], ALU.subtract)
                        k.act("act", S_("ed"), t2[:, 16:32], AF.Exp)
                        k.act("act", S_("dl"), gg[:, 16:32], AF.Exp)

                if cutn == 1:
                    for n_ in names:
                        dbg("sm_" + n_, V(sm[n_].ap, tuple(smres) + tuple(sm["la"].res)), [128, NT, 16])
                cut(1)

                def SM(n, t, hd):
                    r = sm["la"].res[t] if n == "la" else smres[t]
                    return V(sm[n].ap[:, t, hd:hd + 1], r)

                NSLOT = 4
                t128s = [k.ring(8, [128, 128], F32) for _ in range(NSLOT)]
                b128s = [k.ring(8, [128, 128], BF16) for _ in range(NSLOT)]

                def run_units(unit_fn, arglist, nslots):
                    pending = list(enumerate(arglist))
                    active = []
                    free = list(range(nslots))
                    next_chain = 0
                    while pending or active:
                        while pending and free:
                            idx, a_ = pending.pop(0)
                            sl = free.pop(0)
                            active.append([idx, unit_fn(*a_, slot=sl), sl, "prep"])
                        for item in list(active):
                            idx, g, sl, st = item
                            if st == "wait" and idx != next_chain:
                                continue
                            try:
                                r_ = next(g)
                                if r_ == "chain":
                                    item[3] = "wait"
                            except StopIteration:
                                active.remove(item)
                                free.append(sl)
                                next_chain += 1
                with Scope():
                    pre_loc = k.sbv([128, 66, 66], BF16, "pre_loc")
                    pre_ctx = k.sbv([128, 258], BF16, "pre_ctx")
                    k.memset("pool", pre_loc, 0.0)
                    k.memset("pool", pre_ctx, 0.0)
                    dgr = k.ring(1, [128, 9, 128], BF16)
                    KT = k.sbv([128, NTOK], BF16, "KT")
                    QT = k.sbv([128, NOWN * 128], BF16, "QT")
                    ktok = k.sbv([128, NT, 128], BF16, "ktok")
                    vtok = k.sbv([128, NT, 128], BF16, "vtok")
                    gaT = k.sbv([128, NOWN * 128], BF16, "gaT")
                    oBT = k.sbv([128, NOWN * 128], BF16, "oBT")
                    yT = k.ring(1, [128, NOWN * 128], BF16)
                    f512 = k.ring(2, [128, 512], F32)
                    g512 = k.ring(2, [128, 512], F32)
                    vt512 = k.ring(2, [128, 512], BF16)
                    S32 = [k.sbv([128, 128], F32, "S32_%d" % d) for d in range(2)]
                    Sbf = [k.ring(2, [128, 128], BF16) for d in range(2)]

                    def conv_head(w, ct, which):
                        dg = dgr.get()
                        for tap in range(9):
                            k.ts("dve", dg[:, tap, :], I_bf, convw[:, ct, tap:tap + 1], ALU.mult)
                        cut(21)
                        if which != "q":
                            project_fm(w, 0, 256, lambda p: k.cp("act", pre_ctx[:, 1:257], p))
                        for g in range(8 if which != "q" else 5):
                            project_fm(w, TOK0 + 512 * g, 512,
                                       lambda p: k.cp("act", pre_loc[:, 1 + 8 * g:9 + 8 * g, 1:65],
                                                      p.re("p (a b) -> p a b", b=64)))
                        cut(22)
                        def stage1(g):
                            pc = PBIG.get()
                            if g < 0:
                                for kw in range(3):
                                    k.mm(pc[:, 0:256], dg[:, 3 + kw, :], pre_ctx[:, kw:kw + 256], start=(kw == 0), stop=(kw == 2))
                            else:
                                pc3 = pc.re("p (a b) -> p a b", b=64)
                                for tap in range(9):
                                    kh, kw = tap // 3, tap % 3
                                    k.mm(pc3, dg[:, tap, :], pre_loc[:, 8 * g + kh:8 * g + kh + 8, kw:kw + 64],
                                         start=(tap == 0), stop=(tap == 8))
                            return pc

                        def stage2(g, pc):
                            n = 256 if g < 0 else 512
                            c0 = 0 if g < 0 else TOK0 + 512 * g
                            if which == "v":
                                vt = vt512.get()
                                k.act("act", vt[:, 0:n], pc[:, 0:n], AF.Silu)
                                for j in range(n // 128):
                                    pb = PBF.get()
                                    k.tr(pb, vt[:, j * 128:(j + 1) * 128], I_bf)
                                    k.cp("dve", vtok[:, c0 // 128 + j, :], pb)
                            else:
                                qs = f512.get()
                                k.act("act", qs[:, 0:n], pc[:, 0:n], AF.Silu)
                                sq = g512.get()
                                k.tt("dve", sq[:, 0:n], qs[:, 0:n], qs[:, 0:n], ALU.mult)
                                pss = PBIG.get()
                                k.mm(pss[:, 0:n], ONE_f, sq[:, 0:n])
                                k.act("act", sq[:, 0:n], pss[:, 0:n], AF.Sqrt, bias=epsc)
                                k.op("dve", lambda en: en.reciprocal(out=sq.ap[:, 0:n], in_=sq.ap[:, 0:n]), [sq], [sq])
                                if which == "k":
                                    k.tt("dve", KT[:, c0:c0 + n], qs[:, 0:n], sq[:, 0:n], ALU.mult)
                                    for j in range(n // 128):
                                        pb = PBF.get()
                                        k.tr(pb, KT[:, c0 + j * 128:c0 + (j + 1) * 128], I_bf)
                                        k.cp("dve", ktok[:, c0 // 128 + j, :], pb)
                                else:
                                    k.stt("dve", QT[:, c0 - TOK0:c0 - TOK0 + n], qs[:, 0:n], 128.0 ** -0.5, sq[:, 0:n],
                                          ALU.mult, ALU.mult)

                        prev_ = None
                        for g in (range(-1, 8) if which != "q" else range(0, 4)):
                            pc_ = stage1(g)
                            if prev_ is not None:
                                stage2(*prev_)
                            prev_ = (g, pc_)
                        stage2(*prev_)

                    def gdn_unit(h, d, t, full, own_idx, slot=0):
                        T1, B1 = t128s[slot], b128s[slot]
                        hd = d * 8 + h
                        UM = C(C_UF if d == 0 else C_UB)
                        NS = C(C_NSF if d == 0 else C_NSB)
                        NI = C(C_NIF if d == 0 else C_NIB)
                        la_bc = SM("la", t, hd).bc([128, 128])
                        lnb_bc = SM("lnb", t, hd).bc([128, 128])
                        negg = SM("negg", t, hd)
                        cs = slice(t * 128, (t + 1) * 128)
                        qs_ = slice(own_idx * 128, (own_idx + 1) * 128)
                        pA = PSM.get()
                        k.mm(pA, la_bc, UM, start=True, stop=False)
                        k.mm(pA, lnb_bc, I_f, start=False, stop=False)
                        k.mm(pA, I_f, NS, start=False, stop=True)
                        LA = T1.get()
                        k.act("act", LA, pA, AF.Exp, bias=negg)
                        yield
                        pK = PSM.get()
                        k.mm(pK, KT[:, cs], KT[:, cs])
                        Xt = T1.get()
                        k.stt("dve", Xt, pK, -1.0, LA, ALU.mult, ALU.mult)
                        yield
                        pb = PSM.get()
                        k.tr(pb, Xt, I_f)
                        Xn = T1.get()
                        k.cp("act", Xn, pb)
                        yield
                        Rk = T1.get()
                        k.tt("dve", Rk, Xt, I_f, ALU.add)
                        yield
                        for lvl in range(1, 7):
                            pn = PSM.get()
                            k.mm(pn, Xt, Xn)
                            Xn2 = T1.get()
                            k.cp("act", Xn2, pn)
                            yield
                            if lvl < 6:
                                pt_ = PSM.get()
                                k.mm(pt_, Xn, Xt)
                                Xt2 = T1.get()
                                k.cp("dve", Xt2, pt_)
                                yield
                            pr = PSM.get()
                            k.mm(pr, Xn2, Rk)
                            Rk2 = T1.get()
                            k.tt("dve", Rk2, pr, Rk, ALU.add)
                            yield
                            Xn, Rk = Xn2, Rk2
                            if lvl < 6:
                                Xt = Xt2
                        Tb = B1.get()
                        k.act("act", Tb, Rk, AF.Copy, scale=SM("bsc", t, hd))
                        yield
                        Tbg = B1.get()
                        k.act("act", Tbg, Rk, AF.Copy, scale=SM("bg", t, hd))
                        yield
                        pu = PSM.get()
                        k.mm(pu, Tb, vtok[:, t, :])
                        u32 = T1.get()
                        k.cp("act", u32, pu)
                        yield
                        pw = PSM.get()
                        k.mm(pw, ktok[:, t, :], Tbg)
                        wT = B1.get()
                        k.cp("dve", wT, pw)
                        yield
                        kdec = B1.get()
                        k.ts("dve", kdec, ktok[:, t, :], SM("ed", t, hd), ALU.mult)
                        yield
                        if full:
                            pL = PSM.get()
                            k.mm(pL, la_bc, UM, start=True, stop=False)
                            k.mm(pL, I_f, NI, start=False, stop=True)
                            Lt = T1.get()
                            k.act("act", Lt, pL, AF.Exp, bias=negg)
                            yield
                            pQ = PSM.get()
                            k.mm(pQ, KT[:, cs], QT[:, qs_])
                            attnT = B1.get()
                            k.tt("dve", attnT, pQ, Lt, ALU.mult)
                            yield
                            pE = PSM.get()
                            k.mm(pE, la_bc, UM)
                            Er = B1.get()
                            k.act("act", Er, pE, AF.Exp)
                            yield
                            qgT = B1.get()
                            k.tt("dve", qgT, QT[:, qs_], Er, ALU.mult)
                            yield
                        if h == 0 and d == 1 and t == 1:
                            dbg("LA", LA, [128, 128])
                            dbg("Rk", Rk, [128, 128])
                            dbg("Tb", Tb, [128, 128])
                            dbg("u32", u32, [128, 128])
                            dbg("wT", wT, [128, 128])
                            dbg("kdec", kdec, [128, 128])
                            dbg("smla", sm["la"][:, 1, :], [128, 16])
                            dbg("smnegg", V(sm["negg"].ap[:, 1, :], smres[1]), [128, 16])
                            dbg("smbg", V(sm["bg"].ap[:, 1, :], smres[1]), [128, 16])
                            dbg("smed", V(sm["ed"].ap[:, 1, :], smres[1]), [128, 16])
                        yield "chain"
                        Sb = Sbf[d].items[(Sbf[d].i - 1) % 2]
                        pws = PSM.get()
                        k.mm(pws, wT, Sb)
                        vnew = B1.get()
                        k.tt("dve", vnew, u32, pws, ALU.subtract)
                        yield
                        if full:
                            po = PSM.get()
                            k.mm(po, Sb, qgT, start=True, stop=False)
                            k.mm(po, vnew, attnT, start=False, stop=True)
                            oc = slice(own_idx * 128, (own_idx + 1) * 128)
                            if d == 1:
                                k.cp("act", oBT[:, oc], po)
                                yield
                            else:
                                o32 = T1.get()
                                k.tt("dve", o32, po, oBT[:, oc], ALU.add)
                                yield
                                sq = T1.get()
                                k.tt("dve", sq, o32, o32, ALU.mult)
                                yield
                                pss = PSM.get()
                                k.mm(pss, ONE_f, sq)
                                rn = T1.get()
                                rsqrt(rn, pss, 1.0 / 128)
                                k.tt("dve", o32, o32, rn, ALU.mult)
                                yield
                                k.stt("dve", ycur[:, oc], o32, nw[:, 0:1], gaT[:, oc], ALU.mult, ALU.mult)
                                yield
                        pS = PSM.get()
                        k.mm(pS, kdec, vnew)
                        k.stt("dve", S32[d], S32[d], SM("dl", t, hd), pS, ALU.mult, ALU.add)
                        yield
                        Sn = Sbf[d].get()
                        k.cp("dve", Sn, S32[d])
                        yield

                    for h in range(NH if stage >= 3 else 1):
                        wq, wk, wv, wga = (load_w(O_Q + h * 128), load_w(O_K + h * 128),
                                           load_w(O_V + h * 128), load_w(O_GA + h * 128))
                        conv_head(wk, 8 + h, "k")
                        if h == 0:
                            dbg("wk", wk, [128, 8, 128])
                            dbg("wst", wst.items[1], [128, 8, 128])
                            dbg("prectx", pre_ctx, [128, 258])
                            dbg("preloc", pre_loc[:, 0:4, :], [128, 4, 66])
                            dbg("dg", dgr.items[0], [128, 9, 128])
                        cut(2)
                        conv_head(wv, 16 + h, "v")
                        conv_head(wq, h, "q")
                        cut(3)
                        for g in range(4):
                            project_fm(wga, TOK0 + 512 * g, 512,
                                       lambda p: k.act("act", gaT[:, 512 * g:512 * (g + 1)], p, AF.Silu))
                        ycur = yT.get()
                        allseq = []
                        for d in (1, 0):
                            k.memset("pool", S32[d], 0.0)
                            k.memset("pool", Sbf[d].get(), 0.0)
                            seq = []
                            if d == 0:
                                seq += [(t, False, -1) for t in range(NCTX)]
                                seq += [(NCTX + t, True, t) for t in range(NOWN)]
                            else:
                                seq += [(t, False, -1) for t in reversed(range(NCTX))]
                                seq += [(NCTX + t, False, -1) for t in reversed(range(NOWN, NLOC))]
                                seq += [(NCTX + t, True, t) for t in reversed(range(NOWN))]
                            allseq += [(h, d, t, full, oi) for (t, full, oi) in seq]
                        run_units(gdn_unit, allseq, 4)
                        if h == 0:
                            dbg("KT", KT[:, 0:1024], [128, 1024])
                            dbg("vtok", vtok[:, 0:4, :], [128, 4, 128])
                            dbg("ya0", ycur, [128, NOWN * 128])
                            dbg("S32", S32[0], [128, 128])
                        k.dma("sp", V(ydr[0, h], ydr_res[0][h]), ycur)
                cut(7)

                with Scope():
                    vtokH = k.sbv([128, NT, 128], BF16, "vtokH")
                    qbT = k.sbv([128, NOWN * 128], BF16, "qbT")
                    gbT = k.sbv([128, NOWN * 128], BF16, "gbT")
                    oBT = k.sbv([128, NOWN * 128], BF16, "oBTh")
                    yT = k.ring(2, [128, NOWN * 128], BF16)
                    S32 = [k.sbv([128, 128], F32, "HS32_%d" % d) for d in range(2)]
                    Sbf = [k.ring(2, [128, 128], BF16) for d in range(2)]
                    vmall = k.sbv([128, NT, 4, 128], BF16, "vmall")
                    ebrs = [k.ring(1, [128, 4], F32) for _ in range(3)]
                    RST = C(C_RST)
                    PM = C(C_SUB)

                    def hgrn_unit(h, d, t, full, own_idx, wf, slot=0):
                        T1, B1 = t128s[slot], b128s[slot]
                        hd = d * 8 + h
                        HM = C(C_HMF if d == 0 else C_HMB)
                        lbc, omlc, nomlc = lb[:, hd:hd + 1], oml[:, hd:hd + 1], noml[:, hd:hd + 1]
                        hv = hxs(t * 128, 128)
                        pf = PSM.get()
                        for kt in range(8):
                            k.mm(pf, wf[:, kt, :], hv(kt), start=(kt == 0), stop=(kt == 7))
                        sig = T1.get()
                        k.act("act", sig, pf, AF.Sigmoid)
                        yield
                        lf = T1.get()
                        k.act("act", lf, sig, AF.Ln, bias=lbc, scale=omlc)
                        yield
                        kb = T1.get()
                        k.ts("dve", kb, sig, nomlc, ALU.mult, omlc, ALU.add)
                        yield
                        bb = T1.get()
                        k.scan(bb, RST, lf, 0.0, ALU.mult, ALU.add)
                        yield
                        if d == 1:
                            b2 = T1.get()
                            k.stt("dve", b2, bb, -1.0, lf, ALU.mult, ALU.add)
                            yield
                            for c in range(4):
                                k.ts("dve", b2[:, 32 * c:32 * c + 32], b2[:, 32 * c:32 * c + 32],
                                     bb[:, 32 * c + 31:32 * c + 32], ALU.add)
                            bb = b2
                        lastcol = [(32 * c + 31) if d == 0 else (32 * c) for c in range(4)]
                        Dm = T1.get()
                        for c in range(4):
                            k.ts("dve", Dm[:, 32 * c:32 * c + 32], bb[:, 32 * c:32 * c + 32],
                                 bb[:, 32 * c + 15:32 * c + 16], ALU.subtract)
                        EK = T1.get()
                        k.act("act", EK, Dm, AF.Exp, scale=-1.0)
                        yield
                        Kp = B1.get()
                        k.tt("dve", Kp, kb, EK, ALU.mult)
                        yield
                        ED = T1.get()
                        eb = ebrs[slot].get()
                        for c in range(4):
                            lc = lastcol[c]
                            k.act("act", ED[:, 32 * c:32 * c + 32], bb[:, 32 * c:32 * c + 32], AF.Exp,
                                  bias=bb[:, lc:lc + 1], scale=-1.0)
                            yield
                        k.act("act", eb, bb[:, lastcol[0]:128:32], AF.Exp)
                        yield
                        kdT = B1.get()
                        k.tt("dve", kdT, kb, ED, ALU.mult)
                        yield
                        pb = PBF.get()
                        k.tr(pb, kdT, I_bf)
                        kdtok = B1.get()
                        k.cp("dve", kdtok, pb)
                        yield
                        pG = PBIG.get()
                        for c in range(4):
                            k.mm(pG[:, 128 * c:128 * c + 128], kdtok, vmall[:, t, c, :])
                        if full:
                            oc = slice(own_idx * 128, (own_idx + 1) * 128)
                            EQ = T1.get()
                            k.act("act", EQ, Dm, AF.Exp)
                            yield
                            Qp = B1.get()
                            k.tt("dve", Qp, qbT[:, oc], EQ, ALU.mult)
                            yield
                            pa = PSM.get()
                            k.mm(pa, Kp, Qp)
                            aT = B1.get()
                            k.tt("dve", aT, pa, HM, ALU.mult)
                            yield
                            EB = T1.get()
                            k.act("act", EB, bb, AF.Exp)
                            yield
                            Qb = B1.get()
                            k.tt("dve", Qb, qbT[:, oc], EB, ALU.mult)
                            yield
                            po = PSM.get()
                        yield "chain"
                        for c in (range(4) if d == 0 else reversed(range(4))):
                            Sb = Sbf[d].items[(Sbf[d].i - 1) % 2]
                            cs = slice(32 * c, 32 * c + 32)
                            if full:
                                k.mm(po[:, cs], Sb, Qb[:, cs], start=True, stop=False)
                                k.mm(po[:, cs], vtokH[:, t, :], aT[:, cs], start=False, stop=True)
                            k.stt("dve", S32[d], S32[d], eb[:, c:c + 1], pG[:, 128 * c:128 * c + 128], ALU.mult, ALU.add)
                            yield
                            Sn = Sbf[d].get()
                            k.cp("act", Sn, S32[d])
                            yield
                        if full:
                            if d == 1:
                                k.cp("act", oBT[:, oc], po)
                                yield
                            else:
                                o32 = T1.get()
                                k.tt("dve", o32, po, oBT[:, oc], ALU.add)
                                yield
                                sq = T1.get()
                                k.tt("dve", sq, o32, o32, ALU.mult)
                                yield
                                pss = PSM.get()
                                k.mm(pss, ONE_f, sq)
                                rn = T1.get()
                                rsqrt(rn, pss, 1.0 / 128)
                                k.tt("dve", o32, o32, rn, ALU.mult)
                                yield
                                k.stt("dve", ycurH[:, oc], o32, nw[:, 1:2], gbT[:, oc], ALU.mult, ALU.mult)
                                yield

                    for h in range(NH if stage >= 3 else 1):
                        wfF, wfB, wqb, wib, wgb = (load_w(O_F + h * 128), load_w(O_F + 1024 + h * 128),
                                                   load_w(O_QB + h * 128), load_w(O_IB + h * 128),
                                                   load_w(O_GB + h * 128))
                        for t in range(NT):
                            hv = hxs(t * 128, 128)
                            pv = PSM.get()
                            for kt in range(8):
                                k.mm(pv, hv(kt), wib[:, kt, :], start=(kt == 0), stop=(kt == 7))
                            k.cp("act", vtokH[:, t, :], pv)
                            for c in range(4):
                                k.ts("dve", vmall[:, t, c, :], vtokH[:, t, :], PM[:, c:c + 1], ALU.mult)
                        for g in range(4):
                            project_fm(wqb, TOK0 + 512 * g, 512,
                                       lambda p: k.act("act", qbT[:, 512 * g:512 * (g + 1)], p, AF.Silu))
                            project_fm(wgb, TOK0 + 512 * g, 512,
                                       lambda p: k.act("act", gbT[:, 512 * g:512 * (g + 1)], p, AF.Silu))
                        k.ts("dve", qbT, qbT, 128.0 ** -0.5, ALU.mult)
                        ycurH = yT.get()
                        allseq = []
                        for d in (1, 0):
                            k.memset("dve", S32[d], 0.0)
                            k.memset("dve", Sbf[d].get(), 0.0)
                            seq = []
                            if d == 0:
                                seq += [(t, False, -1) for t in range(NCTX)]
                                seq += [(NCTX + t, True, t) for t in range(NOWN)]
                            else:
                                seq += [(t, False, -1) for t in reversed(range(NCTX))]
                                seq += [(NCTX + t, False, -1) for t in reversed(range(NOWN, NLOC))]
                                seq += [(NCTX + t, True, t) for t in reversed(range(NOWN))]
                            allseq += [(h, d, t, full, oi, wfF if d == 0 else wfB) for (t, full, oi) in seq]
                        run_units(hgrn_unit, allseq, 3)
                        if h == 0:
                            dbg("yb0", ycurH, [128, NOWN * 128])
                            dbg("HS", S32[0], [128, 128])
                        k.dma("sp", V(ydr[1, h], ydr_res[1][h]), ycurH)
            cut(8)

            yres = [R() for _ in range(NOWN)]
            h2 = [V(hxT_t[:, :, TOK0 + (NOWN + t) * 128:TOK0 + (NOWN + t + 1) * 128], hx[NCTX + NOWN + t].res)
                  for t in range(NOWN)]

            def load_cols(dst, src_d, ncol_tiles):
                sv = src_d.rearrange("(kt p) n -> p kt n", p=128)
                for ct in range(ncol_tiles):
                    st_ = wst.get()
                    k.dma("sp", st_, sv[:, :, ct * 128:(ct + 1) * 128])
                    k.cp("dve", dst[:, :, ct * 128:(ct + 1) * 128], st_)

            with Scope():
                Wa = k.sbv([128, 8, D], BF16, "Wa")
                Wb = k.sbv([128, 8, D], BF16, "Wb")
                Wo = k.sbv([128, 8, D], BF16, "Wo")
                load_cols(Wa, wa_d, 8)
                load_cols(Wb, wb_d, 8)
                load_cols(Wo, wo_d, 8)
                yA = k.sbv([128, NH, 512], BF16, "yA")
                yB = k.sbv([128, NH, 512], BF16, "yB")
                mgd = k.sbv([128, 8, 512], BF16, "mgd")
                xr = k.ring(1, [128, D], F32)
                o2r = k.ring(1, [128, D], F32)
                sqr = k.ring(1, [128, D], F32)
                str_ = k.ring(4, [128, 2], F32)
                sg = k.ring(1, [128, 512], F32)
                m1r = k.ring(2, [128, 512], F32)
                for G in range(4):
                    gc = slice(512 * G, 512 * (G + 1))
                    for h in range(NH):
                        k.dma("sp", yA[:, h, :], V(ydr[0, h][:, gc], ydr_res[0][h]))
                        k.dma("sp", yB[:, h, :], V(ydr[1, h][:, gc], ydr_res[1][h]))
                    hvG = hxs(TOK0 + 512 * G, 512)
                    for dt in range(8):
                        wga_ = load_w(O_MG + dt * 128)
                        wgb_ = load_w(O_MG + 1024 + dt * 128)
                        dc = slice(dt * 128, (dt + 1) * 128)
                        pga = PBIG.get()
                        for kt in range(8):
                            k.mm(pga, wga_[:, kt, :], hvG(kt), start=(kt == 0), stop=(kt == 7))
                        sga = sg.get()
                        k.act("act", sga, pga, AF.Sigmoid)
                        pa_ = PBIG.get()
                        for h in range(NH):
                            k.mm(pa_, Wa[:, h, dc], yA[:, h, :], start=(h == 0), stop=(h == NH - 1))
                        m1 = m1r.get()
                        k.tt("dve", m1, pa_, sga, ALU.mult)
                        pgb = PBIG.get()
                        for kt in range(8):
                            k.mm(pgb, wgb_[:, kt, :], hvG(kt), start=(kt == 0), stop=(kt == 7))
                        sgb = sg.get()
                        k.act("act", sgb, pgb, AF.Sigmoid)
                        pb_ = PBIG.get()
                        for h in range(NH):
                            k.mm(pb_, Wb[:, h, dc], yB[:, h, :], start=(h == 0), stop=(h == NH - 1))
                        m2 = m1r.get()
                        k.tt("dve", m2, pb_, sgb, ALU.mult)
                        k.tt("dve", mgd[:, dt, :], m1, m2, ALU.add)
                    if G == 0:
                        dbg("yA", yA, [128, NH, 512])
                        dbg("yB", yB, [128, NH, 512])
                        dbg("mgd", mgd, [128, 8, 512])
                    for j in range(4):
                        t = 4 * G + j
                        o2 = o2r.get()
                        for half in range(2):
                            po_ = PBIG.get()
                            for dt in range(8):
                                k.mm(po_, mgd[:, dt, j * 128:(j + 1) * 128], Wo[:, dt, half * 512:(half + 1) * 512],
                                     start=(dt == 0), stop=(dt == 7))
                            k.cp("act", o2[:, half * 512:(half + 1) * 512], po_)
                        sq = sqr.get()
                        st = str_.get()
                        if t == 0:
                            dbg("o2raw", o2, [128, D])
                        k.tt("dve", sq, o2, o2, ALU.mult)
                        k.red("dve", st[:, 0:1], sq)
                        rsqrt(st[:, 1:2], st[:, 0:1], 1.0 / D)
                        if t == 0:
                            dbg("o2st", st, [128, 2])
                        k.act("act", o2, o2, AF.Copy, scale=st[:, 1:2])
                        k.tt("dve", o2, o2, grow[:, 0, :], ALU.mult)
                        if t == 0:
                            dbg("o2sc", o2, [128, D])
                        xt = xr.get()
                        k.dma("sp", xt, x_d[t * 128:(t + 1) * 128, :])
                        k.tt("dve", xt, xt, o2, ALU.add)
                        k.dma("sp", V(y_d[t * 128:(t + 1) * 128, :], yres[t]), xt)
                        if t == 0:
                            dbg("x1t0", xt, [128, D])
                        norm_T(xt, sqr.get(), str_.get(), h2[t], A2, B2, 0, t)
            cut(9)

            w1_v = w1_d.rearrange("(kt p) n -> p kt n", p=128)
            w2_v = w2_d.rearrange("(ft p) n -> p ft n", p=128)
            with Scope():
                hid = k.sbv([128, 32, 512], BF16, "hid")
                w2r = k.ring(3, [128, 512], BF16)
                w2s = k.ring(3, [128, 512], F32)
                rl = k.ring(2, [128, 512], F32)
                xr = k.ring(1, [128, D], F32)
                o3r = k.ring(4, [128, D], F32)
                sqr = k.ring(1, [128, D], F32)
                str_ = k.ring(4, [128, 2], F32)
                PACC = [V(b.ap, b.res, True) for b in _psm]
                for G in range(4):
                    hvG = lambda kt: V(hxT_t[:, kt, TOK0 + (NOWN + 4 * G) * 128:TOK0 + (NOWN + 4 * G + 4) * 128],
                                       tuple(h2[4 * G + j].res for j in range(4)))
                    for ft in range(32):
                        st_ = wst.get()
                        k.dma("sp", st_, w1_v[:, :, ft * 128:(ft + 1) * 128])
                        w1 = wring.get()
                        k.cp("dve", w1, st_)
                        ph = PBIG.get()
                        for kt in range(8):
                            k.mm(ph, w1[:, kt, :], hvG(kt), start=(kt == 0), stop=(kt == 7))
                        r_ = rl.get()
                        k.act("act", r_, ph, AF.Relu)
                        k.tt("dve", hid[:, ft, :], r_, r_, ALU.mult)
                    o3s = [o3r.get() for _ in range(4)]
                    for half in range(2):
                        for ft in range(32):
                            s2 = w2s.get()
                            k.dma("sp", s2, w2_v[:, ft, half * 512:(half + 1) * 512])
                            w2 = w2r.get()
                            k.cp("dve", w2, s2)
                            for j in range(4):
                                k.mm(PACC[j], hid[:, ft, j * 128:(j + 1) * 128], w2, start=(ft == 0), stop=(ft == 31))
                        for j in range(4):
                            k.cp("act", o3s[j][:, half * 512:(half + 1) * 512], PACC[j])
                    for j in range(4):
                        t = 4 * G + j
                        o3 = o3s[j]
                        sq = sqr.get()
                        st = str_.get()
                        k.tt("dve", sq, o3, o3, ALU.mult)
                        k.red("dve", st[:, 0:1], sq)
                        rsqrt(st[:, 1:2], st[:, 0:1], 1.0 / D)
                        k.act("act", o3, o3, AF.Copy, scale=st[:, 1:2])
                        k.tt("dve", o3, o3, grow[:, 1, :], ALU.mult)
                        xt = xr.get()
                        yv = V(y_d[t * 128:(t + 1) * 128, :], yres[t])
                        k.dma("sp", xt, yv)
                        k.tt("dve", xt, xt, o3, ALU.add)
                        k.dma("sp", yv, xt, is_out=True)
            k.finish()
            raise Done()


        try:
            _phase2plus()
            raise Cut()
        except Done:
            return nc, dbg_d
        except Cut:
            z = V(hxT_t[:, 0, 0:2048].bitcast(F32), tuple(h_.res for h_ in hx))
            k.memset("dve", z, 0.0)
            for t in range(NOWN):
                k.dma("sp", y_d[t * 128:(t + 1) * 128, :], z, is_out=True)
        k.finish()
    return nc, dbg_d


def host_prep(inputs):
    f = np.float32
    x = np.asarray(inputs["x"], f)
    c = np.asarray(inputs["c"], f)
    ctx = np.asarray(inputs["ctx"], f)
    c_ctx = np.asarray(inputs["c_ctx"], f)
    w_ada = np.ascontiguousarray(np.asarray(inputs["w_ada"], f)[0])
    b_ada = np.asarray(inputs["b_ada"], f)[0]
    norm_g = np.asarray(inputs["norm_g"], f)[0]
    w_in = np.asarray(inputs["w_in"], f)[0]
    conv_w = np.asarray(inputs["conv_w"], f)[0]
    a_log = np.asarray(inputs["gdn_a_log"], f)[0]
    dt_b = np.asarray(inputs["gdn_dt_bias"], f)[0]
    gnw = np.asarray(inputs["gdn_norm_w"], f)[0]
    lbl = np.asarray(inputs["hgrn_lb_logits"], f)
    hnw = np.asarray(inputs["hgrn_norm_w"], f)[0]

    i = np.arange(128)
    consts = np.zeros((NCONST, 128, 128), f)
    consts[C_I] = np.eye(128)
    consts[C_ONE] = 1.0
    consts[C_UF] = (i[:, None] <= i[None, :])
    consts[C_UB] = (i[:, None] >= i[None, :])
    consts[C_NSF] = np.where(i[None, :] > i[:, None], 0.0, NEG)
    consts[C_NSB] = np.where(i[None, :] < i[:, None], 0.0, NEG)
    consts[C_NIF] = np.where(i[None, :] >= i[:, None], 0.0, NEG)
    consts[C_NIB] = np.where(i[None, :] <= i[:, None], 0.0, NEG)
    same = (i[:, None] // 32) == (i[None, :] // 32)
    consts[C_HMF] = same & (i[:, None] <= i[None, :])
    consts[C_HMB] = same & (i[:, None] >= i[None, :])
    consts[C_RST] = np.broadcast_to((i % 32 != 0).astype(f)[None, :], (128, 128))
    consts[C_SUB][:, 0:4] = ((i[:, None] // 32) == np.arange(4)[None, :])
    consts = np.ascontiguousarray(consts.transpose(1, 0, 2).reshape(128, NCONST * 128))

    bada_fm = np.ascontiguousarray(b_ada.reshape(48, 128).T)
    bgt = np.ascontiguousarray(np.broadcast_to(
        np.stack([b_ada[2 * D:3 * D], b_ada[5 * D:6 * D]])[None], (128, 2, D)))
    g02 = np.ascontiguousarray(np.stack([norm_g[0].reshape(8, 128).T, norm_g[2].reshape(8, 128).T], axis=1))
    g13 = np.ascontiguousarray(np.broadcast_to(np.stack([norm_g[1], norm_g[3]])[None], (128, 2, D)))
    nwv = np.ascontiguousarray(np.stack([gnw, hnw], axis=1))

    def perm_cols(flip):
        cols = []
        cols.append(np.arange(0, 4096))
        for base in (4096, 4112):
            idx = base + np.arange(16).reshape(2, 8)
            cols.append((idx[::-1] if flip else idx).reshape(-1))
        idx = 4128 + np.arange(2048).reshape(2, 1024)
        cols.append((idx[::-1] if flip else idx).reshape(-1))
        cols.append(np.arange(6176, IN_W))
        return np.concatenate(cols)

    maps = []
    shared = dict(w_ada=w_ada, bada_fm=bada_fm, bgt=bgt, g02=g02, g13=g13, nw=nwv,
                  w_a=np.ascontiguousarray(np.asarray(inputs["w_branch_a"], f)[0]),
                  w_b=np.ascontiguousarray(np.asarray(inputs["w_branch_b"], f)[0]),
                  w_o=np.ascontiguousarray(np.asarray(inputs["w_out"], f)[0]),
                  w_1=np.ascontiguousarray(np.asarray(inputs["w_mlp_in"], f)[0]),
                  w_2=np.ascontiguousarray(np.asarray(inputs["w_mlp_out"], f)[0]),
                  consts=consts)
    per_flip = {}
    for flip in (0, 1):
        win_l = np.ascontiguousarray(w_in[:, perm_cols(flip)])
        cw = conv_w[::-1, ::-1, :] if flip else conv_w
        convw = np.ascontiguousarray(cw.reshape(9, 24, 128).transpose(2, 1, 0))
        al = a_log[::-1] if flip else a_log
        db = dt_b[::-1] if flip else dt_b
        abv = np.ascontiguousarray(np.broadcast_to(np.stack([al.reshape(-1), db.reshape(-1)])[None], (128, 2, 16)))
        ll = lbl[:, ::-1, :] if flip else lbl
        lblv = np.ascontiguousarray(ll.reshape(2, 2, 8, 128).transpose(3, 0, 1, 2).reshape(128, 2, 16))
        per_flip[flip] = dict(w_in=win_l, convw=convw, ab=abv, lbl=lblv)
    for core in range(8):
        b, s = core // 2, core % 2
        xb = x[b][::-1] if s else x[b]
        cb = ctx[b][::-1] if s else ctx[b]
        cfm = np.ascontiguousarray(np.stack([c[b], c_ctx], axis=1).reshape(8, 128, 2).transpose(1, 0, 2))
        m = dict(shared)
        m.update(per_flip[s])
        m.update(x=np.ascontiguousarray(xb), ctx=np.ascontiguousarray(cb), cfm=cfm)
        maps.append(m)
    return maps


_NC_CACHE = {}


def kernel(**inputs):
    maps = host_prep(inputs)
    if "nc" not in _NC_CACHE:
        import os
        _NC_CACHE["nc"] = build(cutn=int(os.environ.get("KCUT", "0")))[0]
    nc = _NC_CACHE["nc"]
    res = run_bass_kernel_spmd(nc, maps, core_ids=list(range(8)))
    out = np.zeros((4, 4096, D), np.float32)
    for core in range(8):
        b, s = core // 2, core % 2
        y = np.asarray(res.results[core]["y"])
        if s:
            out[b, 2048:] = y[::-1]
        else:
            out[b, :2048] = y
    return out
```

```python
import numpy as np
from contextlib import ExitStack
import concourse.bass as bass
import concourse.mybir as mybir
from concourse.bass_utils import run_bass_kernel_spmd

F32 = mybir.dt.float32
BF16 = mybir.dt.bfloat16
AF = mybir.ActivationFunctionType
ALU = mybir.AluOpType
AX = mybir.AxisListType

D = 1024
NH = 8
EPS = 1e-6
NOWN = 16
NOTH = 16
NLOC = NOWN + NOTH
NCTX = 2
NT = NCTX + NLOC
TOK0 = NCTX * 128
NTOK = NT * 128
NEG = -30000.0
IN_W = 11296
O_Q, O_K, O_V, O_GA, O_A, O_B, O_F, O_QB, O_IB, O_GB, O_MG = (
    0, 1024, 2048, 3072, 4096, 4112, 4128, 6176, 7200, 8224, 9248)

C_I, C_ONE, C_UF, C_UB, C_NSF, C_NSB, C_NIF, C_NIB, C_HMF, C_HMB, C_RST, C_SUB = range(12)
NCONST = 12


class R:
    __slots__ = ("w", "r")

    def __init__(self):
        self.w = None
        self.r = {}


class V:
    def __init__(self, ap, res, psum=False):
        self.ap = ap
        self.res = res
        self.psum = psum

    def __getitem__(self, k):
        return V(self.ap[k], self.res, self.psum)

    def bc(self, shape):
        return V(self.ap.to_broadcast(list(shape)), self.res, self.psum)

    def re(self, s, **kw):
        return V(self.ap.rearrange(s, **kw), self.res, self.psum)


def _flat(vs):
    out = []
    for v in vs:
        if isinstance(v, V):
            if isinstance(v.res, tuple):
                out.extend(v.res)
            else:
                out.append(v.res)
    return out


class KB:
    def __init__(self, nc, es):
        self.nc = nc
        self.es = es
        self.eng = {"pe": nc.tensor, "act": nc.scalar, "dve": nc.vector,
                    "pool": nc.gpsimd, "sp": nc.sync}
        self.sem = {n: es.enter_context(nc.semaphore("s_" + n)) for n in self.eng}
        self.cnt = {n: 0 for n in self.eng}
        self.known = {n: {} for n in self.eng}
        self.nslots = {"sp": 24, "pool": 12, "act": 4}
        self.dsem = {}
        self.dval = {}
        self.dnext = {q: 0 for q in self.nslots}
        for q, n in self.nslots.items():
            for i in range(n):
                key = ("d", q, i)
                self.dsem[key] = es.enter_context(nc.semaphore("d_%s_%d" % (q, i)))
                self.dval[key] = 0
        self.out_toks = []
        self.uid = 0
        self.ninst = 0

    def sb(self, shape, dt=F32, name=None):
        self.uid += 1
        t = self.es.enter_context(self.nc.sbuf_tensor("sb_" + (name or ("t%d" % self.uid)), list(shape), dt))
        return t

    def sbv(self, shape, dt=F32, name=None):
        t = self.sb(shape, dt, name)
        return V(t[tuple(slice(None) for _ in shape)], R())

    def ps(self, shape, dt=F32, name=None):
        self.uid += 1
        t = self.es.enter_context(self.nc.psum_tensor("ps_" + (name or ("p%d" % self.uid)), list(shape), dt))
        return V(t[tuple(slice(None) for _ in shape)], R(), True)

    def ring(self, n, shape, dt=F32, psum=False):
        items = [(self.ps(shape, dt) if psum else self.sbv(shape, dt)) for _ in range(n)]
        return Ring(items)

    def _semof(self, key):
        return self.sem[key] if isinstance(key, str) else self.dsem[key]

    def _wait(self, e, toks):
        kn = self.known[e]
        for tok in toks:
            if tok is None:
                continue
            key, val, snap = tok
            if kn.get(key, 0) >= val:
                continue
            self.eng[e].wait_ge(self._semof(key), val)
            self.ninst += 1
            kn[key] = val
            for k2, v2 in snap.items():
                if kn.get(k2, 0) < v2:
                    kn[k2] = v2

    @staticmethod
    def _toks(rd, wr):
        toks = []
        for r in rd:
            toks.append(r.w)
        for r in wr:
            toks.append(r.w)
            toks.extend(r.r.values())
        return toks

    @staticmethod
    def _commit(tok, rd, wr):
        for r in rd:
            r.r[tok[0]] = tok
        for r in wr:
            r.w = tok
            r.r = {}

    def op(self, e, fn, rd, wr):
        wr = list(wr) + [v for v in rd if isinstance(v, V) and v.psum]
        rd = _flat(rd)
        wr = _flat(wr)
        self._wait(e, self._toks(rd, wr))
        inst = fn(self.eng[e])
        self.cnt[e] += 1
        self.ninst += 1
        inst.then_inc(self.sem[e], 1)
        tok = (e, self.cnt[e], dict(self.known[e]))
        self._commit(tok, rd, wr)

    def dma(self, q, out, in_, is_out=False):
        rd = _flat([in_])
        wr = _flat([out])
        i = self.dnext[q]
        self.dnext[q] = (i + 1) % self.nslots[q]
        key = ("d", q, i)
        prev = self.dval[key]
        toks = self._toks(rd, wr)
        if prev > 0:
            toks.append((key, prev, {}))
        self._wait(q, toks)
        oap = out.ap if isinstance(out, V) else out
        iap = in_.ap if isinstance(in_, V) else in_
        inst = self.eng[q].dma_start(out=oap, in_=iap)
        self.ninst += 1
        val = prev + 16
        inst.then_inc(self.dsem[key], 16)
        self.dval[key] = val
        tok = (key, val, dict(self.known[q]))
        self._commit(tok, rd, wr)
        if is_out:
            self.out_toks.append(tok)

    def barrier(self):
        toks = [(e, c, {}) for e, c in self.cnt.items() if c > 0]
        toks += [(k_, v, {}) for k_, v in self.dval.items() if v > 0]
        for e in self.eng:
            self._wait(e, toks)

    def finish(self):
        self._wait("sp", self.out_toks)
        toks = [(k, v, {}) for k, v in self.dval.items() if v > 0]
        self._wait("sp", toks)

    def mm(self, out, lhsT, rhs, start=True, stop=True):
        self.op("pe", lambda e: e.matmul(out.ap, lhsT=lhsT.ap, rhs=rhs.ap, start=start, stop=stop),
                [lhsT, rhs] + ([] if start else [out]), [out])

    def tr(self, out, in_, ident):
        self.op("pe", lambda e: e.transpose(out.ap, in_.ap, ident.ap), [in_, ident], [out])

    def act(self, e, out, in_, func, bias=None, scale=1.0):
        kw = {}
        if bias is not None:
            kw["bias"] = bias.ap if isinstance(bias, V) else bias
        kw["scale"] = scale.ap if isinstance(scale, V) else scale
        assert e == "act"
        self.op("act", lambda en: en.activation(out=out.ap, in_=in_.ap, func=func, **kw),
                [in_, bias, scale], [out])

    def tt(self, e, out, a, b, op):
        self.op(e, lambda en: en.tensor_tensor(out=out.ap, in0=a.ap, in1=b.ap, op=op), [a, b], [out])

    def ts(self, e, out, a, s1, op0, s2=None, op1=None):
        s1a = s1.ap if isinstance(s1, V) else s1
        s2a = s2.ap if isinstance(s2, V) else s2
        if op1 is None:
            fn = lambda en: en.tensor_scalar(out=out.ap, in0=a.ap, scalar1=s1a, scalar2=None, op0=op0)
        else:
            fn = lambda en: en.tensor_scalar(out=out.ap, in0=a.ap, scalar1=s1a, scalar2=s2a, op0=op0, op1=op1)
        self.op(e, fn, [a, s1, s2], [out])

    def stt(self, e, out, a, s, b, op0, op1):
        sa = s.ap if isinstance(s, V) else s
        self.op(e, lambda en: en.scalar_tensor_tensor(out=out.ap, in0=a.ap, scalar=sa, in1=b.ap, op0=op0, op1=op1),
                [a, s, b], [out])

    def cp(self, e, out, in_):
        if e == "act":
            self.op("act", lambda en: en.copy(out=out.ap, in_=in_.ap), [in_], [out])
        else:
            self.op(e, lambda en: en.tensor_copy(out=out.ap, in_=in_.ap), [in_], [out])

    def memset(self, e, out, val):
        self.op(e, lambda en: en.memset(out.ap, val), [], [out])

    def red(self, e, out, in_, op=ALU.add):
        self.op(e, lambda en: en.tensor_reduce(out=out.ap, in_=in_.ap, axis=AX.X, op=op), [in_], [out])

    def scan(self, out, d0, d1, init, op0, op1):
        self.op("dve", lambda en: en.tensor_tensor_scan(out=out.ap, data0=d0.ap, data1=d1.ap, initial=init,
                                                        op0=op0, op1=op1), [d0, d1], [out])


class Ring:
    def __init__(self, items):
        self.items = items
        self.i = 0

    def get(self):
        it = self.items[self.i % len(self.items)]
        self.i += 1
        return it


class Cut(Exception):
    pass


class Done(Exception):
    pass


def build(stage=99, dbg_names=(), cutn=0):
    nc = bass.Bass("TRN2", target_bir_lowering=False)

    def cut(n):
        if cutn == n:
            raise Cut()

    def din(name, shape, dt=F32):
        return nc.dram_tensor(name, list(shape), dt, kind="ExternalInput").ap()

    x_d = din("x", [NLOC * 128, D])
    ctx_d = din("ctx", [NCTX * 128, D])
    cfm_d = din("cfm", [128, 8, 2])
    wada_d = din("w_ada", [D, 6 * D])
    badafm_d = din("bada_fm", [128, 48])
    bgt_d = din("bgt", [128, 2, D])
    g02_d = din("g02", [128, 2, 8])
    g13_d = din("g13", [128, 2, D])
    win_d = din("w_in", [D, IN_W])
    convw_d = din("convw", [128, 24, 9])
    ab_d = din("ab", [128, 2, 16])
    lbl_d = din("lbl", [128, 2, 16])
    nw_d = din("nw", [128, 2])
    wa_d = din("w_a", [D, D])
    wb_d = din("w_b", [D, D])
    wo_d = din("w_o", [D, D])
    w1_d = din("w_1", [D, 4 * D])
    w2_d = din("w_2", [4 * D, D])
    cst_d = din("consts", [128, NCONST * 128])
    y_d = nc.dram_tensor("y", [NOWN * 128, D], F32, kind="ExternalOutput").ap()
    dbg_d = {}

    es = ExitStack()
    with es:
        k = KB(nc, es)

        def dbg(name, v, shape):
            if name in dbg_names:
                d = nc.dram_tensor("dbg_" + name, list(shape), v.ap.dtype, kind="ExternalOutput").ap()
                dbg_d[name] = d
                k.dma("sp", d, v, is_out=True)

        PBIG = k.ring(3, [128, 512], F32, psum=True)
        _psm = [k.ps([128, 512], F32) for _ in range(4)]
        PSM = Ring([V(b.ap[:, j * 128:(j + 1) * 128], b.res, True) for j in range(4) for b in _psm])
        _pbf = k.ps([128, 512], BF16)
        PBF = Ring([V(_pbf.ap[:, j * 128:(j + 1) * 128], _pbf.res, True) for j in range(4)])

        class Scope:
            def __enter__(s_):
                s_.es = ExitStack()
                s_.es.__enter__()
                s_.save = k.es
                k.es = s_.es
                return s_

            def __exit__(s_, *a):
                k.barrier()
                k.es = s_.save
                s_.es.__exit__(None, None, None)
                return False

        cst = k.sbv([128, NCONST * 128], F32, "cst")
        k.dma("sp", cst, cst_d)

        def C(i):
            return cst[:, i * 128:(i + 1) * 128]

        cbf = k.sbv([128, 2 * 128], BF16, "cbf")
        k.cp("dve", cbf, cst[:, 0:256])
        I_bf = cbf[:, 0:128]
        ONE_bf = cbf[:, 128:256]
        I_f = C(C_I)
        ONE_f = C(C_ONE)

        cfm = k.sbv([128, 8, 2], F32, "cfm")
        k.dma("sp", cfm, cfm_d)
        csl = k.sbv([128, 8, 2], F32, "csl")
        k.act("act", csl, cfm, AF.Silu)
        badafm = k.sbv([128, 48], F32, "badafm")
        k.dma("sp", badafm, badafm_d)
        g02 = k.sbv([128, 2, 8], F32, "g02")
        k.dma("sp", g02, g02_d)
        ab = k.sbv([128, 2, 16], F32, "ab")
        k.dma("sp", ab, ab_d)
        lbl = k.sbv([128, 2, 16], F32, "lbl")
        k.dma("sp", lbl, lbl_d)
        nw = k.sbv([128, 2], F32, "nw")
        k.dma("sp", nw, nw_d)
        convw = k.sbv([128, 24, 9], F32, "convw")
        k.dma("sp", convw, convw_d)

        nA = k.sbv([128, 16], F32, "nA")
        k.act("act", nA, ab[:, 0, :], AF.Exp)
        k.ts("dve", nA, nA, -1.0, ALU.mult)
        dtb = ab[:, 1, :]
        lbd = k.sbv([128, 16], F32, "lbd")
        k.tt("dve", lbd, lbl[:, 0, :], lbl[:, 1, :], ALU.subtract)
        lb = k.sbv([128, 16], F32, "lb")
        oml = k.sbv([128, 16], F32, "oml")
        k.act("act", lb, lbd, AF.Sigmoid)
        k.act("act", oml, lbd, AF.Sigmoid, scale=-1.0)
        noml = k.sbv([128, 16], F32, "noml")
        k.ts("dve", noml, oml, -1.0, ALU.mult)

        modfm = k.sbv([128, 48, 2], F32, "modfm")
        grow = k.sbv([128, 2, D], F32, "grow")
        wada_v = wada_d.rearrange("(kt p) n -> p kt n", p=128)
        pp_mod = PBIG
        with ExitStack() as es2:
            es_save = k.es
            k.es = es2
            wring = k.ring(2, [128, 8, 512], F32)
            cbc = k.sbv([128, 8, 128], F32, "cbc")
            k.cp("dve", cbc, csl[:, :, 0:1].bc([128, 8, 128]))
            bgt = k.sbv([128, 2, D], F32, "bgt")
            k.dma("sp", bgt, bgt_d)
            g13 = k.sbv([128, 2, D], F32, "g13")
            k.dma("sp", g13, g13_d)
            for ch in (2, 3, 0, 1, 6, 7, 8, 9, 4, 5, 10, 11):
                wt = wring.get()
                k.dma("sp", wt[:, 0:4, :], wada_v[:, 0:4, ch * 512:(ch + 1) * 512])
                k.dma("sp", wt[:, 4:8, :], wada_v[:, 4:8, ch * 512:(ch + 1) * 512])
                pm = pp_mod.get()
                for j in range(4):
                    ct = ch * 4 + j
                    for kt in range(8):
                        k.mm(pm[:, 2 * j:2 * j + 2], wt[:, kt, j * 128:(j + 1) * 128], csl[:, kt, :],
                             start=(kt == 0), stop=(kt == 7))
                for j in range(4):
                    ct = ch * 4 + j
                    k.ts("dve", modfm[:, ct, :], pm[:, 2 * j:2 * j + 2], badafm[:, ct:ct + 1], ALU.add)
                if ch in (4, 5, 10, 11):
                    gi = 0 if ch < 6 else 1
                    c0 = (ch % 2) * 512
                    pg = pp_mod.get()
                    for kt in range(8):
                        k.mm(pg, cbc[:, kt, :], wt[:, kt, :], start=(kt == 0), stop=(kt == 7))
                    k.tt("dve", grow[:, gi, c0:c0 + 512], pg, bgt[:, gi, c0:c0 + 512], ALU.add)
            k.tt("dve", grow[:, 0, :], grow[:, 0, :], g13[:, 0, :], ALU.mult)
            k.tt("dve", grow[:, 1, :], grow[:, 1, :], g13[:, 1, :], ALU.mult)
            k.barrier()
            k.es = es_save
        A0 = k.sbv([128, 8, 2], F32, "A0")
        A2 = k.sbv([128, 8, 2], F32, "A2")
        for c_ in range(2):
            k.stt("dve", A0[:, :, c_], modfm[:, 8:16, c_], 1.0, g02[:, 0, :], ALU.add, ALU.mult)
            k.stt("dve", A2[:, :, c_], modfm[:, 32:40, c_], 1.0, g02[:, 1, :], ALU.add, ALU.mult)
        B0 = modfm[:, 0:8, :]
        B2 = modfm[:, 24:32, :]
        dbg("modfm", modfm, [128, 48, 2])
        dbg("grow", grow, [128, 2, D])

        hxT_t = k.sb([128, 8, NTOK], BF16, "hxT")
        hx = [V(hxT_t[:, :, t * 128:(t + 1) * 128], R()) for t in range(NT)]
        epsc = k.sbv([128, 1], F32, "epsc")
        k.memset("dve", epsc, EPS)
        onec = k.sbv([128, 1], F32, "onec")
        k.memset("dve", onec, 1.0)

        def hxs(c0, n):
            ts_ = tuple(hx[t].res for t in range(c0 // 128, (c0 + n + 127) // 128))
            return lambda kt: V(hxT_t[:, kt, c0:c0 + n], ts_)

        def rsqrt(out, in_, scale):
            k.act("act", out, in_, AF.Sqrt, bias=epsc[0:out.ap.shape[0], :], scale=scale)
            k.op("dve", lambda en: en.reciprocal(out=out.ap, in_=out.ap), [out], [out])

        def rsqrt_le(out, in_, scale):
            k.act("act", out, in_, AF.Ln, bias=epsc[0:out.ap.shape[0], :], scale=scale)
            k.act("act", out, out, AF.Exp, scale=-0.5)

        def norm_T(xt, sq, st, dst, A, Bm, col, eng_flip):
            k.tt("dve", sq, xt, xt, ALU.mult)
            k.red("dve", st[:, 0:1], sq)
            rsqrt(st[:, 1:2], st[:, 0:1], 1.0 / D)
            k.act("act", sq, xt, AF.Copy, scale=st[:, 1:2])
            for half in range(2):
                p = PBIG.get()
                for j in range(4):
                    kt = half * 4 + j
                    k.tr(p[:, j * 128:(j + 1) * 128], sq[:, kt * 128:(kt + 1) * 128], I_f)
                for j in range(4):
                    kt = half * 4 + j
                    if (kt + eng_flip) % 2 == 0:
                        k.act("act", dst[:, kt, :], p[:, j * 128:(j + 1) * 128], AF.Identity,
                              bias=Bm[:, kt, col:col + 1], scale=A[:, kt, col:col + 1])
                    else:
                        k.ts("dve", dst[:, kt, :], p[:, j * 128:(j + 1) * 128], A[:, kt, col:col + 1], ALU.mult,
                             Bm[:, kt, col:col + 1], ALU.add)

        with Scope():
            xring = k.ring(3, [128, D], F32)
            sqring = k.ring(2, [128, D], F32)
            stat = k.ring(4, [128, 2], F32)
            for t in range(NT):
                xt = xring.get()
                src = ctx_d[t * 128:(t + 1) * 128, :] if t < NCTX else x_d[(t - NCTX) * 128:(t - NCTX + 1) * 128, :]
                k.dma("sp", xt, src)
                norm_T(xt, sqring.get(), stat.get(), hx[t], A0, B0, 1 if t < NCTX else 0, t)
            dbg("hxT", V(hxT_t[:, :, 0:512], hx[0].res), [128, 8, 512])

        win_v = win_d.rearrange("(kt p) n -> p kt n", p=128)
        def _phase2plus():
            ydr = nc.dram_tensor("ydr", [2, NH, 128, NOWN * 128], BF16, kind="Internal").ap()
            ydr_res = [[R() for _ in range(NH)] for _ in range(2)]

            wring = k.ring(5, [128, 8, 128], BF16)

            wst = k.ring(2, [128, 8, 128], F32)

            def load_w(col0):
                st_ = wst.get()
                k.dma("sp", st_, win_v[:, :, col0:col0 + 128])
                w = wring.get()
                k.cp("dve", w, st_)
                return w

            def project_fm(w, c0, n, dst_fn):
                hv = hxs(c0, n)
                p = PBIG.get()
                for kt in range(8):
                    k.mm(p[:, 0:n], w[:, kt, :], hv(kt), start=(kt == 0), stop=(kt == 7))
                dst_fn(p[:, 0:n])

            with Scope():
                names = ("la", "lnb", "negg", "bsc", "bg", "ed", "dl")
                sm = {n: k.sbv([128, NT, 16], F32, "sm_" + n) for n in names}
                sm["la"].res = tuple(R() for _ in range(NT))
                smres = [R() for _ in range(NT)]
                with Scope():
                    wab = k.sbv([128, 8, 32], BF16, "wab")
                    wab32 = k.sbv([128, 8, 32], F32, "wab32")
                    k.dma("sp", wab32, win_v[:, :, O_A:O_A + 32])
                    k.cp("dve", wab, wab32)
                    tmp = k.ring(6, [128, 32], F32)
                    for t in range(NT):
                        hv = hxs(t * 128, 128)
                        pab = PSM.get()
                        for kt in range(8):
                            k.mm(pab[:, 0:32], hv(kt), wab[:, kt, :], start=(kt == 0), stop=(kt == 7))
                        xx = tmp.get()
                        k.tt("dve", xx[:, 0:16], pab[:, 0:16], dtb, ALU.add)
                        k.ts("dve", xx[:, 16:32], pab[:, 16:32], -1.0, ALU.mult)
                        if t == 0:
                            dbg("d_wab", wab, [128, 8, 32])
                            dbg("d_wab32", wab32, [128, 8, 32])
                            dbg("d_hx0", hx[0], [128, 8, 128])
                            dbg("d_xx", xx, [128, 32])
                            dbg("d_nA", nA, [128, 16])
                            dbg("d_ab", ab, [128, 2, 16])
                        aa = tmp.get()
                        k.ts("dve", aa, xx, -1.0, ALU.mult)
                        k.tt("dve", aa, aa, xx, ALU.max)
                        if t == 0:
                            dbg("d_abs", aa, [128, 32])
                        k.act("act", aa, aa, AF.Exp, scale=-1.0)
                        if t == 0:
                            dbg("d_exp", aa, [128, 32])
                        k.act("act", aa, aa, AF.Ln, bias=onec)
                        if t == 0:
                            dbg("d_ln", aa, [128, 32])
                        k.ts("dve", xx, xx, 0.0, ALU.max)
                        k.tt("dve", xx, xx, aa, ALU.add)
                        sres = smres[t]

                        def S_(n):
                            return V(sm[n].ap[:, t, :], sres)
                        la_t = V(sm["la"].ap[:, t, :], sm["la"].res[t])
                        k.tt("dve", la_t, xx[:, 0:16], nA, ALU.mult)
                        k.ts("dve", S_("lnb"), xx[:, 16:32], -1.0, ALU.mult)
                        pg = PSM.get()
                        k.mm(pg[:, 0:8], C(C_UF), la_t[:, 0:8])
                        k.mm(pg[:, 8:16], C(C_UB), la_t[:, 8:16])
                        k.mm(pg[:, 16:32], ONE_f, la_t)
                        gg = tmp.get()
                        k.cp("dve", gg, pg[:, 0:32])
                        k.ts("dve", S_("negg"), gg[:, 0:16], -1.0, ALU.mult)
                        k.act("act", S_("bsc"), S_("lnb"), AF.Exp)
                        t2 = tmp.get()
                        k.tt("dve", t2[:, 0:16], S_("lnb"), gg[:, 0:16], ALU.add)
                        k.act("act", S_("bg"), t2[:, 0:16], AF.Exp)
                        k.tt("dve", t2[:, 16:32], gg[:, 16:32], gg[:, 0:16], ALU.subtract)
                        k.act("act", S_("ed"), t2[:, 16:32], AF.Exp)
                        k.act("act", S_("dl"), gg[:, 16:32], AF.Exp)

                if cutn == 1:
                    for n_ in names:
                        dbg("sm_" + n_, V(sm[n_].ap, tuple(smres) + tuple(sm["la"].res)), [128, NT, 16])
                cut(1)

                def SM(n, t, hd):
                    r = sm["la"].res[t] if n == "la" else smres[t]
                    return V(sm[n].ap[:, t, hd:hd + 1], r)

                NSLOT = 4
                t128s = [k.ring(8, [128, 128], F32) for _ in range(NSLOT)]
                b128s = [k.ring(8, [128, 128], BF16) for _ in range(NSLOT)]

                def run_units(unit_fn, arglist, nslots):
                    pending = list(enumerate(arglist))
                    active = []
                    free = list(range(nslots))
                    next_chain = 0
                    while pending or active:
                        while pending and free:
                            idx, a_ = pending.pop(0)
                            sl = free.pop(0)
                            active.append([idx, unit_fn(*a_, slot=sl), sl, "prep"])
                        for item in list(active):
                            idx, g, sl, st = item
                            if st == "wait" and idx != next_chain:
                                continue
                            try:
                                r_ = next(g)
                                if r_ == "chain":
                                    item[3] = "wait"
                            except StopIteration:
                                active.remove(item)
                                free.append(sl)
                                next_chain += 1
                with Scope():
                    pre_loc = k.sbv([128, 66, 66], BF16, "pre_loc")
                    pre_ctx = k.sbv([128, 258], BF16, "pre_ctx")
                    k.memset("pool", pre_loc, 0.0)
                    k.memset("pool", pre_ctx, 0.0)
                    dgr = k.ring(1, [128, 9, 128], BF16)
                    KT = k.sbv([128, NTOK], BF16, "KT")
                    QT = k.sbv([128, NOWN * 128], BF16, "QT")
                    ktok = k.sbv([128, NT, 128], BF16, "ktok")
                    vtok = k.sbv([128, NT, 128], BF16, "vtok")
                    gaT = k.sbv([128, NOWN * 128], BF16, "gaT")
                    oBT = k.sbv([128, NOWN * 128], BF16, "oBT")
                    yT = k.ring(1, [128, NOWN * 128], BF16)
                    f512 = k.ring(2, [128, 512], F32)
                    g512 = k.ring(2, [128, 512], F32)
                    vt512 = k.ring(2, [128, 512], BF16)
                    S32 = [k.sbv([128, 128], F32, "S32_%d" % d) for d in range(2)]
                    Sbf = [k.ring(2, [128, 128], BF16) for d in range(2)]

                    def conv_head(w, ct, which):
                        dg = dgr.get()
                        for tap in range(9):
                            k.ts("dve", dg[:, tap, :], I_bf, convw[:, ct, tap:tap + 1], ALU.mult)
                        cut(21)
                        if which != "q":
                            project_fm(w, 0, 256, lambda p: k.cp("act", pre_ctx[:, 1:257], p))
                        for g in range(8 if which != "q" else 5):
                            project_fm(w, TOK0 + 512 * g, 512,
                                       lambda p: k.cp("act", pre_loc[:, 1 + 8 * g:9 + 8 * g, 1:65],
                                                      p.re("p (a b) -> p a b", b=64)))
                        cut(22)
                        def stage1(g):
                            pc = PBIG.get()
                            if g < 0:
                                for kw in range(3):
                                    k.mm(pc[:, 0:256], dg[:, 3 + kw, :], pre_ctx[:, kw:kw + 256], start=(kw == 0), stop=(kw == 2))
                            else:
                                pc3 = pc.re("p (a b) -> p a b", b=64)
                                for tap in range(9):
                                    kh, kw = tap // 3, tap % 3
                                    k.mm(pc3, dg[:, tap, :], pre_loc[:, 8 * g + kh:8 * g + kh + 8, kw:kw + 64],
                                         start=(tap == 0), stop=(tap == 8))
                            return pc

                        def stage2(g, pc):
                            n = 256 if g < 0 else 512
                            c0 = 0 if g < 0 else TOK0 + 512 * g
                            if which == "v":
                                vt = vt512.get()
                                k.act("act", vt[:, 0:n], pc[:, 0:n], AF.Silu)
                                for j in range(n // 128):
                                    pb = PBF.get()
                                    k.tr(pb, vt[:, j * 128:(j + 1) * 128], I_bf)
                                    k.cp("dve", vtok[:, c0 // 128 + j, :], pb)
                            else:
                                qs = f512.get()
                                k.act("act", qs[:, 0:n], pc[:, 0:n], AF.Silu)
                                sq = g512.get()
                                k.tt("dve", sq[:, 0:n], qs[:, 0:n], qs[:, 0:n], ALU.mult)
                                pss = PBIG.get()
                                k.mm(pss[:, 0:n], ONE_f, sq[:, 0:n])
                                k.act("act", sq[:, 0:n], pss[:, 0:n], AF.Sqrt, bias=epsc)
                                k.op("dve", lambda en: en.reciprocal(out=sq.ap[:, 0:n], in_=sq.ap[:, 0:n]), [sq], [sq])
                                if which == "k":
                                    k.tt("dve", KT[:, c0:c0 + n], qs[:, 0:n], sq[:, 0:n], ALU.mult)
                                    for j in range(n // 128):
                                        pb = PBF.get()
                                        k.tr(pb, KT[:, c0 + j * 128:c0 + (j + 1) * 128], I_bf)
                                        k.cp("dve", ktok[:, c0 // 128 + j, :], pb)
                                else:
                                    k.stt("dve", QT[:, c0 - TOK0:c0 - TOK0 + n], qs[:, 0:n], 128.0 ** -0.5, sq[:, 0:n],
                                          ALU.mult, ALU.mult)

                        prev_ = None
                        for g in (range(-1, 8) if which != "q" else range(0, 4)):
                            pc_ = stage1(g)
                            if prev_ is not None:
                                stage2(*prev_)
                            prev_ = (g, pc_)
                        stage2(*prev_)

                    def gdn_unit(h, d, t, full, own_idx, slot=0):
                        T1, B1 = t128s[slot], b128s[slot]
                        hd = d * 8 + h
                        UM = C(C_UF if d == 0 else C_UB)
                        NS = C(C_NSF if d == 0 else C_NSB)
                        NI = C(C_NIF if d == 0 else C_NIB)
                        la_bc = SM("la", t, hd).bc([128, 128])
                        lnb_bc = SM("lnb", t, hd).bc([128, 128])
                        negg = SM("negg", t, hd)
                        cs = slice(t * 128, (t + 1) * 128)
                        qs_ = slice(own_idx * 128, (own_idx + 1) * 128)
                        pA = PSM.get()
                        k.mm(pA, la_bc, UM, start=True, stop=False)
                        k.mm(pA, lnb_bc, I_f, start=False, stop=False)
                        k.mm(pA, I_f, NS, start=False, stop=True)
                        LA = T1.get()
                        k.act("act", LA, pA, AF.Exp, bias=negg)
                        yield
                        pK = PSM.get()
                        k.mm(pK, KT[:, cs], KT[:, cs])
                        Xt = T1.get()
                        k.stt("dve", Xt, pK, -1.0, LA, ALU.mult, ALU.mult)
                        yield
                        pb = PSM.get()
                        k.tr(pb, Xt, I_f)
                        Xn = T1.get()
                        k.cp("act", Xn, pb)
                        yield
                        Rk = T1.get()
                        k.tt("dve", Rk, Xt, I_f, ALU.add)
                        yield
                        for lvl in range(1, 7):
                            pn = PSM.get()
                            k.mm(pn, Xt, Xn)
                            Xn2 = T1.get()
                            k.cp("act", Xn2, pn)
                            yield
                            if lvl < 6:
                                pt_ = PSM.get()
                                k.mm(pt_, Xn, Xt)
                                Xt2 = T1.get()
                                k.cp("dve", Xt2, pt_)
                                yield
                            pr = PSM.get()
                            k.mm(pr, Xn2, Rk)
                            Rk2 = T1.get()
                            k.tt("dve", Rk2, pr, Rk, ALU.add)
                            yield
                            Xn, Rk = Xn2, Rk2
                            if lvl < 6:
                                Xt = Xt2
                        Tb = B1.get()
                        k.act("act", Tb, Rk, AF.Copy, scale=SM("bsc", t, hd))
                        yield
                        Tbg = B1.get()
                        k.act("act", Tbg, Rk, AF.Copy, scale=SM("bg", t, hd))
                        yield
                        pu = PSM.get()
                        k.mm(pu, Tb, vtok[:, t, :])
                        u32 = T1.get()
                        k.cp("act", u32, pu)
                        yield
                        pw = PSM.get()
                        k.mm(pw, ktok[:, t, :], Tbg)
                        wT = B1.get()
                        k.cp("dve", wT, pw)
                        yield
                        kdec = B1.get()
                        k.ts("dve", kdec, ktok[:, t, :], SM("ed", t, hd), ALU.mult)
                        yield
                        if full:
                            pL = PSM.get()
                            k.mm(pL, la_bc, UM, start=True, stop=False)
                            k.mm(pL, I_f, NI, start=False, stop=True)
                            Lt = T1.get()
                            k.act("act", Lt, pL, AF.Exp, bias=negg)
                            yield
                            pQ = PSM.get()
                            k.mm(pQ, KT[:, cs], QT[:, qs_])
                            attnT = B1.get()
                            k.tt("dve", attnT, pQ, Lt, ALU.mult)
                            yield
                            pE = PSM.get()
                            k.mm(pE, la_bc, UM)
                            Er = B1.get()
                            k.act("act", Er, pE, AF.Exp)
                            yield
                            qgT = B1.get()
                            k.tt("dve", qgT, QT[:, qs_], Er, ALU.mult)
                            yield
                        if h == 0 and d == 1 and t == 1:
                            dbg("LA", LA, [128, 128])
                            dbg("Rk", Rk, [128, 128])
                            dbg("Tb", Tb, [128, 128])
                            dbg("u32", u32, [128, 128])
                            dbg("wT", wT, [128, 128])
                            dbg("kdec", kdec, [128, 128])
                            dbg("smla", sm["la"][:, 1, :], [128, 16])
                            dbg("smnegg", V(sm["negg"].ap[:, 1, :], smres[1]), [128, 16])
                            dbg("smbg", V(sm["bg"].ap[:, 1, :], smres[1]), [128, 16])
                            dbg("smed", V(sm["ed"].ap[:, 1, :], smres[1]), [128, 16])
                        yield "chain"
                        Sb = Sbf[d].items[(Sbf[d].i - 1) % 2]
                        pws = PSM.get()
                        k.mm(pws, wT, Sb)
                        vnew = B1.get()
                        k.tt("dve", vnew, u32, pws, ALU.subtract)
                        yield
                        if full:
                            po = PSM.get()
                            k.mm(po, Sb, qgT, start=True, stop=False)
                            k.mm(po, vnew, attnT, start=False, stop=True)
                            oc = slice(own_idx * 128, (own_idx + 1) * 128)
                            if d == 1:
                                k.cp("act", oBT[:, oc], po)
                                yield
                            else:
                                o32 = T1.get()
                                k.tt("dve", o32, po, oBT[:, oc], ALU.add)
                                yield
                                sq = T1.get()
                                k.tt("dve", sq, o32, o32, ALU.mult)
                                yield
                                pss = PSM.get()
                                k.mm(pss, ONE_f, sq)
                                rn = T1.get()
                                rsqrt_le(rn, pss, 1.0 / 128)
                                k.tt("dve", o32, o32, rn, ALU.mult)
                                yield
                                k.stt("dve", ycur[:, oc], o32, nw[:, 0:1], gaT[:, oc], ALU.mult, ALU.mult)
                                yield
                        pS = PSM.get()
                        k.mm(pS, kdec, vnew)
                        k.stt("dve", S32[d], S32[d], SM("dl", t, hd), pS, ALU.mult, ALU.add)
                        yield
                        Sn = Sbf[d].get()
                        k.cp("dve", Sn, S32[d])
                        yield

                    for h in range(NH if stage >= 3 else 1):
                        wq, wk, wv, wga = (load_w(O_Q + h * 128), load_w(O_K + h * 128),
                                           load_w(O_V + h * 128), load_w(O_GA + h * 128))
                        conv_head(wk, 8 + h, "k")
                        if h == 0:
                            dbg("wk", wk, [128, 8, 128])
                            dbg("wst", wst.items[1], [128, 8, 128])
                            dbg("prectx", pre_ctx, [128, 258])
                            dbg("preloc", pre_loc[:, 0:4, :], [128, 4, 66])
                            dbg("dg", dgr.items[0], [128, 9, 128])
                        cut(2)
                        conv_head(wv, 16 + h, "v")
                        conv_head(wq, h, "q")
                        cut(3)
                        for g in range(4):
                            project_fm(wga, TOK0 + 512 * g, 512,
                                       lambda p: k.act("act", gaT[:, 512 * g:512 * (g + 1)], p, AF.Silu))
                        ycur = yT.get()
                        allseq = []
                        for d in (1, 0):
                            k.memset("pool", S32[d], 0.0)
                            k.memset("pool", Sbf[d].get(), 0.0)
                            seq = []
                            if d == 0:
                                seq += [(t, False, -1) for t in range(NCTX)]
                                seq += [(NCTX + t, True, t) for t in range(NOWN)]
                            else:
                                seq += [(t, False, -1) for t in reversed(range(NCTX))]
                                seq += [(NCTX + t, False, -1) for t in reversed(range(NOWN, NLOC))]
                                seq += [(NCTX + t, True, t) for t in reversed(range(NOWN))]
                            allseq += [(h, d, t, full, oi) for (t, full, oi) in seq]
                        run_units(gdn_unit, allseq, 4)
                        if h == 0:
                            dbg("KT", KT[:, 0:1024], [128, 1024])
                            dbg("vtok", vtok[:, 0:4, :], [128, 4, 128])
                            dbg("ya0", ycur, [128, NOWN * 128])
                            dbg("S32", S32[0], [128, 128])
                        k.dma("sp", V(ydr[0, h], ydr_res[0][h]), ycur)
                cut(7)

                with Scope():
                    vtokH = k.sbv([128, NT, 128], BF16, "vtokH")
                    qbT = k.sbv([128, NOWN * 128], BF16, "qbT")
                    gbT = k.sbv([128, NOWN * 128], BF16, "gbT")
                    oBT = k.sbv([128, NOWN * 128], BF16, "oBTh")
                    yT = k.ring(2, [128, NOWN * 128], BF16)
                    S32 = [k.sbv([128, 128], F32, "HS32_%d" % d) for d in range(2)]
                    Sbf = [k.ring(2, [128, 128], BF16) for d in range(2)]
                    vmall = k.sbv([128, NT, 4, 128], BF16, "vmall")
                    ebrs = [k.ring(1, [128, 4], F32) for _ in range(3)]
                    RST = C(C_RST)
                    PM = C(C_SUB)

                    def hgrn_unit(h, d, t, full, own_idx, wf, slot=0):
                        T1, B1 = t128s[slot], b128s[slot]
                        hd = d * 8 + h
                        HM = C(C_HMF if d == 0 else C_HMB)
                        lbc, omlc, nomlc = lb[:, hd:hd + 1], oml[:, hd:hd + 1], noml[:, hd:hd + 1]
                        hv = hxs(t * 128, 128)
                        pf = PSM.get()
                        for kt in range(8):
                            k.mm(pf, wf[:, kt, :], hv(kt), start=(kt == 0), stop=(kt == 7))
                        sig = T1.get()
                        k.act("act", sig, pf, AF.Exp, scale=-1.0)
                        yield
                        k.ts("dve", sig, sig, 1.0, ALU.add)
                        yield
                        k.op("dve", lambda en: en.reciprocal(out=sig.ap, in_=sig.ap), [sig], [sig])
                        yield
                        lf = T1.get()
                        k.act("act", lf, sig, AF.Ln, bias=lbc, scale=omlc)
                        yield
                        kb = T1.get()
                        k.ts("dve", kb, sig, nomlc, ALU.mult, omlc, ALU.add)
                        yield
                        bb = T1.get()
                        k.scan(bb, RST, lf, 0.0, ALU.mult, ALU.add)
                        yield
                        if d == 1:
                            b2 = T1.get()
                            k.stt("dve", b2, bb, -1.0, lf, ALU.mult, ALU.add)
                            yield
                            for c in range(4):
                                k.ts("dve", b2[:, 32 * c:32 * c + 32], b2[:, 32 * c:32 * c + 32],
                                     bb[:, 32 * c + 31:32 * c + 32], ALU.add)
                            bb = b2
                        lastcol = [(32 * c + 31) if d == 0 else (32 * c) for c in range(4)]
                        Dm = T1.get()
                        for c in range(4):
                            k.ts("dve", Dm[:, 32 * c:32 * c + 32], bb[:, 32 * c:32 * c + 32],
                                 bb[:, 32 * c + 15:32 * c + 16], ALU.subtract)
                        EK = T1.get()
                        k.act("act", EK, Dm, AF.Exp, scale=-1.0)
                        yield
                        Kp = B1.get()
                        k.tt("dve", Kp, kb, EK, ALU.mult)
                        yield
                        ED = T1.get()
                        eb = ebrs[slot].get()
                        for c in range(4):
                            lc = lastcol[c]
                            k.act("act", ED[:, 32 * c:32 * c + 32], bb[:, 32 * c:32 * c + 32], AF.Exp,
                                  bias=bb[:, lc:lc + 1], scale=-1.0)
                            yield
                        k.act("act", eb, bb[:, lastcol[0]:128:32], AF.Exp)
                        yield
                        kdT = B1.get()
                        k.tt("dve", kdT, kb, ED, ALU.mult)
                        yield
                        pb = PBF.get()
                        k.tr(pb, kdT, I_bf)
                        kdtok = B1.get()
                        k.cp("dve", kdtok, pb)
                        yield
                        pG = PBIG.get()
                        for c in range(4):
                            k.mm(pG[:, 128 * c:128 * c + 128], kdtok, vmall[:, t, c, :])
                        if full:
                            oc = slice(own_idx * 128, (own_idx + 1) * 128)
                            EQ = T1.get()
                            k.act("act", EQ, Dm, AF.Exp)
                            yield
                            Qp = B1.get()
                            k.tt("dve", Qp, qbT[:, oc], EQ, ALU.mult)
                            yield
                            pa = PSM.get()
                            k.mm(pa, Kp, Qp)
                            aT = B1.get()
                            k.tt("dve", aT, pa, HM, ALU.mult)
                            yield
                            EB = T1.get()
                            k.act("act", EB, bb, AF.Exp)
                            yield
                            Qb = B1.get()
                            k.tt("dve", Qb, qbT[:, oc], EB, ALU.mult)
                            yield
                            po = PSM.get()
                        yield "chain"
                        for c in (range(4) if d == 0 else reversed(range(4))):
                            Sb = Sbf[d].items[(Sbf[d].i - 1) % 2]
                            cs = slice(32 * c, 32 * c + 32)
                            if full:
                                k.mm(po[:, cs], Sb, Qb[:, cs], start=True, stop=False)
                                k.mm(po[:, cs], vtokH[:, t, :], aT[:, cs], start=False, stop=True)
                            k.stt("dve", S32[d], S32[d], eb[:, c:c + 1], pG[:, 128 * c:128 * c + 128], ALU.mult, ALU.add)
                            yield
                            Sn = Sbf[d].get()
                            k.cp("act", Sn, S32[d])
                            yield
                        if full:
                            if d == 1:
                                k.cp("act", oBT[:, oc], po)
                                yield
                            else:
                                o32 = T1.get()
                                k.tt("dve", o32, po, oBT[:, oc], ALU.add)
                                yield
                                sq = T1.get()
                                k.tt("dve", sq, o32, o32, ALU.mult)
                                yield
                                pss = PSM.get()
                                k.mm(pss, ONE_f, sq)
                                rn = T1.get()
                                rsqrt_le(rn, pss, 1.0 / 128)
                                k.tt("dve", o32, o32, rn, ALU.mult)
                                yield
                                k.stt("dve", ycurH[:, oc], o32, nw[:, 1:2], gbT[:, oc], ALU.mult, ALU.mult)
                                yield

                    for h in range(NH if stage >= 3 else 1):
                        wfF, wfB, wqb, wib, wgb = (load_w(O_F + h * 128), load_w(O_F + 1024 + h * 128),
                                                   load_w(O_QB + h * 128), load_w(O_IB + h * 128),
                                                   load_w(O_GB + h * 128))
                        for t in range(NT):
                            hv = hxs(t * 128, 128)
                            pv = PSM.get()
                            for kt in range(8):
                                k.mm(pv, hv(kt), wib[:, kt, :], start=(kt == 0), stop=(kt == 7))
                            k.cp("act", vtokH[:, t, :], pv)
                            for c in range(4):
                                k.ts("dve", vmall[:, t, c, :], vtokH[:, t, :], PM[:, c:c + 1], ALU.mult)
                        for g in range(4):
                            project_fm(wqb, TOK0 + 512 * g, 512,
                                       lambda p: k.act("act", qbT[:, 512 * g:512 * (g + 1)], p, AF.Silu))
                            project_fm(wgb, TOK0 + 512 * g, 512,
                                       lambda p: k.act("act", gbT[:, 512 * g:512 * (g + 1)], p, AF.Silu))
                        k.ts("dve", qbT, qbT, 128.0 ** -0.5, ALU.mult)
                        ycurH = yT.get()
                        allseq = []
                        for d in (1, 0):
                            k.memset("dve", S32[d], 0.0)
                            k.memset("dve", Sbf[d].get(), 0.0)
                            seq = []
                            if d == 0:
                                seq += [(t, False, -1) for t in range(NCTX)]
                                seq += [(NCTX + t, True, t) for t in range(NOWN)]
                            else:
                                seq += [(t, False, -1) for t in reversed(range(NCTX))]
                                seq += [(NCTX + t, False, -1) for t in reversed(range(NOWN, NLOC))]
                                seq += [(NCTX + t, True, t) for t in reversed(range(NOWN))]
                            allseq += [(h, d, t, full, oi, wfF if d == 0 else wfB) for (t, full, oi) in seq]
                        run_units(hgrn_unit, allseq, 3)
                        if h == 0:
                            dbg("yb0", ycurH, [128, NOWN * 128])
                            dbg("HS", S32[0], [128, 128])
                        k.dma("sp", V(ydr[1, h], ydr_res[1][h]), ycurH)
            cut(8)

            yres = [R() for _ in range(NOWN)]
            h2 = [V(hxT_t[:, :, TOK0 + (NOWN + t) * 128:TOK0 + (NOWN + t + 1) * 128], hx[NCTX + NOWN + t].res)
                  for t in range(NOWN)]

            def load_cols(dst, src_d, ncol_tiles):
                sv = src_d.rearrange("(kt p) n -> p kt n", p=128)
                for ct in range(ncol_tiles):
                    st_ = wst.get()
                    k.dma("sp", st_, sv[:, :, ct * 128:(ct + 1) * 128])
                    k.cp("dve", dst[:, :, ct * 128:(ct + 1) * 128], st_)

            with Scope():
                Wa = k.sbv([128, 8, D], BF16, "Wa")
                Wb = k.sbv([128, 8, D], BF16, "Wb")
                Wo = k.sbv([128, 8, D], BF16, "Wo")
                load_cols(Wa, wa_d, 8)
                load_cols(Wb, wb_d, 8)
                load_cols(Wo, wo_d, 8)
                yA = k.sbv([128, NH, 512], BF16, "yA")
                yB = k.sbv([128, NH, 512], BF16, "yB")
                mgd = k.sbv([128, 8, 512], BF16, "mgd")
                xr = k.ring(1, [128, D], F32)
                o2r = k.ring(1, [128, D], F32)
                sqr = k.ring(1, [128, D], F32)
                str_ = k.ring(4, [128, 2], F32)
                sg = k.ring(1, [128, 512], F32)
                m1r = k.ring(2, [128, 512], F32)
                for G in range(4):
                    gc = slice(512 * G, 512 * (G + 1))
                    for h in range(NH):
                        k.dma("sp", yA[:, h, :], V(ydr[0, h][:, gc], ydr_res[0][h]))
                        k.dma("sp", yB[:, h, :], V(ydr[1, h][:, gc], ydr_res[1][h]))
                    hvG = hxs(TOK0 + 512 * G, 512)
                    for dt in range(8):
                        wga_ = load_w(O_MG + dt * 128)
                        wgb_ = load_w(O_MG + 1024 + dt * 128)
                        dc = slice(dt * 128, (dt + 1) * 128)
                        pga = PBIG.get()
                        for kt in range(8):
                            k.mm(pga, wga_[:, kt, :], hvG(kt), start=(kt == 0), stop=(kt == 7))
                        sga = sg.get()
                        k.act("act", sga, pga, AF.Sigmoid)
                        pa_ = PBIG.get()
                        for h in range(NH):
                            k.mm(pa_, Wa[:, h, dc], yA[:, h, :], start=(h == 0), stop=(h == NH - 1))
                        m1 = m1r.get()
                        k.tt("dve", m1, pa_, sga, ALU.mult)
                        pgb = PBIG.get()
                        for kt in range(8):
                            k.mm(pgb, wgb_[:, kt, :], hvG(kt), start=(kt == 0), stop=(kt == 7))
                        sgb = sg.get()
                        k.act("act", sgb, pgb, AF.Sigmoid)
                        pb_ = PBIG.get()
                        for h in range(NH):
                            k.mm(pb_, Wb[:, h, dc], yB[:, h, :], start=(h == 0), stop=(h == NH - 1))
                        m2 = m1r.get()
                        k.tt("dve", m2, pb_, sgb, ALU.mult)
                        k.tt("dve", mgd[:, dt, :], m1, m2, ALU.add)
                    if G == 0:
                        dbg("yA", yA, [128, NH, 512])
                        dbg("yB", yB, [128, NH, 512])
                        dbg("mgd", mgd, [128, 8, 512])
                    for j in range(4):
                        t = 4 * G + j
                        o2 = o2r.get()
                        for half in range(2):
                            po_ = PBIG.get()
                            for dt in range(8):
                                k.mm(po_, mgd[:, dt, j * 128:(j + 1) * 128], Wo[:, dt, half * 512:(half + 1) * 512],
                                     start=(dt == 0), stop=(dt == 7))
                            k.cp("act", o2[:, half * 512:(half + 1) * 512], po_)
                        sq = sqr.get()
                        st = str_.get()
                        if t == 0:
                            dbg("o2raw", o2, [128, D])
                        k.tt("dve", sq, o2, o2, ALU.mult)
                        k.red("dve", st[:, 0:1], sq)
                        rsqrt(st[:, 1:2], st[:, 0:1], 1.0 / D)
                        if t == 0:
                            dbg("o2st", st, [128, 2])
                        k.act("act", o2, o2, AF.Copy, scale=st[:, 1:2])
                        k.tt("dve", o2, o2, grow[:, 0, :], ALU.mult)
                        if t == 0:
                            dbg("o2sc", o2, [128, D])
                        xt = xr.get()
                        k.dma("sp", xt, x_d[t * 128:(t + 1) * 128, :])
                        k.tt("dve", xt, xt, o2, ALU.add)
                        k.dma("sp", V(y_d[t * 128:(t + 1) * 128, :], yres[t]), xt)
                        if t == 0:
                            dbg("x1t0", xt, [128, D])
                        norm_T(xt, sqr.get(), str_.get(), h2[t], A2, B2, 0, t)
            cut(9)

            w1_v = w1_d.rearrange("(kt p) n -> p kt n", p=128)
            w2_v = w2_d.rearrange("(ft p) n -> p ft n", p=128)
            with Scope():
                hid = k.sbv([128, 32, 512], BF16, "hid")
                w2r = k.ring(3, [128, 512], BF16)
                w2s = k.ring(3, [128, 512], F32)
                rl = k.ring(2, [128, 512], F32)
                xr = k.ring(1, [128, D], F32)
                o3r = k.ring(4, [128, D], F32)
                sqr = k.ring(1, [128, D], F32)
                str_ = k.ring(4, [128, 2], F32)
                PACC = [V(b.ap, b.res, True) for b in _psm]
                for G in range(4):
                    hvG = lambda kt: V(hxT_t[:, kt, TOK0 + (NOWN + 4 * G) * 128:TOK0 + (NOWN + 4 * G + 4) * 128],
                                       tuple(h2[4 * G + j].res for j in range(4)))
                    for ft in range(32):
                        st_ = wst.get()
                        k.dma("sp", st_, w1_v[:, :, ft * 128:(ft + 1) * 128])
                        w1 = wring.get()
                        k.cp("dve", w1, st_)
                        ph = PBIG.get()
                        for kt in range(8):
                            k.mm(ph, w1[:, kt, :], hvG(kt), start=(kt == 0), stop=(kt == 7))
                        r_ = rl.get()
                        k.act("act", r_, ph, AF.Relu)
                        k.tt("dve", hid[:, ft, :], r_, r_, ALU.mult)
                    o3s = [o3r.get() for _ in range(4)]
                    for half in range(2):
                        for ft in range(32):
                            s2 = w2s.get()
                            k.dma("sp", s2, w2_v[:, ft, half * 512:(half + 1) * 512])
                            w2 = w2r.get()
                            k.cp("dve", w2, s2)
                            for j in range(4):
                                k.mm(PACC[j], hid[:, ft, j * 128:(j + 1) * 128], w2, start=(ft == 0), stop=(ft == 31))
                        for j in range(4):
                            k.cp("act", o3s[j][:, half * 512:(half + 1) * 512], PACC[j])
                    for j in range(4):
                        t = 4 * G + j
                        o3 = o3s[j]
                        sq = sqr.get()
                        st = str_.get()
                        k.tt("dve", sq, o3, o3, ALU.mult)
                        k.red("dve", st[:, 0:1], sq)
                        rsqrt(st[:, 1:2], st[:, 0:1], 1.0 / D)
                        k.act("act", o3, o3, AF.Copy, scale=st[:, 1:2])
                        k.tt("dve", o3, o3, grow[:, 1, :], ALU.mult)
                        xt = xr.get()
                        yv = V(y_d[t * 128:(t + 1) * 128, :], yres[t])
                        k.dma("sp", xt, yv)
                        k.tt("dve", xt, xt, o3, ALU.add)
                        k.dma("sp", yv, xt, is_out=True)
            k.finish()
            raise Done()


        try:
            _phase2plus()
            raise Cut()
        except Done:
            return nc, dbg_d
        except Cut:
            z = V(hxT_t[:, 0, 0:2048].bitcast(F32), tuple(h_.res for h_ in hx))
            k.memset("dve", z, 0.0)
            for t in range(NOWN):
                k.dma("sp", y_d[t * 128:(t + 1) * 128, :], z, is_out=True)
        k.finish()
    return nc, dbg_d


def host_prep(inputs):
    f = np.float32
    x = np.asarray(inputs["x"], f)
    c = np.asarray(inputs["c"], f)
    ctx = np.asarray(inputs["ctx"], f)
    c_ctx = np.asarray(inputs["c_ctx"], f)
    w_ada = np.ascontiguousarray(np.asarray(inputs["w_ada"], f)[0])
    b_ada = np.asarray(inputs["b_ada"], f)[0]
    norm_g = np.asarray(inputs["norm_g"], f)[0]
    w_in = np.asarray(inputs["w_in"], f)[0]
    conv_w = np.asarray(inputs["conv_w"], f)[0]
    a_log = np.asarray(inputs["gdn_a_log"], f)[0]
    dt_b = np.asarray(inputs["gdn_dt_bias"], f)[0]
    gnw = np.asarray(inputs["gdn_norm_w"], f)[0]
    lbl = np.asarray(inputs["hgrn_lb_logits"], f)
    hnw = np.asarray(inputs["hgrn_norm_w"], f)[0]

    i = np.arange(128)
    consts = np.zeros((NCONST, 128, 128), f)
    consts[C_I] = np.eye(128)
    consts[C_ONE] = 1.0
    consts[C_UF] = (i[:, None] <= i[None, :])
    consts[C_UB] = (i[:, None] >= i[None, :])
    consts[C_NSF] = np.where(i[None, :] > i[:, None], 0.0, NEG)
    consts[C_NSB] = np.where(i[None, :] < i[:, None], 0.0, NEG)
    consts[C_NIF] = np.where(i[None, :] >= i[:, None], 0.0, NEG)
    consts[C_NIB] = np.where(i[None, :] <= i[:, None], 0.0, NEG)
    same = (i[:, None] // 32) == (i[None, :] // 32)
    consts[C_HMF] = same & (i[:, None] <= i[None, :])
    consts[C_HMB] = same & (i[:, None] >= i[None, :])
    consts[C_RST] = np.broadcast_to((i % 32 != 0).astype(f)[None, :], (128, 128))
    consts[C_SUB][:, 0:4] = ((i[:, None] // 32) == np.arange(4)[None, :])
    consts = np.ascontiguousarray(consts.transpose(1, 0, 2).reshape(128, NCONST * 128))

    bada_fm = np.ascontiguousarray(b_ada.reshape(48, 128).T)
    bgt = np.ascontiguousarray(np.broadcast_to(
        np.stack([b_ada[2 * D:3 * D], b_ada[5 * D:6 * D]])[None], (128, 2, D)))
    g02 = np.ascontiguousarray(np.stack([norm_g[0].reshape(8, 128).T, norm_g[2].reshape(8, 128).T], axis=1))
    g13 = np.ascontiguousarray(np.broadcast_to(np.stack([norm_g[1], norm_g[3]])[None], (128, 2, D)))
    nwv = np.ascontiguousarray(np.stack([gnw, hnw], axis=1))

    def perm_cols(flip):
        cols = []
        cols.append(np.arange(0, 4096))
        for base in (4096, 4112):
            idx = base + np.arange(16).reshape(2, 8)
            cols.append((idx[::-1] if flip else idx).reshape(-1))
        idx = 4128 + np.arange(2048).reshape(2, 1024)
        cols.append((idx[::-1] if flip else idx).reshape(-1))
        cols.append(np.arange(6176, IN_W))
        return np.concatenate(cols)

    maps = []
    shared = dict(w_ada=w_ada, bada_fm=bada_fm, bgt=bgt, g02=g02, g13=g13, nw=nwv,
                  w_a=np.ascontiguousarray(np.asarray(inputs["w_branch_a"], f)[0]),
                  w_b=np.ascontiguousarray(np.asarray(inputs["w_branch_b"], f)[0]),
                  w_o=np.ascontiguousarray(np.asarray(inputs["w_out"], f)[0]),
                  w_1=np.ascontiguousarray(np.asarray(inputs["w_mlp_in"], f)[0]),
                  w_2=np.ascontiguousarray(np.asarray(inputs["w_mlp_out"], f)[0]),
                  consts=consts)
    per_flip = {}
    for flip in (0, 1):
        win_l = np.ascontiguousarray(w_in[:, perm_cols(flip)])
        cw = conv_w[::-1, ::-1, :] if flip else conv_w
        convw = np.ascontiguousarray(cw.reshape(9, 24, 128).transpose(2, 1, 0))
        al = a_log[::-1] if flip else a_log
        db = dt_b[::-1] if flip else dt_b
        abv = np.ascontiguousarray(np.broadcast_to(np.stack([al.reshape(-1), db.reshape(-1)])[None], (128, 2, 16)))
        ll = lbl[:, ::-1, :] if flip else lbl
        lblv = np.ascontiguousarray(ll.reshape(2, 2, 8, 128).transpose(3, 0, 1, 2).reshape(128, 2, 16))
        per_flip[flip] = dict(w_in=win_l, convw=convw, ab=abv, lbl=lblv)
    for core in range(8):
        b, s = core // 2, core % 2
        xb = x[b][::-1] if s else x[b]
        cb = ctx[b][::-1] if s else ctx[b]
        cfm = np.ascontiguousarray(np.stack([c[b], c_ctx], axis=1).reshape(8, 128, 2).transpose(1, 0, 2))
        m = dict(shared)
        m.update(per_flip[s])
        m.update(x=np.ascontiguousarray(xb), ctx=np.ascontiguousarray(cb), cfm=cfm)
        maps.append(m)
    return maps


_NC_CACHE = {}


def kernel(**inputs):
    maps = host_prep(inputs)
    if "nc" not in _NC_CACHE:
        import os
        _NC_CACHE["nc"] = build(cutn=int(os.environ.get("KCUT", "0")))[0]
    nc = _NC_CACHE["nc"]
    res = run_bass_kernel_spmd(nc, maps, core_ids=list(range(8)))
    out = np.zeros((4, 4096, D), np.float32)
    for core in range(8):
        b, s = core // 2, core % 2
        y = np.asarray(res.results[core]["y"])
        if s:
            out[b, 2048:] = y[::-1]
        else:
            out[b, :2048] = y
    return out
```

```python
import numpy as np
from contextlib import ExitStack
import concourse.bass as bass
import concourse.mybir as mybir
from concourse.bass_utils import run_bass_kernel_spmd

F32 = mybir.dt.float32
BF16 = mybir.dt.bfloat16
AF = mybir.ActivationFunctionType
ALU = mybir.AluOpType
AX = mybir.AxisListType

D = 1024
NH = 8
EPS = 1e-6
NOWN = 16
NOTH = 16
NLOC = NOWN + NOTH
NCTX = 2
NT = NCTX + NLOC
TOK0 = NCTX * 128
NTOK = NT * 128
NEG = -30000.0
IN_W = 11296
O_Q, O_K, O_V, O_GA, O_A, O_B, O_F, O_QB, O_IB, O_GB, O_MG = (
    0, 1024, 2048, 3072, 4096, 4112, 4128, 6176, 7200, 8224, 9248)

C_I, C_ONE, C_UF, C_UB, C_NSF, C_NSB, C_NIF, C_NIB, C_HMF, C_HMB, C_RST, C_SUB = range(12)
NCONST = 12


class R:
    __slots__ = ("w", "r")

    def __init__(self):
        self.w = None
        self.r = {}


class V:
    def __init__(self, ap, res, psum=False):
        self.ap = ap
        self.res = res
        self.psum = psum

    def __getitem__(self, k):
        return V(self.ap[k], self.res, self.psum)

    def bc(self, shape):
        return V(self.ap.to_broadcast(list(shape)), self.res, self.psum)

    def re(self, s, **kw):
        return V(self.ap.rearrange(s, **kw), self.res, self.psum)


def _flat(vs):
    out = []
    for v in vs:
        if isinstance(v, V):
            if isinstance(v.res, tuple):
                out.extend(v.res)
            else:
                out.append(v.res)
    return out


class KB:
    def __init__(self, nc, es):
        self.nc = nc
        self.es = es
        self.eng = {"pe": nc.tensor, "act": nc.scalar, "dve": nc.vector,
                    "pool": nc.gpsimd, "sp": nc.sync}
        self.sem = {n: es.enter_context(nc.semaphore("s_" + n)) for n in self.eng}
        self.cnt = {n: 0 for n in self.eng}
        self.known = {n: {} for n in self.eng}
        self.nslots = {"sp": 24, "pool": 12, "act": 4}
        self.dsem = {}
        self.dval = {}
        self.dnext = {q: 0 for q in self.nslots}
        for q, n in self.nslots.items():
            for i in range(n):
                key = ("d", q, i)
                self.dsem[key] = es.enter_context(nc.semaphore("d_%s_%d" % (q, i)))
                self.dval[key] = 0
        self.out_toks = []
        self.uid = 0
        self.ninst = 0

    def sb(self, shape, dt=F32, name=None):
        self.uid += 1
        t = self.es.enter_context(self.nc.sbuf_tensor("sb_" + (name or ("t%d" % self.uid)), list(shape), dt))
        return t

    def sbv(self, shape, dt=F32, name=None):
        t = self.sb(shape, dt, name)
        return V(t[tuple(slice(None) for _ in shape)], R())

    def ps(self, shape, dt=F32, name=None):
        self.uid += 1
        t = self.es.enter_context(self.nc.psum_tensor("ps_" + (name or ("p%d" % self.uid)), list(shape), dt))
        return V(t[tuple(slice(None) for _ in shape)], R(), True)

    def ring(self, n, shape, dt=F32, psum=False):
        items = [(self.ps(shape, dt) if psum else self.sbv(shape, dt)) for _ in range(n)]
        return Ring(items)

    def _semof(self, key):
        return self.sem[key] if isinstance(key, str) else self.dsem[key]

    def _wait(self, e, toks):
        kn = self.known[e]
        for tok in toks:
            if tok is None:
                continue
            key, val, snap = tok
            if kn.get(key, 0) >= val:
                continue
            self.eng[e].wait_ge(self._semof(key), val)
            self.ninst += 1
            kn[key] = val
            for k2, v2 in snap.items():
                if kn.get(k2, 0) < v2:
                    kn[k2] = v2

    @staticmethod
    def _toks(rd, wr):
        toks = []
        for r in rd:
            toks.append(r.w)
        for r in wr:
            toks.append(r.w)
            toks.extend(r.r.values())
        return toks

    @staticmethod
    def _commit(tok, rd, wr):
        for r in rd:
            r.r[tok[0]] = tok
        for r in wr:
            r.w = tok
            r.r = {}

    def op(self, e, fn, rd, wr):
        wr = list(wr) + [v for v in rd if isinstance(v, V) and v.psum]
        rd = _flat(rd)
        wr = _flat(wr)
        self._wait(e, self._toks(rd, wr))
        inst = fn(self.eng[e])
        self.cnt[e] += 1
        self.ninst += 1
        inst.then_inc(self.sem[e], 1)
        tok = (e, self.cnt[e], dict(self.known[e]))
        self._commit(tok, rd, wr)

    def dma(self, q, out, in_, is_out=False):
        rd = _flat([in_])
        wr = _flat([out])
        i = self.dnext[q]
        self.dnext[q] = (i + 1) % self.nslots[q]
        key = ("d", q, i)
        prev = self.dval[key]
        toks = self._toks(rd, wr)
        if prev > 0:
            toks.append((key, prev, {}))
        self._wait(q, toks)
        oap = out.ap if isinstance(out, V) else out
        iap = in_.ap if isinstance(in_, V) else in_
        inst = self.eng[q].dma_start(out=oap, in_=iap)
        self.ninst += 1
        val = prev + 16
        inst.then_inc(self.dsem[key], 16)
        self.dval[key] = val
        tok = (key, val, dict(self.known[q]))
        self._commit(tok, rd, wr)
        if is_out:
            self.out_toks.append(tok)

    def barrier(self):
        toks = [(e, c, {}) for e, c in self.cnt.items() if c > 0]
        toks += [(k_, v, {}) for k_, v in self.dval.items() if v > 0]
        for e in self.eng:
            self._wait(e, toks)

    def finish(self):
        self._wait("sp", self.out_toks)
        toks = [(k, v, {}) for k, v in self.dval.items() if v > 0]
        self._wait("sp", toks)

    def mm(self, out, lhsT, rhs, start=True, stop=True):
        self.op("pe", lambda e: e.matmul(out.ap, lhsT=lhsT.ap, rhs=rhs.ap, start=start, stop=stop),
                [lhsT, rhs] + ([] if start else [out]), [out])

    def tr(self, out, in_, ident):
        self.op("pe", lambda e: e.transpose(out.ap, in_.ap, ident.ap), [in_, ident], [out])

    def act(self, e, out, in_, func, bias=None, scale=1.0):
        kw = {}
        if bias is not None:
            kw["bias"] = bias.ap if isinstance(bias, V) else bias
        kw["scale"] = scale.ap if isinstance(scale, V) else scale
        assert e == "act"
        self.op("act", lambda en: en.activation(out=out.ap, in_=in_.ap, func=func, **kw),
                [in_, bias, scale], [out])

    def tt(self, e, out, a, b, op):
        self.op(e, lambda en: en.tensor_tensor(out=out.ap, in0=a.ap, in1=b.ap, op=op), [a, b], [out])

    def ts(self, e, out, a, s1, op0, s2=None, op1=None):
        s1a = s1.ap if isinstance(s1, V) else s1
        s2a = s2.ap if isinstance(s2, V) else s2
        if op1 is None:
            fn = lambda en: en.tensor_scalar(out=out.ap, in0=a.ap, scalar1=s1a, scalar2=None, op0=op0)
        else:
            fn = lambda en: en.tensor_scalar(out=out.ap, in0=a.ap, scalar1=s1a, scalar2=s2a, op0=op0, op1=op1)
        self.op(e, fn, [a, s1, s2], [out])

    def stt(self, e, out, a, s, b, op0, op1):
        sa = s.ap if isinstance(s, V) else s
        self.op(e, lambda en: en.scalar_tensor_tensor(out=out.ap, in0=a.ap, scalar=sa, in1=b.ap, op0=op0, op1=op1),
                [a, s, b], [out])

    def cp(self, e, out, in_):
        if e == "act":
            self.op("act", lambda en: en.copy(out=out.ap, in_=in_.ap), [in_], [out])
        else:
            self.op(e, lambda en: en.tensor_copy(out=out.ap, in_=in_.ap), [in_], [out])

    def memset(self, e, out, val):
        self.op(e, lambda en: en.memset(out.ap, val), [], [out])

    def red(self, e, out, in_, op=ALU.add):
        self.op(e, lambda en: en.tensor_reduce(out=out.ap, in_=in_.ap, axis=AX.X, op=op), [in_], [out])

    def scan(self, out, d0, d1, init, op0, op1):
        self.op("dve", lambda en: en.tensor_tensor_scan(out=out.ap, data0=d0.ap, data1=d1.ap, initial=init,
                                                        op0=op0, op1=op1), [d0, d1], [out])


class Ring:
    def __init__(self, items):
        self.items = items
        self.i = 0

    def get(self):
        it = self.items[self.i % len(self.items)]
        self.i += 1
        return it


class Cut(Exception):
    pass


class Done(Exception):
    pass


def build(stage=99, dbg_names=(), cutn=0):
    nc = bass.Bass("TRN2", target_bir_lowering=False)

    def cut(n):
        if cutn == n:
            raise Cut()

    def din(name, shape, dt=F32):
        return nc.dram_tensor(name, list(shape), dt, kind="ExternalInput").ap()

    x_d = din("x", [NLOC * 128, D])
    ctx_d = din("ctx", [NCTX * 128, D])
    cfm_d = din("cfm", [128, 8, 2])
    wada_d = din("w_ada", [D, 6 * D])
    badafm_d = din("bada_fm", [128, 48])
    bgt_d = din("bgt", [128, 2, D])
    g02_d = din("g02", [128, 2, 8])
    g13_d = din("g13", [128, 2, D])
    win_d = din("w_in", [D, IN_W])
    convw_d = din("convw", [128, 24, 9])
    ab_d = din("ab", [128, 2, 16])
    lbl_d = din("lbl", [128, 2, 16])
    nw_d = din("nw", [128, 2])
    wa_d = din("w_a", [D, D])
    wb_d = din("w_b", [D, D])
    wo_d = din("w_o", [D, D])
    w1_d = din("w_1", [D, 4 * D])
    w2_d = din("w_2", [4 * D, D])
    cst_d = din("consts", [128, NCONST * 128])
    y_d = nc.dram_tensor("y", [NOWN * 128, D], F32, kind="ExternalOutput").ap()
    dbg_d = {}

    es = ExitStack()
    with es:
        k = KB(nc, es)

        def dbg(name, v, shape):
            if name in dbg_names:
                d = nc.dram_tensor("dbg_" + name, list(shape), v.ap.dtype, kind="ExternalOutput").ap()
                dbg_d[name] = d
                k.dma("sp", d, v, is_out=True)

        PBIG = k.ring(3, [128, 512], F32, psum=True)
        _psm = [k.ps([128, 512], F32) for _ in range(4)]
        PSM = Ring([V(b.ap[:, j * 128:(j + 1) * 128], b.res, True) for j in range(4) for b in _psm])
        _pbf = k.ps([128, 512], BF16)
        PBF = Ring([V(_pbf.ap[:, j * 128:(j + 1) * 128], _pbf.res, True) for j in range(4)])

        class Scope:
            def __enter__(s_):
                s_.es = ExitStack()
                s_.es.__enter__()
                s_.save = k.es
                k.es = s_.es
                return s_

            def __exit__(s_, *a):
                k.barrier()
                k.es = s_.save
                s_.es.__exit__(None, None, None)
                return False

        cst = k.sbv([128, NCONST * 128], F32, "cst")
        k.dma("sp", cst, cst_d)

        def C(i):
            return cst[:, i * 128:(i + 1) * 128]

        cbf = k.sbv([128, 2 * 128], BF16, "cbf")
        k.cp("dve", cbf, cst[:, 0:256])
        I_bf = cbf[:, 0:128]
        ONE_bf = cbf[:, 128:256]
        I_f = C(C_I)
        ONE_f = C(C_ONE)

        cfm = k.sbv([128, 8, 2], F32, "cfm")
        k.dma("sp", cfm, cfm_d)
        csl = k.sbv([128, 8, 2], F32, "csl")
        k.act("act", csl, cfm, AF.Silu)
        badafm = k.sbv([128, 48], F32, "badafm")
        k.dma("sp", badafm, badafm_d)
        g02 = k.sbv([128, 2, 8], F32, "g02")
        k.dma("sp", g02, g02_d)
        ab = k.sbv([128, 2, 16], F32, "ab")
        k.dma("sp", ab, ab_d)
        lbl = k.sbv([128, 2, 16], F32, "lbl")
        k.dma("sp", lbl, lbl_d)
        nw = k.sbv([128, 2], F32, "nw")
        k.dma("sp", nw, nw_d)
        convw = k.sbv([128, 24, 9], F32, "convw")
        k.dma("sp", convw, convw_d)

        nA = k.sbv([128, 16], F32, "nA")
        k.act("act", nA, ab[:, 0, :], AF.Exp)
        k.ts("dve", nA, nA, -1.0, ALU.mult)
        dtb = ab[:, 1, :]
        lbd = k.sbv([128, 16], F32, "lbd")
        k.tt("dve", lbd, lbl[:, 0, :], lbl[:, 1, :], ALU.subtract)
        lb = k.sbv([128, 16], F32, "lb")
        oml = k.sbv([128, 16], F32, "oml")
        k.act("act", lb, lbd, AF.Sigmoid)
        k.act("act", oml, lbd, AF.Sigmoid, scale=-1.0)
        noml = k.sbv([128, 16], F32, "noml")
        k.ts("dve", noml, oml, -1.0, ALU.mult)

        modfm = k.sbv([128, 48, 2], F32, "modfm")
        grow = k.sbv([128, 2, D], F32, "grow")
        wada_v = wada_d.rearrange("(kt p) n -> p kt n", p=128)
        pp_mod = PBIG
        with ExitStack() as es2:
            es_save = k.es
            k.es = es2
            wring = k.ring(2, [128, 8, 512], F32)
            cbc = k.sbv([128, 8, 128], F32, "cbc")
            k.cp("dve", cbc, csl[:, :, 0:1].bc([128, 8, 128]))
            bgt = k.sbv([128, 2, D], F32, "bgt")
            k.dma("sp", bgt, bgt_d)
            g13 = k.sbv([128, 2, D], F32, "g13")
            k.dma("sp", g13, g13_d)
            for ch in (2, 3, 0, 1, 6, 7, 8, 9, 4, 5, 10, 11):
                wt = wring.get()
                k.dma("sp", wt[:, 0:4, :], wada_v[:, 0:4, ch * 512:(ch + 1) * 512])
                k.dma("sp", wt[:, 4:8, :], wada_v[:, 4:8, ch * 512:(ch + 1) * 512])
                pm = pp_mod.get()
                for j in range(4):
                    ct = ch * 4 + j
                    for kt in range(8):
                        k.mm(pm[:, 2 * j:2 * j + 2], wt[:, kt, j * 128:(j + 1) * 128], csl[:, kt, :],
                             start=(kt == 0), stop=(kt == 7))
                for j in range(4):
                    ct = ch * 4 + j
                    k.ts("dve", modfm[:, ct, :], pm[:, 2 * j:2 * j + 2], badafm[:, ct:ct + 1], ALU.add)
                if ch in (4, 5, 10, 11):
                    gi = 0 if ch < 6 else 1
                    c0 = (ch % 2) * 512
                    pg = pp_mod.get()
                    for kt in range(8):
                        k.mm(pg, cbc[:, kt, :], wt[:, kt, :], start=(kt == 0), stop=(kt == 7))
                    k.tt("dve", grow[:, gi, c0:c0 + 512], pg, bgt[:, gi, c0:c0 + 512], ALU.add)
            k.tt("dve", grow[:, 0, :], grow[:, 0, :], g13[:, 0, :], ALU.mult)
            k.tt("dve", grow[:, 1, :], grow[:, 1, :], g13[:, 1, :], ALU.mult)
            k.barrier()
            k.es = es_save
        A0 = k.sbv([128, 8, 2], F32, "A0")
        A2 = k.sbv([128, 8, 2], F32, "A2")
        for c_ in range(2):
            k.stt("dve", A0[:, :, c_], modfm[:, 8:16, c_], 1.0, g02[:, 0, :], ALU.add, ALU.mult)
            k.stt("dve", A2[:, :, c_], modfm[:, 32:40, c_], 1.0, g02[:, 1, :], ALU.add, ALU.mult)
        B0 = modfm[:, 0:8, :]
        B2 = modfm[:, 24:32, :]
        dbg("modfm", modfm, [128, 48, 2])
        dbg("grow", grow, [128, 2, D])

        hxT_t = k.sb([128, 8, NTOK], BF16, "hxT")
        hx = [V(hxT_t[:, :, t * 128:(t + 1) * 128], R()) for t in range(NT)]
        epsc = k.sbv([128, 1], F32, "epsc")
        k.memset("dve", epsc, EPS)
        onec = k.sbv([128, 1], F32, "onec")
        k.memset("dve", onec, 1.0)

        def hxs(c0, n):
            ts_ = tuple(hx[t].res for t in range(c0 // 128, (c0 + n + 127) // 128))
            return lambda kt: V(hxT_t[:, kt, c0:c0 + n], ts_)

        def rsqrt(out, in_, scale):
            k.act("act", out, in_, AF.Sqrt, bias=epsc[0:out.ap.shape[0], :], scale=scale)
            k.op("dve", lambda en: en.reciprocal(out=out.ap, in_=out.ap), [out], [out])

        def rsqrt_le(out, in_, scale):
            k.act("act", out, in_, AF.Ln, bias=epsc[0:out.ap.shape[0], :], scale=scale)
            k.act("act", out, out, AF.Exp, scale=-0.5)

        def norm_T(xt, sq, st, dst, A, Bm, col, eng_flip):
            k.tt("dve", sq, xt, xt, ALU.mult)
            k.red("dve", st[:, 0:1], sq)
            rsqrt(st[:, 1:2], st[:, 0:1], 1.0 / D)
            k.act("act", sq, xt, AF.Copy, scale=st[:, 1:2])
            for half in range(2):
                p = PBIG.get()
                for j in range(4):
                    kt = half * 4 + j
                    k.tr(p[:, j * 128:(j + 1) * 128], sq[:, kt * 128:(kt + 1) * 128], I_f)
                for j in range(4):
                    kt = half * 4 + j
                    if (kt + eng_flip) % 2 == 0:
                        k.act("act", dst[:, kt, :], p[:, j * 128:(j + 1) * 128], AF.Identity,
                              bias=Bm[:, kt, col:col + 1], scale=A[:, kt, col:col + 1])
                    else:
                        k.ts("dve", dst[:, kt, :], p[:, j * 128:(j + 1) * 128], A[:, kt, col:col + 1], ALU.mult,
                             Bm[:, kt, col:col + 1], ALU.add)

        with Scope():
            xring = k.ring(3, [128, D], F32)
            sqring = k.ring(2, [128, D], F32)
            stat = k.ring(4, [128, 2], F32)
            for t in range(NT):
                xt = xring.get()
                src = ctx_d[t * 128:(t + 1) * 128, :] if t < NCTX else x_d[(t - NCTX) * 128:(t - NCTX + 1) * 128, :]
                k.dma("sp", xt, src)
                norm_T(xt, sqring.get(), stat.get(), hx[t], A0, B0, 1 if t < NCTX else 0, t)
            dbg("hxT", V(hxT_t[:, :, 0:512], hx[0].res), [128, 8, 512])

        win_v = win_d.rearrange("(kt p) n -> p kt n", p=128)
        def _phase2plus():
            ydr = nc.dram_tensor("ydr", [2, NH, 128, NOWN * 128], BF16, kind="Internal").ap()
            ydr_res = [[R() for _ in range(NH)] for _ in range(2)]

            wring = k.ring(5, [128, 8, 128], BF16)

            wst = k.ring(2, [128, 8, 128], F32)

            def load_w(col0):
                st_ = wst.get()
                k.dma("sp", st_, win_v[:, :, col0:col0 + 128])
                w = wring.get()
                k.cp("dve", w, st_)
                return w

            def project_fm(w, c0, n, dst_fn):
                hv = hxs(c0, n)
                p = PBIG.get()
                for kt in range(8):
                    k.mm(p[:, 0:n], w[:, kt, :], hv(kt), start=(kt == 0), stop=(kt == 7))
                dst_fn(p[:, 0:n])

            with Scope():
                names = ("la", "lnb", "negg", "bsc", "bg", "ed", "dl")
                sm = {n: k.sbv([128, NT, 16], F32, "sm_" + n) for n in names}
                sm["la"].res = tuple(R() for _ in range(NT))
                smres = [R() for _ in range(NT)]
                with Scope():
                    wab = k.sbv([128, 8, 32], BF16, "wab")
                    wab32 = k.sbv([128, 8, 32], F32, "wab32")
                    k.dma("sp", wab32, win_v[:, :, O_A:O_A + 32])
                    k.cp("dve", wab, wab32)
                    tmp = k.ring(6, [128, 32], F32)
                    for t in range(NT):
                        hv = hxs(t * 128, 128)
                        pab = PSM.get()
                        for kt in range(8):
                            k.mm(pab[:, 0:32], hv(kt), wab[:, kt, :], start=(kt == 0), stop=(kt == 7))
                        xx = tmp.get()
                        k.tt("dve", xx[:, 0:16], pab[:, 0:16], dtb, ALU.add)
                        k.ts("dve", xx[:, 16:32], pab[:, 16:32], -1.0, ALU.mult)
                        if t == 0:
                            dbg("d_wab", wab, [128, 8, 32])
                            dbg("d_wab32", wab32, [128, 8, 32])
                            dbg("d_hx0", hx[0], [128, 8, 128])
                            dbg("d_xx", xx, [128, 32])
                            dbg("d_nA", nA, [128, 16])
                            dbg("d_ab", ab, [128, 2, 16])
                        aa = tmp.get()
                        k.ts("dve", aa, xx, -1.0, ALU.mult)
                        k.tt("dve", aa, aa, xx, ALU.max)
                        if t == 0:
                            dbg("d_abs", aa, [128, 32])
                        k.act("act", aa, aa, AF.Exp, scale=-1.0)
                        if t == 0:
                            dbg("d_exp", aa, [128, 32])
                        k.act("act", aa, aa, AF.Ln, bias=onec)
                        if t == 0:
                            dbg("d_ln", aa, [128, 32])
                        k.ts("dve", xx, xx, 0.0, ALU.max)
                        k.tt("dve", xx, xx, aa, ALU.add)
                        sres = smres[t]

                        def S_(n):
                            return V(sm[n].ap[:, t, :], sres)
                        la_t = V(sm["la"].ap[:, t, :], sm["la"].res[t])
                        k.tt("dve", la_t, xx[:, 0:16], nA, ALU.mult)
                        k.ts("dve", S_("lnb"), xx[:, 16:32], -1.0, ALU.mult)
                        pg = PSM.get()
                        k.mm(pg[:, 0:8], C(C_UF), la_t[:, 0:8])
                        k.mm(pg[:, 8:16], C(C_UB), la_t[:, 8:16])
                        k.mm(pg[:, 16:32], ONE_f, la_t)
                        gg = tmp.get()
                        k.cp("dve", gg, pg[:, 0:32])
                        k.ts("dve", S_("negg"), gg[:, 0:16], -1.0, ALU.mult)
                        k.act("act", S_("bsc"), S_("lnb"), AF.Exp)
                        t2 = tmp.get()
                        k.tt("dve", t2[:, 0:16], S_("lnb"), gg[:, 0:16], ALU.add)
                        k.act("act", S_("bg"), t2[:, 0:16], AF.Exp)
                        k.tt("dve", t2[:, 16:32], gg[:, 16:32], gg[:, 0:16], ALU.subtract)
                        k.act("act", S_("ed"), t2[:, 16:32], AF.Exp)
                        k.act("act", S_("dl"), gg[:, 16:32], AF.Exp)

                if cutn == 1:
                    for n_ in names:
                        dbg("sm_" + n_, V(sm[n_].ap, tuple(smres) + tuple(sm["la"].res)), [128, NT, 16])
                cut(1)

                def SM(n, t, hd):
                    r = sm["la"].res[t] if n == "la" else smres[t]
                    return V(sm[n].ap[:, t, hd:hd + 1], r)

                NSLOT = 4
                t128s = [k.ring(8, [128, 128], F32) for _ in range(NSLOT)]
                b128s = [k.ring(8, [128, 128], BF16) for _ in range(NSLOT)]

                def run_units(unit_fn, arglist, nslots):
                    pending = list(enumerate(arglist))
                    active = []
                    free = list(range(nslots))
                    next_chain = 0
                    while pending or active:
                        while pending and free:
                            idx, a_ = pending.pop(0)
                            sl = free.pop(0)
                            active.append([idx, unit_fn(*a_, slot=sl), sl, "prep"])
                        for item in list(active):
                            idx, g, sl, st = item
                            if st == "wait" and idx != next_chain:
                                continue
                            try:
                                r_ = next(g)
                                if r_ == "chain":
                                    item[3] = "wait"
                            except StopIteration:
                                active.remove(item)
                                free.append(sl)
                                next_chain += 1
                with Scope():
                    pre_loc = k.sbv([128, 66, 66], BF16, "pre_loc")
                    pre_ctx = k.sbv([128, 258], BF16, "pre_ctx")
                    k.memset("pool", pre_loc, 0.0)
                    k.memset("pool", pre_ctx, 0.0)
                    dgr = k.ring(1, [128, 9, 128], BF16)
                    KT = k.sbv([128, NTOK], BF16, "KT")
                    QT = k.sbv([128, NOWN * 128], BF16, "QT")
                    ktok = k.sbv([128, NT, 128], BF16, "ktok")
                    vtok = k.sbv([128, NT, 128], BF16, "vtok")
                    gaT = k.sbv([128, NOWN * 128], BF16, "gaT")
                    oBT = k.sbv([128, NOWN * 128], BF16, "oBT")
                    yT = k.ring(1, [128, NOWN * 128], BF16)
                    f512 = k.ring(2, [128, 512], F32)
                    g512 = k.ring(2, [128, 512], F32)
                    vt512 = k.ring(2, [128, 512], BF16)
                    S32 = [k.sbv([128, 128], F32, "S32_%d" % d) for d in range(2)]
                    Sbf = [k.ring(2, [128, 128], BF16) for d in range(2)]

                    def conv_head(w, ct, which):
                        dg = dgr.get()
                        for tap in range(9):
                            k.ts("dve", dg[:, tap, :], I_bf, convw[:, ct, tap:tap + 1], ALU.mult)
                        cut(21)
                        if which != "q":
                            project_fm(w, 0, 256, lambda p: k.cp("act", pre_ctx[:, 1:257], p))
                        for g in range(8 if which != "q" else 5):
                            project_fm(w, TOK0 + 512 * g, 512,
                                       lambda p: k.cp("act", pre_loc[:, 1 + 8 * g:9 + 8 * g, 1:65],
                                                      p.re("p (a b) -> p a b", b=64)))
                        cut(22)
                        def stage1(g):
                            pc = PBIG.get()
                            if g < 0:
                                for kw in range(3):
                                    k.mm(pc[:, 0:256], dg[:, 3 + kw, :], pre_ctx[:, kw:kw + 256], start=(kw == 0), stop=(kw == 2))
                            else:
                                pc3 = pc.re("p (a b) -> p a b", b=64)
                                for tap in range(9):
                                    kh, kw = tap // 3, tap % 3
                                    k.mm(pc3, dg[:, tap, :], pre_loc[:, 8 * g + kh:8 * g + kh + 8, kw:kw + 64],
                                         start=(tap == 0), stop=(tap == 8))
                            return pc

                        def stage2(g, pc):
                            n = 256 if g < 0 else 512
                            c0 = 0 if g < 0 else TOK0 + 512 * g
                            if which == "v":
                                vt = vt512.get()
                                k.act("act", vt[:, 0:n], pc[:, 0:n], AF.Silu)
                                for j in range(n // 128):
                                    pb = PBF.get()
                                    k.tr(pb, vt[:, j * 128:(j + 1) * 128], I_bf)
                                    k.cp("dve", vtok[:, c0 // 128 + j, :], pb)
                            else:
                                qs = f512.get()
                                k.act("act", qs[:, 0:n], pc[:, 0:n], AF.Silu)
                                sq = g512.get()
                                k.tt("dve", sq[:, 0:n], qs[:, 0:n], qs[:, 0:n], ALU.mult)
                                pss = PBIG.get()
                                k.mm(pss[:, 0:n], ONE_f, sq[:, 0:n])
                                k.act("act", sq[:, 0:n], pss[:, 0:n], AF.Ln, bias=epsc)
                                k.act("act", sq[:, 0:n], sq[:, 0:n], AF.Exp, scale=-0.5)
                                if which == "k":
                                    k.tt("dve", KT[:, c0:c0 + n], qs[:, 0:n], sq[:, 0:n], ALU.mult)
                                    for j in range(n // 128):
                                        pb = PBF.get()
                                        k.tr(pb, KT[:, c0 + j * 128:c0 + (j + 1) * 128], I_bf)
                                        k.cp("dve", ktok[:, c0 // 128 + j, :], pb)
                                else:
                                    k.stt("dve", QT[:, c0 - TOK0:c0 - TOK0 + n], qs[:, 0:n], 128.0 ** -0.5, sq[:, 0:n],
                                          ALU.mult, ALU.mult)

                        prev_ = None
                        for g in (range(-1, 8) if which != "q" else range(0, 4)):
                            pc_ = stage1(g)
                            if prev_ is not None:
                                stage2(*prev_)
                            prev_ = (g, pc_)
                        stage2(*prev_)

                    def gdn_unit(h, d, t, full, own_idx, slot=0):
                        T1, B1 = t128s[slot], b128s[slot]
                        hd = d * 8 + h
                        UM = C(C_UF if d == 0 else C_UB)
                        NS = C(C_NSF if d == 0 else C_NSB)
                        NI = C(C_NIF if d == 0 else C_NIB)
                        la_bc = SM("la", t, hd).bc([128, 128])
                        lnb_bc = SM("lnb", t, hd).bc([128, 128])
                        negg = SM("negg", t, hd)
                        cs = slice(t * 128, (t + 1) * 128)
                        qs_ = slice(own_idx * 128, (own_idx + 1) * 128)
                        pA = PSM.get()
                        k.mm(pA, la_bc, UM, start=True, stop=False)
                        k.mm(pA, lnb_bc, I_f, start=False, stop=False)
                        k.mm(pA, I_f, NS, start=False, stop=True)
                        LA = T1.get()
                        k.act("act", LA, pA, AF.Exp, bias=negg)
                        yield
                        pK = PSM.get()
                        k.mm(pK, KT[:, cs], KT[:, cs])
                        Xt = T1.get()
                        k.stt("dve", Xt, pK, -1.0, LA, ALU.mult, ALU.mult)
                        yield
                        pb = PSM.get()
                        k.tr(pb, Xt, I_f)
                        Xn = T1.get()
                        k.cp("act", Xn, pb)
                        yield
                        Rk = T1.get()
                        k.tt("dve", Rk, Xt, I_f, ALU.add)
                        yield
                        for lvl in range(1, 7):
                            pn = PSM.get()
                            k.mm(pn, Xt, Xn)
                            Xn2 = T1.get()
                            k.cp("act", Xn2, pn)
                            yield
                            if lvl < 6:
                                pt_ = PSM.get()
                                k.mm(pt_, Xn, Xt)
                                Xt2 = T1.get()
                                k.cp("dve", Xt2, pt_)
                                yield
                            pr = PSM.get()
                            k.mm(pr, Xn2, Rk)
                            Rk2 = T1.get()
                            k.tt("dve", Rk2, pr, Rk, ALU.add)
                            yield
                            Xn, Rk = Xn2, Rk2
                            if lvl < 6:
                                Xt = Xt2
                        Tb = B1.get()
                        k.act("act", Tb, Rk, AF.Copy, scale=SM("bsc", t, hd))
                        yield
                        Tbg = B1.get()
                        k.act("act", Tbg, Rk, AF.Copy, scale=SM("bg", t, hd))
                        yield
                        pu = PSM.get()
                        k.mm(pu, Tb, vtok[:, t, :])
                        u32 = T1.get()
                        k.cp("act", u32, pu)
                        yield
                        pw = PSM.get()
                        k.mm(pw, ktok[:, t, :], Tbg)
                        wT = B1.get()
                        k.cp("dve", wT, pw)
                        yield
                        kdec = B1.get()
                        k.ts("dve", kdec, ktok[:, t, :], SM("ed", t, hd), ALU.mult)
                        yield
                        if full:
                            pL = PSM.get()
                            k.mm(pL, la_bc, UM, start=True, stop=False)
                            k.mm(pL, I_f, NI, start=False, stop=True)
                            Lt = T1.get()
                            k.act("act", Lt, pL, AF.Exp, bias=negg)
                            yield
                            pQ = PSM.get()
                            k.mm(pQ, KT[:, cs], QT[:, qs_])
                            attnT = B1.get()
                            k.tt("dve", attnT, pQ, Lt, ALU.mult)
                            yield
                            pE = PSM.get()
                            k.mm(pE, la_bc, UM)
                            Er = B1.get()
                            k.act("act", Er, pE, AF.Exp)
                            yield
                            qgT = B1.get()
                            k.tt("dve", qgT, QT[:, qs_], Er, ALU.mult)
                            yield
                        if h == 0 and d == 1 and t == 1:
                            dbg("LA", LA, [128, 128])
                            dbg("Rk", Rk, [128, 128])
                            dbg("Tb", Tb, [128, 128])
                            dbg("u32", u32, [128, 128])
                            dbg("wT", wT, [128, 128])
                            dbg("kdec", kdec, [128, 128])
                            dbg("smla", sm["la"][:, 1, :], [128, 16])
                            dbg("smnegg", V(sm["negg"].ap[:, 1, :], smres[1]), [128, 16])
                            dbg("smbg", V(sm["bg"].ap[:, 1, :], smres[1]), [128, 16])
                            dbg("smed", V(sm["ed"].ap[:, 1, :], smres[1]), [128, 16])
                        yield "chain"
                        Sb = Sbf[d].items[(Sbf[d].i - 1) % 2]
                        pws = PSM.get()
                        k.mm(pws, wT, Sb)
                        vnew = B1.get()
                        k.tt("dve", vnew, u32, pws, ALU.subtract)
                        yield
                        if full:
                            po = PSM.get()
                            k.mm(po, Sb, qgT, start=True, stop=False)
                            k.mm(po, vnew, attnT, start=False, stop=True)
                            oc = slice(own_idx * 128, (own_idx + 1) * 128)
                            if d == 1:
                                k.cp("act", oBT[:, oc], po)
                                yield
                            else:
                                o32 = T1.get()
                                k.tt("dve", o32, po, oBT[:, oc], ALU.add)
                                yield
                                sq = T1.get()
                                k.tt("dve", sq, o32, o32, ALU.mult)
                                yield
                                pss = PSM.get()
                                k.mm(pss, ONE_f, sq)
                                rn = T1.get()
                                rsqrt_le(rn, pss, 1.0 / 128)
                                k.tt("dve", o32, o32, rn, ALU.mult)
                                yield
                                k.stt("dve", ycur[:, oc], o32, nw[:, 0:1], gaT[:, oc], ALU.mult, ALU.mult)
                                yield
                        pS = PSM.get()
                        k.mm(pS, kdec, vnew)
                        k.stt("dve", S32[d], S32[d], SM("dl", t, hd), pS, ALU.mult, ALU.add)
                        yield
                        Sn = Sbf[d].get()
                        k.cp("dve", Sn, S32[d])
                        yield

                    for h in range(NH if stage >= 3 else 1):
                        wq, wk, wv, wga = (load_w(O_Q + h * 128), load_w(O_K + h * 128),
                                           load_w(O_V + h * 128), load_w(O_GA + h * 128))
                        conv_head(wk, 8 + h, "k")
                        if h == 0:
                            dbg("wk", wk, [128, 8, 128])
                            dbg("wst", wst.items[1], [128, 8, 128])
                            dbg("prectx", pre_ctx, [128, 258])
                            dbg("preloc", pre_loc[:, 0:4, :], [128, 4, 66])
                            dbg("dg", dgr.items[0], [128, 9, 128])
                        cut(2)
                        conv_head(wv, 16 + h, "v")
                        conv_head(wq, h, "q")
                        cut(3)
                        for g in range(4):
                            project_fm(wga, TOK0 + 512 * g, 512,
                                       lambda p: k.act("act", gaT[:, 512 * g:512 * (g + 1)], p, AF.Silu))
                        ycur = yT.get()
                        allseq = []
                        for d in (1, 0):
                            k.memset("pool", S32[d], 0.0)
                            k.memset("pool", Sbf[d].get(), 0.0)
                            seq = []
                            if d == 0:
                                seq += [(t, False, -1) for t in range(NCTX)]
                                seq += [(NCTX + t, True, t) for t in range(NOWN)]
                            else:
                                seq += [(t, False, -1) for t in reversed(range(NCTX))]
                                seq += [(NCTX + t, False, -1) for t in reversed(range(NOWN, NLOC))]
                                seq += [(NCTX + t, True, t) for t in reversed(range(NOWN))]
                            allseq += [(h, d, t, full, oi) for (t, full, oi) in seq]
                        run_units(gdn_unit, allseq, 4)
                        if h == 0:
                            dbg("KT", KT[:, 0:1024], [128, 1024])
                            dbg("vtok", vtok[:, 0:4, :], [128, 4, 128])
                            dbg("ya0", ycur, [128, NOWN * 128])
                            dbg("S32", S32[0], [128, 128])
                        k.dma("sp", V(ydr[0, h], ydr_res[0][h]), ycur)
                cut(7)

                with Scope():
                    vtokH = k.sbv([128, NT, 128], BF16, "vtokH")
                    qbT = k.sbv([128, NOWN * 128], BF16, "qbT")
                    gbT = k.sbv([128, NOWN * 128], BF16, "gbT")
                    oBT = k.sbv([128, NOWN * 128], BF16, "oBTh")
                    yT = k.ring(2, [128, NOWN * 128], BF16)
                    S32 = [k.sbv([128, 128], F32, "HS32_%d" % d) for d in range(2)]
                    Sbf = [k.ring(2, [128, 128], BF16) for d in range(2)]
                    vmall = k.sbv([128, NT, 4, 128], BF16, "vmall")
                    ebrs = [k.ring(1, [128, 4], F32) for _ in range(3)]
                    RST = C(C_RST)
                    PM = C(C_SUB)

                    def hgrn_unit(h, d, t, full, own_idx, wf, slot=0):
                        T1, B1 = t128s[slot], b128s[slot]
                        hd = d * 8 + h
                        HM = C(C_HMF if d == 0 else C_HMB)
                        lbc, omlc, nomlc = lb[:, hd:hd + 1], oml[:, hd:hd + 1], noml[:, hd:hd + 1]
                        hv = hxs(t * 128, 128)
                        pf = PSM.get()
                        for kt in range(8):
                            k.mm(pf, wf[:, kt, :], hv(kt), start=(kt == 0), stop=(kt == 7))
                        sig = T1.get()
                        k.act("act", sig, pf, AF.Exp, scale=-1.0)
                        yield
                        k.ts("dve", sig, sig, 1.0, ALU.add)
                        yield
                        k.op("dve", lambda en: en.reciprocal(out=sig.ap, in_=sig.ap), [sig], [sig])
                        yield
                        lf = T1.get()
                        k.act("act", lf, sig, AF.Ln, bias=lbc, scale=omlc)
                        yield
                        kb = T1.get()
                        k.ts("dve", kb, sig, nomlc, ALU.mult, omlc, ALU.add)
                        yield
                        bb = T1.get()
                        k.scan(bb, RST, lf, 0.0, ALU.mult, ALU.add)
                        yield
                        if d == 1:
                            b2 = T1.get()
                            k.stt("dve", b2, bb, -1.0, lf, ALU.mult, ALU.add)
                            yield
                            for c in range(4):
                                k.ts("dve", b2[:, 32 * c:32 * c + 32], b2[:, 32 * c:32 * c + 32],
                                     bb[:, 32 * c + 31:32 * c + 32], ALU.add)
                            bb = b2
                        lastcol = [(32 * c + 31) if d == 0 else (32 * c) for c in range(4)]
                        Dm = T1.get()
                        for c in range(4):
                            k.ts("dve", Dm[:, 32 * c:32 * c + 32], bb[:, 32 * c:32 * c + 32],
                                 bb[:, 32 * c + 15:32 * c + 16], ALU.subtract)
                        EK = T1.get()
                        k.act("act", EK, Dm, AF.Exp, scale=-1.0)
                        yield
                        Kp = B1.get()
                        k.tt("dve", Kp, kb, EK, ALU.mult)
                        yield
                        ED = T1.get()
                        eb = ebrs[slot].get()
                        for c in range(4):
                            lc = lastcol[c]
                            k.act("act", ED[:, 32 * c:32 * c + 32], bb[:, 32 * c:32 * c + 32], AF.Exp,
                                  bias=bb[:, lc:lc + 1], scale=-1.0)
                            yield
                        k.act("act", eb, bb[:, lastcol[0]:128:32], AF.Exp)
                        yield
                        kdT = B1.get()
                        k.tt("dve", kdT, kb, ED, ALU.mult)
                        yield
                        pb = PBF.get()
                        k.tr(pb, kdT, I_bf)
                        kdtok = B1.get()
                        k.cp("dve", kdtok, pb)
                        yield
                        pG = PBIG.get()
                        for c in range(4):
                            k.mm(pG[:, 128 * c:128 * c + 128], kdtok, vmall[:, t, c, :])
                        if full:
                            oc = slice(own_idx * 128, (own_idx + 1) * 128)
                            EQ = T1.get()
                            k.act("act", EQ, Dm, AF.Exp)
                            yield
                            Qp = B1.get()
                            k.tt("dve", Qp, qbT[:, oc], EQ, ALU.mult)
                            yield
                            pa = PSM.get()
                            k.mm(pa, Kp, Qp)
                            aT = B1.get()
                            k.tt("dve", aT, pa, HM, ALU.mult)
                            yield
                            EB = T1.get()
                            k.act("act", EB, bb, AF.Exp)
                            yield
                            Qb = B1.get()
                            k.tt("dve", Qb, qbT[:, oc], EB, ALU.mult)
                            yield
                            po = PSM.get()
                        yield "chain"
                        for c in (range(4) if d == 0 else reversed(range(4))):
                            Sb = Sbf[d].items[(Sbf[d].i - 1) % 2]
                            cs = slice(32 * c, 32 * c + 32)
                            if full:
                                k.mm(po[:, cs], Sb, Qb[:, cs], start=True, stop=False)
                                k.mm(po[:, cs], vtokH[:, t, :], aT[:, cs], start=False, stop=True)
                            k.stt("dve", S32[d], S32[d], eb[:, c:c + 1], pG[:, 128 * c:128 * c + 128], ALU.mult, ALU.add)
                            yield
                            Sn = Sbf[d].get()
                            k.cp("act", Sn, S32[d])
                            yield
                        if full:
                            if d == 1:
                                k.cp("act", oBT[:, oc], po)
                                yield
                            else:
                                o32 = T1.get()
                                k.tt("dve", o32, po, oBT[:, oc], ALU.add)
                                yield
                                sq = T1.get()
                                k.tt("dve", sq, o32, o32, ALU.mult)
                                yield
                                pss = PSM.get()
                                k.mm(pss, ONE_f, sq)
                                rn = T1.get()
                                rsqrt_le(rn, pss, 1.0 / 128)
                                k.tt("dve", o32, o32, rn, ALU.mult)
                                yield
                                k.stt("dve", ycurH[:, oc], o32, nw[:, 1:2], gbT[:, oc], ALU.mult, ALU.mult)
                                yield

                    for h in range(NH if stage >= 3 else 1):
                        wfF, wfB, wqb, wib, wgb = (load_w(O_F + h * 128), load_w(O_F + 1024 + h * 128),
                                                   load_w(O_QB + h * 128), load_w(O_IB + h * 128),
                                                   load_w(O_GB + h * 128))
                        for t in range(NT):
                            hv = hxs(t * 128, 128)
                            pv = PSM.get()
                            for kt in range(8):
                                k.mm(pv, hv(kt), wib[:, kt, :], start=(kt == 0), stop=(kt == 7))
                            k.cp("act", vtokH[:, t, :], pv)
                            for c in range(4):
                                k.ts("dve", vmall[:, t, c, :], vtokH[:, t, :], PM[:, c:c + 1], ALU.mult)
                        for g in range(4):
                            project_fm(wqb, TOK0 + 512 * g, 512,
                                       lambda p: k.act("act", qbT[:, 512 * g:512 * (g + 1)], p, AF.Silu))
                            project_fm(wgb, TOK0 + 512 * g, 512,
                                       lambda p: k.act("act", gbT[:, 512 * g:512 * (g + 1)], p, AF.Silu))
                        k.ts("dve", qbT, qbT, 128.0 ** -0.5, ALU.mult)
                        ycurH = yT.get()
                        allseq = []
                        for d in (1, 0):
                            k.memset("dve", S32[d], 0.0)
                            k.memset("dve", Sbf[d].get(), 0.0)
                            seq = []
                            if d == 0:
                                seq += [(t, False, -1) for t in range(NCTX)]
                                seq += [(NCTX + t, True, t) for t in range(NOWN)]
                            else:
                                seq += [(t, False, -1) for t in reversed(range(NCTX))]
                                seq += [(NCTX + t, False, -1) for t in reversed(range(NOWN, NLOC))]
                                seq += [(NCTX + t, True, t) for t in reversed(range(NOWN))]
                            allseq += [(h, d, t, full, oi, wfF if d == 0 else wfB) for (t, full, oi) in seq]
                        run_units(hgrn_unit, allseq, 3)
                        if h == 0:
                            dbg("yb0", ycurH, [128, NOWN * 128])
                            dbg("HS", S32[0], [128, 128])
                        k.dma("sp", V(ydr[1, h], ydr_res[1][h]), ycurH)
            cut(8)

            yres = [R() for _ in range(NOWN)]
            h2 = [V(hxT_t[:, :, TOK0 + (NOWN + t) * 128:TOK0 + (NOWN + t + 1) * 128], hx[NCTX + NOWN + t].res)
                  for t in range(NOWN)]

            def load_cols(dst, src_d, ncol_tiles):
                sv = src_d.rearrange("(kt p) n -> p kt n", p=128)
                for ct in range(ncol_tiles):
                    st_ = wst.get()
                    k.dma("sp", st_, sv[:, :, ct * 128:(ct + 1) * 128])
                    k.cp("dve", dst[:, :, ct * 128:(ct + 1) * 128], st_)

            with Scope():
                Wa = k.sbv([128, 8, D], BF16, "Wa")
                Wb = k.sbv([128, 8, D], BF16, "Wb")
                Wo = k.sbv([128, 8, D], BF16, "Wo")
                load_cols(Wa, wa_d, 8)
                load_cols(Wb, wb_d, 8)
                load_cols(Wo, wo_d, 8)
                yA = k.sbv([128, NH, 512], BF16, "yA")
                yB = k.sbv([128, NH, 512], BF16, "yB")
                mgd = k.sbv([128, 8, 512], BF16, "mgd")
                xr = k.ring(1, [128, D], F32)
                o2r = k.ring(1, [128, D], F32)
                sqr = k.ring(1, [128, D], F32)
                str_ = k.ring(4, [128, 2], F32)
                sg = k.ring(1, [128, 512], F32)
                m1r = k.ring(2, [128, 512], F32)
                for G in range(4):
                    gc = slice(512 * G, 512 * (G + 1))
                    for h in range(NH):
                        k.dma("sp", yA[:, h, :], V(ydr[0, h][:, gc], ydr_res[0][h]))
                        k.dma("sp", yB[:, h, :], V(ydr[1, h][:, gc], ydr_res[1][h]))
                    hvG = hxs(TOK0 + 512 * G, 512)
                    for dt in range(8):
                        wga_ = load_w(O_MG + dt * 128)
                        wgb_ = load_w(O_MG + 1024 + dt * 128)
                        dc = slice(dt * 128, (dt + 1) * 128)
                        pga = PBIG.get()
                        for kt in range(8):
                            k.mm(pga, wga_[:, kt, :], hvG(kt), start=(kt == 0), stop=(kt == 7))
                        sga = sg.get()
                        k.act("act", sga, pga, AF.Sigmoid)
                        pa_ = PBIG.get()
                        for h in range(NH):
                            k.mm(pa_, Wa[:, h, dc], yA[:, h, :], start=(h == 0), stop=(h == NH - 1))
                        m1 = m1r.get()
                        k.tt("dve", m1, pa_, sga, ALU.mult)
                        pgb = PBIG.get()
                        for kt in range(8):
                            k.mm(pgb, wgb_[:, kt, :], hvG(kt), start=(kt == 0), stop=(kt == 7))
                        sgb = sg.get()
                        k.act("act", sgb, pgb, AF.Sigmoid)
                        pb_ = PBIG.get()
                        for h in range(NH):
                            k.mm(pb_, Wb[:, h, dc], yB[:, h, :], start=(h == 0), stop=(h == NH - 1))
                        m2 = m1r.get()
                        k.tt("dve", m2, pb_, sgb, ALU.mult)
                        k.tt("dve", mgd[:, dt, :], m1, m2, ALU.add)
                    if G == 0:
                        dbg("yA", yA, [128, NH, 512])
                        dbg("yB", yB, [128, NH, 512])
                        dbg("mgd", mgd, [128, 8, 512])
                    for j in range(4):
                        t = 4 * G + j
                        o2 = o2r.get()
                        for half in range(2):
                            po_ = PBIG.get()
                            for dt in range(8):
                                k.mm(po_, mgd[:, dt, j * 128:(j + 1) * 128], Wo[:, dt, half * 512:(half + 1) * 512],
                                     start=(dt == 0), stop=(dt == 7))
                            k.cp("act", o2[:, half * 512:(half + 1) * 512], po_)
                        sq = sqr.get()
                        st = str_.get()
                        if t == 0:
                            dbg("o2raw", o2, [128, D])
                        k.tt("dve", sq, o2, o2, ALU.mult)
                        k.red("dve", st[:, 0:1], sq)
                        rsqrt(st[:, 1:2], st[:, 0:1], 1.0 / D)
                        if t == 0:
                            dbg("o2st", st, [128, 2])
                        k.act("act", o2, o2, AF.Copy, scale=st[:, 1:2])
                        k.tt("dve", o2, o2, grow[:, 0, :], ALU.mult)
                        if t == 0:
                            dbg("o2sc", o2, [128, D])
                        xt = xr.get()
                        k.dma("sp", xt, x_d[t * 128:(t + 1) * 128, :])
                        k.tt("dve", xt, xt, o2, ALU.add)
                        k.dma("sp", V(y_d[t * 128:(t + 1) * 128, :], yres[t]), xt)
                        if t == 0:
                            dbg("x1t0", xt, [128, D])
                        norm_T(xt, sqr.get(), str_.get(), h2[t], A2, B2, 0, t)
            cut(9)

            w1_v = w1_d.rearrange("(kt p) n -> p kt n", p=128)
            w2_v = w2_d.rearrange("(ft p) n -> p ft n", p=128)
            with Scope():
                hid = k.sbv([128, 32, 512], BF16, "hid")
                w2r = k.ring(3, [128, 512], BF16)
                w2s = k.ring(3, [128, 512], F32)
                rl = k.ring(2, [128, 512], F32)
                xr = k.ring(1, [128, D], F32)
                o3r = k.ring(4, [128, D], F32)
                sqr = k.ring(1, [128, D], F32)
                str_ = k.ring(4, [128, 2], F32)
                PACC = [V(b.ap, b.res, True) for b in _psm]
                for G in range(4):
                    hvG = lambda kt: V(hxT_t[:, kt, TOK0 + (NOWN + 4 * G) * 128:TOK0 + (NOWN + 4 * G + 4) * 128],
                                       tuple(h2[4 * G + j].res for j in range(4)))
                    for ft in range(32):
                        st_ = wst.get()
                        k.dma("sp", st_, w1_v[:, :, ft * 128:(ft + 1) * 128])
                        w1 = wring.get()
                        k.cp("dve", w1, st_)
                        ph = PBIG.get()
                        for kt in range(8):
                            k.mm(ph, w1[:, kt, :], hvG(kt), start=(kt == 0), stop=(kt == 7))
                        r_ = rl.get()
                        k.act("act", r_, ph, AF.Relu)
                        k.tt("dve", hid[:, ft, :], r_, r_, ALU.mult)
                    o3s = [o3r.get() for _ in range(4)]
                    for half in range(2):
                        for ft in range(32):
                            s2 = w2s.get()
                            k.dma("sp", s2, w2_v[:, ft, half * 512:(half + 1) * 512])
                            w2 = w2r.get()
                            k.cp("dve", w2, s2)
                            for j in range(4):
                                k.mm(PACC[j], hid[:, ft, j * 128:(j + 1) * 128], w2, start=(ft == 0), stop=(ft == 31))
                        for j in range(4):
                            k.cp("act", o3s[j][:, half * 512:(half + 1) * 512], PACC[j])
                    for j in range(4):
                        t = 4 * G + j
                        o3 = o3s[j]
                        sq = sqr.get()
                        st = str_.get()
                        k.tt("dve", sq, o3, o3, ALU.mult)
                        k.red("dve", st[:, 0:1], sq)
                        rsqrt(st[:, 1:2], st[:, 0:1], 1.0 / D)
                        k.act("act", o3, o3, AF.Copy, scale=st[:, 1:2])
                        k.tt("dve", o3, o3, grow[:, 1, :], ALU.mult)
                        xt = xr.get()
                        yv = V(y_d[t * 128:(t + 1) * 128, :], yres[t])
                        k.dma("sp", xt, yv)
                        k.tt("dve", xt, xt, o3, ALU.add)
                        k.dma("sp", yv, xt, is_out=True)
            k.finish()
            raise Done()


        try:
            _phase2plus()
            raise Cut()
        except Done:
            return nc, dbg_d
        except Cut:
            z = V(hxT_t[:, 0, 0:2048].bitcast(F32), tuple(h_.res for h_ in hx))
            k.memset("dve", z, 0.0)
            for t in range(NOWN):
                k.dma("sp", y_d[t * 128:(t + 1) * 128, :], z, is_out=True)
        k.finish()
    return nc, dbg_d


def host_prep(inputs):
    f = np.float32
    x = np.asarray(inputs["x"], f)
    c = np.asarray(inputs["c"], f)
    ctx = np.asarray(inputs["ctx"], f)
    c_ctx = np.asarray(inputs["c_ctx"], f)
    w_ada = np.ascontiguousarray(np.asarray(inputs["w_ada"], f)[0])
    b_ada = np.asarray(inputs["b_ada"], f)[0]
    norm_g = np.asarray(inputs["norm_g"], f)[0]
    w_in = np.asarray(inputs["w_in"], f)[0]
    conv_w = np.asarray(inputs["conv_w"], f)[0]
    a_log = np.asarray(inputs["gdn_a_log"], f)[0]
    dt_b = np.asarray(inputs["gdn_dt_bias"], f)[0]
    gnw = np.asarray(inputs["gdn_norm_w"], f)[0]
    lbl = np.asarray(inputs["hgrn_lb_logits"], f)
    hnw = np.asarray(inputs["hgrn_norm_w"], f)[0]

    i = np.arange(128)
    consts = np.zeros((NCONST, 128, 128), f)
    consts[C_I] = np.eye(128)
    consts[C_ONE] = 1.0
    consts[C_UF] = (i[:, None] <= i[None, :])
    consts[C_UB] = (i[:, None] >= i[None, :])
    consts[C_NSF] = np.where(i[None, :] > i[:, None], 0.0, NEG)
    consts[C_NSB] = np.where(i[None, :] < i[:, None], 0.0, NEG)
    consts[C_NIF] = np.where(i[None, :] >= i[:, None], 0.0, NEG)
    consts[C_NIB] = np.where(i[None, :] <= i[:, None], 0.0, NEG)
    same = (i[:, None] // 32) == (i[None, :] // 32)
    consts[C_HMF] = same & (i[:, None] <= i[None, :])
    consts[C_HMB] = same & (i[:, None] >= i[None, :])
    consts[C_RST] = np.broadcast_to((i % 32 != 0).astype(f)[None, :], (128, 128))
    consts[C_SUB][:, 0:4] = ((i[:, None] // 32) == np.arange(4)[None, :])
    consts = np.ascontiguousarray(consts.transpose(1, 0, 2).reshape(128, NCONST * 128))

    bada_fm = np.ascontiguousarray(b_ada.reshape(48, 128).T)
    bgt = np.ascontiguousarray(np.broadcast_to(
        np.stack([b_ada[2 * D:3 * D], b_ada[5 * D:6 * D]])[None], (128, 2, D)))
    g02 = np.ascontiguousarray(np.stack([norm_g[0].reshape(8, 128).T, norm_g[2].reshape(8, 128).T], axis=1))
    g13 = np.ascontiguousarray(np.broadcast_to(np.stack([norm_g[1], norm_g[3]])[None], (128, 2, D)))
    nwv = np.ascontiguousarray(np.stack([gnw, hnw], axis=1))

    def perm_cols(flip):
        cols = []
        cols.append(np.arange(0, 4096))
        for base in (4096, 4112):
            idx = base + np.arange(16).reshape(2, 8)
            cols.append((idx[::-1] if flip else idx).reshape(-1))
        idx = 4128 + np.arange(2048).reshape(2, 1024)
        cols.append((idx[::-1] if flip else idx).reshape(-1))
        cols.append(np.arange(6176, IN_W))
        return np.concatenate(cols)

    maps = []
    shared = dict(w_ada=w_ada, bada_fm=bada_fm, bgt=bgt, g02=g02, g13=g13, nw=nwv,
                  w_a=np.ascontiguousarray(np.asarray(inputs["w_branch_a"], f)[0]),
                  w_b=np.ascontiguousarray(np.asarray(inputs["w_branch_b"], f)[0]),
                  w_o=np.ascontiguousarray(np.asarray(inputs["w_out"], f)[0]),
                  w_1=np.ascontiguousarray(np.asarray(inputs["w_mlp_in"], f)[0]),
                  w_2=np.ascontiguousarray(np.asarray(inputs["w_mlp_out"], f)[0]),
                  consts=consts)
    per_flip = {}
    for flip in (0, 1):
        win_l = np.ascontiguousarray(w_in[:, perm_cols(flip)])
        cw = conv_w[::-1, ::-1, :] if flip else conv_w
        convw = np.ascontiguousarray(cw.reshape(9, 24, 128).transpose(2, 1, 0))
        al = a_log[::-1] if flip else a_log
        db = dt_b[::-1] if flip else dt_b
        abv = np.ascontiguousarray(np.broadcast_to(np.stack([al.reshape(-1), db.reshape(-1)])[None], (128, 2, 16)))
        ll = lbl[:, ::-1, :] if flip else lbl
        lblv = np.ascontiguousarray(ll.reshape(2, 2, 8, 128).transpose(3, 0, 1, 2).reshape(128, 2, 16))
        per_flip[flip] = dict(w_in=win_l, convw=convw, ab=abv, lbl=lblv)
    for core in range(8):
        b, s = core // 2, core % 2
        xb = x[b][::-1] if s else x[b]
        cb = ctx[b][::-1] if s else ctx[b]
        cfm = np.ascontiguousarray(np.stack([c[b], c_ctx], axis=1).reshape(8, 128, 2).transpose(1, 0, 2))
        m = dict(shared)
        m.update(per_flip[s])
        m.update(x=np.ascontiguousarray(xb), ctx=np.ascontiguousarray(cb), cfm=cfm)
        maps.append(m)
    return maps


_NC_CACHE = {}


def kernel(**inputs):
    maps = host_prep(inputs)
    if "nc" not in _NC_CACHE:
        import os
        _NC_CACHE["nc"] = build(cutn=int(os.environ.get("KCUT", "0")))[0]
    nc = _NC_CACHE["nc"]
    res = run_bass_kernel_spmd(nc, maps, core_ids=list(range(8)))
    out = np.zeros((4, 4096, D), np.float32)
    for core in range(8):
        b, s = core // 2, core % 2
        y = np.asarray(res.results[core]["y"])
        if s:
            out[b, 2048:] = y[::-1]
        else:
            out[b, :2048] = y
    return out
```

```python
import numpy as np
from contextlib import ExitStack
import concourse.bass as bass
import concourse.mybir as mybir
from concourse.bass_utils import run_bass_kernel_spmd

F32 = mybir.dt.float32
BF16 = mybir.dt.bfloat16
AF = mybir.ActivationFunctionType
ALU = mybir.AluOpType
AX = mybir.AxisListType

D = 1024
NH = 8
EPS = 1e-6
NOWN = 16
NOTH = 16
NLOC = NOWN + NOTH
NCTX = 2
NT = NCTX + NLOC
TOK0 = NCTX * 128
NTOK = NT * 128
NEG = -30000.0
IN_W = 11296
O_Q, O_K, O_V, O_GA, O_A, O_B, O_F, O_QB, O_IB, O_GB, O_MG = (
    0, 1024, 2048, 3072, 4096, 4112, 4128, 6176, 7200, 8224, 9248)

C_I, C_ONE, C_UF, C_UB, C_NSF, C_NSB, C_NIF, C_NIB, C_HMF, C_HMB, C_RST, C_SUB = range(12)
NCONST = 12


class R:
    __slots__ = ("w", "r")

    def __init__(self):
        self.w = None
        self.r = {}


class V:
    def __init__(self, ap, res, psum=False):
        self.ap = ap
        self.res = res
        self.psum = psum

    def __getitem__(self, k):
        return V(self.ap[k], self.res, self.psum)

    def bc(self, shape):
        return V(self.ap.to_broadcast(list(shape)), self.res, self.psum)

    def re(self, s, **kw):
        return V(self.ap.rearrange(s, **kw), self.res, self.psum)


def _flat(vs):
    out = []
    for v in vs:
        if isinstance(v, V):
            if isinstance(v.res, tuple):
                out.extend(v.res)
            else:
                out.append(v.res)
    return out


class KB:
    def __init__(self, nc, es):
        self.nc = nc
        self.es = es
        self.eng = {"pe": nc.tensor, "act": nc.scalar, "dve": nc.vector,
                    "pool": nc.gpsimd, "sp": nc.sync}
        self.sem = {n: es.enter_context(nc.semaphore("s_" + n)) for n in self.eng}
        self.cnt = {n: 0 for n in self.eng}
        self.known = {n: {} for n in self.eng}
        self.nslots = {"sp": 24, "pool": 12, "act": 4}
        self.dsem = {}
        self.dval = {}
        self.dnext = {q: 0 for q in self.nslots}
        for q, n in self.nslots.items():
            for i in range(n):
                key = ("d", q, i)
                self.dsem[key] = es.enter_context(nc.semaphore("d_%s_%d" % (q, i)))
                self.dval[key] = 0
        self.out_toks = []
        self.uid = 0
        self.ninst = 0

    def sb(self, shape, dt=F32, name=None):
        self.uid += 1
        t = self.es.enter_context(self.nc.sbuf_tensor("sb_" + (name or ("t%d" % self.uid)), list(shape), dt))
        return t

    def sbv(self, shape, dt=F32, name=None):
        t = self.sb(shape, dt, name)
        return V(t[tuple(slice(None) for _ in shape)], R())

    def ps(self, shape, dt=F32, name=None):
        self.uid += 1
        t = self.es.enter_context(self.nc.psum_tensor("ps_" + (name or ("p%d" % self.uid)), list(shape), dt))
        return V(t[tuple(slice(None) for _ in shape)], R(), True)

    def ring(self, n, shape, dt=F32, psum=False):
        items = [(self.ps(shape, dt) if psum else self.sbv(shape, dt)) for _ in range(n)]
        return Ring(items)

    def _semof(self, key):
        return self.sem[key] if isinstance(key, str) else self.dsem[key]

    def _wait(self, e, toks):
        kn = self.known[e]
        for tok in toks:
            if tok is None:
                continue
            key, val, snap = tok
            if kn.get(key, 0) >= val:
                continue
            self.eng[e].wait_ge(self._semof(key), val)
            self.ninst += 1
            kn[key] = val
            for k2, v2 in snap.items():
                if kn.get(k2, 0) < v2:
                    kn[k2] = v2

    @staticmethod
    def _toks(rd, wr):
        toks = []
        for r in rd:
            toks.append(r.w)
        for r in wr:
            toks.append(r.w)
            toks.extend(r.r.values())
        return toks

    @staticmethod
    def _commit(tok, rd, wr):
        for r in rd:
            r.r[tok[0]] = tok
        for r in wr:
            r.w = tok
            r.r = {}

    def op(self, e, fn, rd, wr):
        wr = list(wr) + [v for v in rd if isinstance(v, V) and v.psum]
        rd = _flat(rd)
        wr = _flat(wr)
        self._wait(e, self._toks(rd, wr))
        inst = fn(self.eng[e])
        self.cnt[e] += 1
        self.ninst += 1
        inst.then_inc(self.sem[e], 1)
        tok = (e, self.cnt[e], dict(self.known[e]))
        self._commit(tok, rd, wr)

    def dma(self, q, out, in_, is_out=False):
        rd = _flat([in_])
        wr = _flat([out])
        i = self.dnext[q]
        self.dnext[q] = (i + 1) % self.nslots[q]
        key = ("d", q, i)
        prev = self.dval[key]
        toks = self._toks(rd, wr)
        if prev > 0:
            toks.append((key, prev, {}))
        self._wait(q, toks)
        oap = out.ap if isinstance(out, V) else out
        iap = in_.ap if isinstance(in_, V) else in_
        inst = self.eng[q].dma_start(out=oap, in_=iap)
        self.ninst += 1
        val = prev + 16
        inst.then_inc(self.dsem[key], 16)
        self.dval[key] = val
        tok = (key, val, dict(self.known[q]))
        self._commit(tok, rd, wr)
        if is_out:
            self.out_toks.append(tok)

    def barrier(self):
        toks = [(e, c, {}) for e, c in self.cnt.items() if c > 0]
        toks += [(k_, v, {}) for k_, v in self.dval.items() if v > 0]
        for e in self.eng:
            self._wait(e, toks)

    def finish(self):
        self._wait("sp", self.out_toks)
        toks = [(k, v, {}) for k, v in self.dval.items() if v > 0]
        self._wait("sp", toks)

    def mm(self, out, lhsT, rhs, start=True, stop=True):
        self.op("pe", lambda e: e.matmul(out.ap, lhsT=lhsT.ap, rhs=rhs.ap, start=start, stop=stop),
                [lhsT, rhs] + ([] if start else [out]), [out])

    def tr(self, out, in_, ident):
        self.op("pe", lambda e: e.transpose(out.ap, in_.ap, ident.ap), [in_, ident], [out])

    def act(self, e, out, in_, func, bias=None, scale=1.0):
        kw = {}
        if bias is not None:
            kw["bias"] = bias.ap if isinstance(bias, V) else bias
        kw["scale"] = scale.ap if isinstance(scale, V) else scale
        assert e == "act"
        self.op("act", lambda en: en.activation(out=out.ap, in_=in_.ap, func=func, **kw),
                [in_, bias, scale], [out])

    def tt(self, e, out, a, b, op):
        self.op(e, lambda en: en.tensor_tensor(out=out.ap, in0=a.ap, in1=b.ap, op=op), [a, b], [out])

    def ts(self, e, out, a, s1, op0, s2=None, op1=None):
        s1a = s1.ap if isinstance(s1, V) else s1
        s2a = s2.ap if isinstance(s2, V) else s2
        if op1 is None:
            fn = lambda en: en.tensor_scalar(out=out.ap, in0=a.ap, scalar1=s1a, scalar2=None, op0=op0)
        else:
            fn = lambda en: en.tensor_scalar(out=out.ap, in0=a.ap, scalar1=s1a, scalar2=s2a, op0=op0, op1=op1)
        self.op(e, fn, [a, s1, s2], [out])

    def stt(self, e, out, a, s, b, op0, op1):
        sa = s.ap if isinstance(s, V) else s
        self.op(e, lambda en: en.scalar_tensor_tensor(out=out.ap, in0=a.ap, scalar=sa, in1=b.ap, op0=op0, op1=op1),
                [a, s, b], [out])

    def cp(self, e, out, in_):
        if e == "act":
            self.op("act", lambda en: en.copy(out=out.ap, in_=in_.ap), [in_], [out])
        else:
            self.op(e, lambda en: en.tensor_copy(out=out.ap, in_=in_.ap), [in_], [out])

    def memset(self, e, out, val):
        self.op(e, lambda en: en.memset(out.ap, val), [], [out])

    def red(self, e, out, in_, op=ALU.add):
        self.op(e, lambda en: en.tensor_reduce(out=out.ap, in_=in_.ap, axis=AX.X, op=op), [in_], [out])

    def scan(self, out, d0, d1, init, op0, op1):
        self.op("dve", lambda en: en.tensor_tensor_scan(out=out.ap, data0=d0.ap, data1=d1.ap, initial=init,
                                                        op0=op0, op1=op1), [d0, d1], [out])


class Ring:
    def __init__(self, items):
        self.items = items
        self.i = 0

    def get(self):
        it = self.items[self.i % len(self.items)]
        self.i += 1
        return it


class Cut(Exception):
    pass


class Done(Exception):
    pass


def build(stage=99, dbg_names=(), cutn=0):
    nc = bass.Bass("TRN2", target_bir_lowering=False)

    def cut(n):
        if cutn == n:
            raise Cut()

    def din(name, shape, dt=F32):
        return nc.dram_tensor(name, list(shape), dt, kind="ExternalInput").ap()

    x_d = din("x", [NLOC * 128, D])
    ctx_d = din("ctx", [NCTX * 128, D])
    cfm_d = din("cfm", [128, 8, 2])
    wada_d = din("w_ada", [D, 6 * D])
    badafm_d = din("bada_fm", [128, 48])
    bgt_d = din("bgt", [128, 2, D])
    g02_d = din("g02", [128, 2, 8])
    g13_d = din("g13", [128, 2, D])
    win_d = din("w_in", [D, IN_W])
    convw_d = din("convw", [128, 24, 9])
    ab_d = din("ab", [128, 2, 16])
    lbl_d = din("lbl", [128, 2, 16])
    nw_d = din("nw", [128, 2])
    wa_d = din("w_a", [D, D])
    wb_d = din("w_b", [D, D])
    wo_d = din("w_o", [D, D])
    w1_d = din("w_1", [D, 4 * D])
    w2_d = din("w_2", [4 * D, D])
    cst_d = din("consts", [128, NCONST * 128])
    y_d = nc.dram_tensor("y", [NOWN * 128, D], F32, kind="ExternalOutput").ap()
    dbg_d = {}

    es = ExitStack()
    with es:
        k = KB(nc, es)

        def dbg(name, v, shape):
            if name in dbg_names:
                d = nc.dram_tensor("dbg_" + name, list(shape), v.ap.dtype, kind="ExternalOutput").ap()
                dbg_d[name] = d
                k.dma("sp", d, v, is_out=True)

        PBIG = k.ring(3, [128, 512], F32, psum=True)
        _psm = [k.ps([128, 512], F32) for _ in range(4)]
        PSM = Ring([V(b.ap[:, j * 128:(j + 1) * 128], b.res, True) for j in range(4) for b in _psm])
        _pbf = k.ps([128, 512], BF16)
        PBF = Ring([V(_pbf.ap[:, j * 128:(j + 1) * 128], _pbf.res, True) for j in range(4)])

        class Scope:
            def __enter__(s_):
                s_.es = ExitStack()
                s_.es.__enter__()
                s_.save = k.es
                k.es = s_.es
                return s_

            def __exit__(s_, *a):
                k.barrier()
                k.es = s_.save
                s_.es.__exit__(None, None, None)
                return False

        cst = k.sbv([128, NCONST * 128], F32, "cst")
        k.dma("sp", cst, cst_d)

        def C(i):
            return cst[:, i * 128:(i + 1) * 128]

        cbf = k.sbv([128, 2 * 128], BF16, "cbf")
        k.cp("dve", cbf, cst[:, 0:256])
        I_bf = cbf[:, 0:128]
        ONE_bf = cbf[:, 128:256]
        I_f = C(C_I)
        ONE_f = C(C_ONE)

        cfm = k.sbv([128, 8, 2], F32, "cfm")
        k.dma("sp", cfm, cfm_d)
        csl = k.sbv([128, 8, 2], F32, "csl")
        k.act("act", csl, cfm, AF.Silu)
        badafm = k.sbv([128, 48], F32, "badafm")
        k.dma("sp", badafm, badafm_d)
        g02 = k.sbv([128, 2, 8], F32, "g02")
        k.dma("sp", g02, g02_d)
        ab = k.sbv([128, 2, 16], F32, "ab")
        k.dma("sp", ab, ab_d)
        lbl = k.sbv([128, 2, 16], F32, "lbl")
        k.dma("sp", lbl, lbl_d)
        nw = k.sbv([128, 2], F32, "nw")
        k.dma("sp", nw, nw_d)
        convw = k.sbv([128, 24, 9], F32, "convw")
        k.dma("sp", convw, convw_d)

        nA = k.sbv([128, 16], F32, "nA")
        k.act("act", nA, ab[:, 0, :], AF.Exp)
        k.ts("dve", nA, nA, -1.0, ALU.mult)
        dtb = ab[:, 1, :]
        lbd = k.sbv([128, 16], F32, "lbd")
        k.tt("dve", lbd, lbl[:, 0, :], lbl[:, 1, :], ALU.subtract)
        lb = k.sbv([128, 16], F32, "lb")
        oml = k.sbv([128, 16], F32, "oml")
        k.act("act", lb, lbd, AF.Sigmoid)
        k.act("act", oml, lbd, AF.Sigmoid, scale=-1.0)
        noml = k.sbv([128, 16], F32, "noml")
        k.ts("dve", noml, oml, -1.0, ALU.mult)

        modfm = k.sbv([128, 48, 2], F32, "modfm")
        grow = k.sbv([128, 2, D], F32, "grow")
        wada_v = wada_d.rearrange("(kt p) n -> p kt n", p=128)
        pp_mod = PBIG
        with ExitStack() as es2:
            es_save = k.es
            k.es = es2
            wring = k.ring(4, [128, 8, 512], F32)
            cbc = k.sbv([128, 8, 128], F32, "cbc")
            k.cp("dve", cbc, csl[:, :, 0:1].bc([128, 8, 128]))
            bgt = k.sbv([128, 2, D], F32, "bgt")
            k.dma("sp", bgt, bgt_d)
            g13 = k.sbv([128, 2, D], F32, "g13")
            k.dma("sp", g13, g13_d)
            for ch in (2, 3, 0, 1, 6, 7, 8, 9, 4, 5, 10, 11):
                wt = wring.get()
                k.dma("sp", wt[:, 0:4, :], wada_v[:, 0:4, ch * 512:(ch + 1) * 512])
                k.dma("sp", wt[:, 4:8, :], wada_v[:, 4:8, ch * 512:(ch + 1) * 512])
                pm = pp_mod.get()
                for j in range(4):
                    ct = ch * 4 + j
                    for kt in range(8):
                        k.mm(pm[:, 2 * j:2 * j + 2], wt[:, kt, j * 128:(j + 1) * 128], csl[:, kt, :],
                             start=(kt == 0), stop=(kt == 7))
                for j in range(4):
                    ct = ch * 4 + j
                    k.ts("dve", modfm[:, ct, :], pm[:, 2 * j:2 * j + 2], badafm[:, ct:ct + 1], ALU.add)
                if ch in (4, 5, 10, 11):
                    gi = 0 if ch < 6 else 1
                    c0 = (ch % 2) * 512
                    pg = pp_mod.get()
                    for kt in range(8):
                        k.mm(pg, cbc[:, kt, :], wt[:, kt, :], start=(kt == 0), stop=(kt == 7))
                    k.tt("dve", grow[:, gi, c0:c0 + 512], pg, bgt[:, gi, c0:c0 + 512], ALU.add)
            k.tt("dve", grow[:, 0, :], grow[:, 0, :], g13[:, 0, :], ALU.mult)
            k.tt("dve", grow[:, 1, :], grow[:, 1, :], g13[:, 1, :], ALU.mult)
            k.barrier()
            k.es = es_save
        A0 = k.sbv([128, 8, 2], F32, "A0")
        A2 = k.sbv([128, 8, 2], F32, "A2")
        for c_ in range(2):
            k.stt("dve", A0[:, :, c_], modfm[:, 8:16, c_], 1.0, g02[:, 0, :], ALU.add, ALU.mult)
            k.stt("dve", A2[:, :, c_], modfm[:, 32:40, c_], 1.0, g02[:, 1, :], ALU.add, ALU.mult)
        B0 = modfm[:, 0:8, :]
        B2 = modfm[:, 24:32, :]
        dbg("modfm", modfm, [128, 48, 2])
        dbg("grow", grow, [128, 2, D])

        hxT_t = k.sb([128, 8, NTOK], BF16, "hxT")
        hx = [V(hxT_t[:, :, t * 128:(t + 1) * 128], R()) for t in range(NT)]
        epsc = k.sbv([128, 1], F32, "epsc")
        k.memset("dve", epsc, EPS)
        onec = k.sbv([128, 1], F32, "onec")
        k.memset("dve", onec, 1.0)

        def hxs(c0, n):
            ts_ = tuple(hx[t].res for t in range(c0 // 128, (c0 + n + 127) // 128))
            return lambda kt: V(hxT_t[:, kt, c0:c0 + n], ts_)

        def rsqrt(out, in_, scale):
            k.act("act", out, in_, AF.Sqrt, bias=epsc[0:out.ap.shape[0], :], scale=scale)
            k.op("dve", lambda en: en.reciprocal(out=out.ap, in_=out.ap), [out], [out])

        def rsqrt_le(out, in_, scale):
            k.act("act", out, in_, AF.Ln, bias=epsc[0:out.ap.shape[0], :], scale=scale)
            k.act("act", out, out, AF.Exp, scale=-0.5)

        def norm_T(xt, sq, st, dst, A, Bm, col, eng_flip):
            k.tt("dve", sq, xt, xt, ALU.mult)
            k.red("dve", st[:, 0:1], sq)
            rsqrt(st[:, 1:2], st[:, 0:1], 1.0 / D)
            k.act("act", sq, xt, AF.Copy, scale=st[:, 1:2])
            for half in range(2):
                p = PBIG.get()
                for j in range(4):
                    kt = half * 4 + j
                    k.tr(p[:, j * 128:(j + 1) * 128], sq[:, kt * 128:(kt + 1) * 128], I_f)
                for j in range(4):
                    kt = half * 4 + j
                    if (kt + eng_flip) % 2 == 0:
                        k.act("act", dst[:, kt, :], p[:, j * 128:(j + 1) * 128], AF.Identity,
                              bias=Bm[:, kt, col:col + 1], scale=A[:, kt, col:col + 1])
                    else:
                        k.ts("dve", dst[:, kt, :], p[:, j * 128:(j + 1) * 128], A[:, kt, col:col + 1], ALU.mult,
                             Bm[:, kt, col:col + 1], ALU.add)

        with Scope():
            xring = k.ring(6, [128, D], F32)
            sqring = k.ring(3, [128, D], F32)
            stat = k.ring(4, [128, 2], F32)
            for t in range(NT):
                xt = xring.get()
                src = ctx_d[t * 128:(t + 1) * 128, :] if t < NCTX else x_d[(t - NCTX) * 128:(t - NCTX + 1) * 128, :]
                k.dma("sp", xt, src)
                norm_T(xt, sqring.get(), stat.get(), hx[t], A0, B0, 1 if t < NCTX else 0, t)
            dbg("hxT", V(hxT_t[:, :, 0:512], hx[0].res), [128, 8, 512])

        win_v = win_d.rearrange("(kt p) n -> p kt n", p=128)
        def _phase2plus():
            ydr = nc.dram_tensor("ydr", [2, NH, 128, NOWN * 128], BF16, kind="Internal").ap()
            ydr_res = [[R() for _ in range(NH)] for _ in range(2)]

            wring = k.ring(5, [128, 8, 128], BF16)

            wst = k.ring(2, [128, 8, 128], F32)

            def load_w(col0):
                st_ = wst.get()
                k.dma("sp", st_, win_v[:, :, col0:col0 + 128])
                w = wring.get()
                k.cp("dve", w, st_)
                return w

            def project_fm(w, c0, n, dst_fn):
                hv = hxs(c0, n)
                p = PBIG.get()
                for kt in range(8):
                    k.mm(p[:, 0:n], w[:, kt, :], hv(kt), start=(kt == 0), stop=(kt == 7))
                dst_fn(p[:, 0:n])

            with Scope():
                names = ("la", "lnb", "negg", "bsc", "bg", "ed", "dl")
                sm = {n: k.sbv([128, NT, 16], F32, "sm_" + n) for n in names}
                sm["la"].res = tuple(R() for _ in range(NT))
                smres = [R() for _ in range(NT)]
                with Scope():
                    wab = k.sbv([128, 8, 32], BF16, "wab")
                    wab32 = k.sbv([128, 8, 32], F32, "wab32")
                    k.dma("sp", wab32, win_v[:, :, O_A:O_A + 32])
                    k.cp("dve", wab, wab32)
                    tmp = k.ring(6, [128, 32], F32)
                    for t in range(NT):
                        hv = hxs(t * 128, 128)
                        pab = PSM.get()
                        for kt in range(8):
                            k.mm(pab[:, 0:32], hv(kt), wab[:, kt, :], start=(kt == 0), stop=(kt == 7))
                        xx = tmp.get()
                        k.tt("dve", xx[:, 0:16], pab[:, 0:16], dtb, ALU.add)
                        k.ts("dve", xx[:, 16:32], pab[:, 16:32], -1.0, ALU.mult)
                        if t == 0:
                            dbg("d_wab", wab, [128, 8, 32])
                            dbg("d_wab32", wab32, [128, 8, 32])
                            dbg("d_hx0", hx[0], [128, 8, 128])
                            dbg("d_xx", xx, [128, 32])
                            dbg("d_nA", nA, [128, 16])
                            dbg("d_ab", ab, [128, 2, 16])
                        aa = tmp.get()
                        k.ts("dve", aa, xx, -1.0, ALU.mult)
                        k.tt("dve", aa, aa, xx, ALU.max)
                        if t == 0:
                            dbg("d_abs", aa, [128, 32])
                        k.act("act", aa, aa, AF.Exp, scale=-1.0)
                        if t == 0:
                            dbg("d_exp", aa, [128, 32])
                        k.act("act", aa, aa, AF.Ln, bias=onec)
                        if t == 0:
                            dbg("d_ln", aa, [128, 32])
                        k.ts("dve", xx, xx, 0.0, ALU.max)
                        k.tt("dve", xx, xx, aa, ALU.add)
                        sres = smres[t]

                        def S_(n):
                            return V(sm[n].ap[:, t, :], sres)
                        la_t = V(sm["la"].ap[:, t, :], sm["la"].res[t])
                        k.tt("dve", la_t, xx[:, 0:16], nA, ALU.mult)
                        k.ts("dve", S_("lnb"), xx[:, 16:32], -1.0, ALU.mult)
                        pg = PSM.get()
                        k.mm(pg[:, 0:8], C(C_UF), la_t[:, 0:8])
                        k.mm(pg[:, 8:16], C(C_UB), la_t[:, 8:16])
                        k.mm(pg[:, 16:32], ONE_f, la_t)
                        gg = tmp.get()
                        k.cp("dve", gg, pg[:, 0:32])
                        k.ts("dve", S_("negg"), gg[:, 0:16], -1.0, ALU.mult)
                        k.act("act", S_("bsc"), S_("lnb"), AF.Exp)
                        t2 = tmp.get()
                        k.tt("dve", t2[:, 0:16], S_("lnb"), gg[:, 0:16], ALU.add)
                        k.act("act", S_("bg"), t2[:, 0:16], AF.Exp)
                        k.tt("dve", t2[:, 16:32], gg[:, 16:32], gg[:, 0:16], ALU.subtract)
                        k.act("act", S_("ed"), t2[:, 16:32], AF.Exp)
                        k.act("act", S_("dl"), gg[:, 16:32], AF.Exp)

                if cutn == 1:
                    for n_ in names:
                        dbg("sm_" + n_, V(sm[n_].ap, tuple(smres) + tuple(sm["la"].res)), [128, NT, 16])
                cut(1)

                def SM(n, t, hd):
                    r = sm["la"].res[t] if n == "la" else smres[t]
                    return V(sm[n].ap[:, t, hd:hd + 1], r)

                NSLOT = 4
                t128s = [k.ring(8, [128, 128], F32) for _ in range(NSLOT)]
                b128s = [k.ring(8, [128, 128], BF16) for _ in range(NSLOT)]

                def run_units(unit_fn, arglist, nslots):
                    pending = list(enumerate(arglist))
                    active = []
                    free = list(range(nslots))
                    next_chain = 0
                    while pending or active:
                        while pending and free:
                            idx, a_ = pending.pop(0)
                            sl = free.pop(0)
                            active.append([idx, unit_fn(*a_, slot=sl), sl, "prep"])
                        for item in list(active):
                            idx, g, sl, st = item
                            if st == "wait" and idx != next_chain:
                                continue
                            try:
                                r_ = next(g)
                                if r_ == "chain":
                                    item[3] = "wait"
                            except StopIteration:
                                active.remove(item)
                                free.append(sl)
                                next_chain += 1
                with Scope():
                    pre_loc = k.sbv([128, 66, 66], BF16, "pre_loc")
                    pre_ctx = k.sbv([128, 258], BF16, "pre_ctx")
                    k.memset("pool", pre_loc, 0.0)
                    k.memset("pool", pre_ctx, 0.0)
                    dgr = k.ring(1, [128, 9, 128], BF16)
                    KT = k.sbv([128, NTOK], BF16, "KT")
                    QT = k.sbv([128, NOWN * 128], BF16, "QT")
                    ktok = k.sbv([128, NT, 128], BF16, "ktok")
                    vtok = k.sbv([128, NT, 128], BF16, "vtok")
                    gaT = k.sbv([128, NOWN * 128], BF16, "gaT")
                    oBT = k.sbv([128, NOWN * 128], BF16, "oBT")
                    yT = k.ring(1, [128, NOWN * 128], BF16)
                    f512 = k.ring(2, [128, 512], F32)
                    g512 = k.ring(2, [128, 512], F32)
                    vt512 = k.ring(2, [128, 512], BF16)
                    S32 = [k.sbv([128, 128], F32, "S32_%d" % d) for d in range(2)]
                    Sbf = [k.ring(2, [128, 128], BF16) for d in range(2)]

                    def conv_head(w, ct, which):
                        dg = dgr.get()
                        for tap in range(9):
                            k.ts("dve", dg[:, tap, :], I_bf, convw[:, ct, tap:tap + 1], ALU.mult)
                        cut(21)
                        if which != "q":
                            project_fm(w, 0, 256, lambda p: k.cp("act", pre_ctx[:, 1:257], p))
                        for g in range(8 if which != "q" else 5):
                            project_fm(w, TOK0 + 512 * g, 512,
                                       lambda p: k.cp("act", pre_loc[:, 1 + 8 * g:9 + 8 * g, 1:65],
                                                      p.re("p (a b) -> p a b", b=64)))
                        cut(22)
                        def stage1(g):
                            pc = PBIG.get()
                            if g < 0:
                                for kw in range(3):
                                    k.mm(pc[:, 0:256], dg[:, 3 + kw, :], pre_ctx[:, kw:kw + 256], start=(kw == 0), stop=(kw == 2))
                            else:
                                pc3 = pc.re("p (a b) -> p a b", b=64)
                                for tap in range(9):
                                    kh, kw = tap // 3, tap % 3
                                    k.mm(pc3, dg[:, tap, :], pre_loc[:, 8 * g + kh:8 * g + kh + 8, kw:kw + 64],
                                         start=(tap == 0), stop=(tap == 8))
                            return pc

                        def stage2(g, pc):
                            n = 256 if g < 0 else 512
                            c0 = 0 if g < 0 else TOK0 + 512 * g
                            if which == "v":
                                vt = vt512.get()
                                k.act("act", vt[:, 0:n], pc[:, 0:n], AF.Silu)
                                for j in range(n // 128):
                                    pb = PBF.get()
                                    k.tr(pb, vt[:, j * 128:(j + 1) * 128], I_bf)
                                    k.cp("dve", vtok[:, c0 // 128 + j, :], pb)
                            else:
                                qs = f512.get()
                                k.act("act", qs[:, 0:n], pc[:, 0:n], AF.Silu)
                                sq = g512.get()
                                k.tt("dve", sq[:, 0:n], qs[:, 0:n], qs[:, 0:n], ALU.mult)
                                pss = PBIG.get()
                                k.mm(pss[:, 0:n], ONE_f, sq[:, 0:n])
                                k.act("act", sq[:, 0:n], pss[:, 0:n], AF.Ln, bias=epsc)
                                k.act("act", sq[:, 0:n], sq[:, 0:n], AF.Exp, scale=-0.5)
                                if which == "k":
                                    k.tt("dve", KT[:, c0:c0 + n], qs[:, 0:n], sq[:, 0:n], ALU.mult)
                                    for j in range(n // 128):
                                        pb = PBF.get()
                                        k.tr(pb, KT[:, c0 + j * 128:c0 + (j + 1) * 128], I_bf)
                                        k.cp("dve", ktok[:, c0 // 128 + j, :], pb)
                                else:
                                    k.stt("dve", QT[:, c0 - TOK0:c0 - TOK0 + n], qs[:, 0:n], 128.0 ** -0.5, sq[:, 0:n],
                                          ALU.mult, ALU.mult)

                        prev_ = None
                        for g in (range(-1, 8) if which != "q" else range(0, 4)):
                            pc_ = stage1(g)
                            if prev_ is not None:
                                stage2(*prev_)
                            prev_ = (g, pc_)
                        stage2(*prev_)

                    def gdn_unit(h, d, t, full, own_idx, slot=0):
                        T1, B1 = t128s[slot], b128s[slot]
                        hd = d * 8 + h
                        UM = C(C_UF if d == 0 else C_UB)
                        NS = C(C_NSF if d == 0 else C_NSB)
                        NI = C(C_NIF if d == 0 else C_NIB)
                        la_bc = SM("la", t, hd).bc([128, 128])
                        lnb_bc = SM("lnb", t, hd).bc([128, 128])
                        negg = SM("negg", t, hd)
                        cs = slice(t * 128, (t + 1) * 128)
                        qs_ = slice(own_idx * 128, (own_idx + 1) * 128)
                        pA = PSM.get()
                        k.mm(pA, la_bc, UM, start=True, stop=False)
                        k.mm(pA, lnb_bc, I_f, start=False, stop=False)
                        k.mm(pA, I_f, NS, start=False, stop=True)
                        LA = T1.get()
                        k.act("act", LA, pA, AF.Exp, bias=negg)
                        yield
                        pK = PSM.get()
                        k.mm(pK, KT[:, cs], KT[:, cs])
                        Xt = T1.get()
                        k.stt("dve", Xt, pK, -1.0, LA, ALU.mult, ALU.mult)
                        yield
                        pb = PSM.get()
                        k.tr(pb, Xt, I_f)
                        Xn = T1.get()
                        k.cp("act", Xn, pb)
                        yield
                        Rk = T1.get()
                        k.tt("dve", Rk, Xt, I_f, ALU.add)
                        yield
                        for lvl in range(1, 7):
                            pn = PSM.get()
                            k.mm(pn, Xt, Xn)
                            Xn2 = T1.get()
                            k.cp("act", Xn2, pn)
                            yield
                            if lvl < 6:
                                pt_ = PSM.get()
                                k.mm(pt_, Xn, Xt)
                                Xt2 = T1.get()
                                k.cp("dve", Xt2, pt_)
                                yield
                            pr = PSM.get()
                            k.mm(pr, Xn2, Rk)
                            Rk2 = T1.get()
                            k.tt("dve", Rk2, pr, Rk, ALU.add)
                            yield
                            Xn, Rk = Xn2, Rk2
                            if lvl < 6:
                                Xt = Xt2
                        Tb = B1.get()
                        k.act("act", Tb, Rk, AF.Copy, scale=SM("bsc", t, hd))
                        yield
                        Tbg = B1.get()
                        k.act("act", Tbg, Rk, AF.Copy, scale=SM("bg", t, hd))
                        yield
                        pu = PSM.get()
                        k.mm(pu, Tb, vtok[:, t, :])
                        u32 = T1.get()
                        k.cp("act", u32, pu)
                        yield
                        pw = PSM.get()
                        k.mm(pw, ktok[:, t, :], Tbg)
                        wT = B1.get()
                        k.cp("dve", wT, pw)
                        yield
                        kdec = B1.get()
                        k.ts("dve", kdec, ktok[:, t, :], SM("ed", t, hd), ALU.mult)
                        yield
                        if full:
                            pL = PSM.get()
                            k.mm(pL, la_bc, UM, start=True, stop=False)
                            k.mm(pL, I_f, NI, start=False, stop=True)
                            Lt = T1.get()
                            k.act("act", Lt, pL, AF.Exp, bias=negg)
                            yield
                            pQ = PSM.get()
                            k.mm(pQ, KT[:, cs], QT[:, qs_])
                            attnT = B1.get()
                            k.tt("dve", attnT, pQ, Lt, ALU.mult)
                            yield
                            pE = PSM.get()
                            k.mm(pE, la_bc, UM)
                            Er = B1.get()
                            k.act("act", Er, pE, AF.Exp)
                            yield
                            qgT = B1.get()
                            k.tt("dve", qgT, QT[:, qs_], Er, ALU.mult)
                            yield
                        if h == 0 and d == 1 and t == 1:
                            dbg("LA", LA, [128, 128])
                            dbg("Rk", Rk, [128, 128])
                            dbg("Tb", Tb, [128, 128])
                            dbg("u32", u32, [128, 128])
                            dbg("wT", wT, [128, 128])
                            dbg("kdec", kdec, [128, 128])
                            dbg("smla", sm["la"][:, 1, :], [128, 16])
                            dbg("smnegg", V(sm["negg"].ap[:, 1, :], smres[1]), [128, 16])
                            dbg("smbg", V(sm["bg"].ap[:, 1, :], smres[1]), [128, 16])
                            dbg("smed", V(sm["ed"].ap[:, 1, :], smres[1]), [128, 16])
                        yield "chain"
                        Sb = Sbf[d].items[(Sbf[d].i - 1) % 2]
                        pws = PSM.get()
                        k.mm(pws, wT, Sb)
                        vnew = B1.get()
                        k.tt("dve", vnew, u32, pws, ALU.subtract)
                        yield
                        if full:
                            po = PSM.get()
                            k.mm(po, Sb, qgT, start=True, stop=False)
                            k.mm(po, vnew, attnT, start=False, stop=True)
                            oc = slice(own_idx * 128, (own_idx + 1) * 128)
                            if d == 1:
                                k.cp("act", oBT[:, oc], po)
                                yield
                            else:
                                o32 = T1.get()
                                k.tt("dve", o32, po, oBT[:, oc], ALU.add)
                                yield
                                sq = T1.get()
                                k.tt("dve", sq, o32, o32, ALU.mult)
                                yield
                                pss = PSM.get()
                                k.mm(pss, ONE_f, sq)
                                rn = T1.get()
                                rsqrt_le(rn, pss, 1.0 / 128)
                                k.tt("dve", o32, o32, rn, ALU.mult)
                                yield
                                k.stt("dve", ycur[:, oc], o32, nw[:, 0:1], gaT[:, oc], ALU.mult, ALU.mult)
                                yield
                        pS = PSM.get()
                        k.mm(pS, kdec, vnew)
                        k.stt("dve", S32[d], S32[d], SM("dl", t, hd), pS, ALU.mult, ALU.add)
                        yield
                        Sn = Sbf[d].get()
                        k.cp("dve", Sn, S32[d])
                        yield

                    for h in range(NH if stage >= 3 else 1):
                        wq, wk, wv, wga = (load_w(O_Q + h * 128), load_w(O_K + h * 128),
                                           load_w(O_V + h * 128), load_w(O_GA + h * 128))
                        conv_head(wk, 8 + h, "k")
                        if h == 0:
                            dbg("wk", wk, [128, 8, 128])
                            dbg("wst", wst.items[1], [128, 8, 128])
                            dbg("prectx", pre_ctx, [128, 258])
                            dbg("preloc", pre_loc[:, 0:4, :], [128, 4, 66])
                            dbg("dg", dgr.items[0], [128, 9, 128])
                        cut(2)
                        conv_head(wv, 16 + h, "v")
                        conv_head(wq, h, "q")
                        cut(3)
                        for g in range(4):
                            project_fm(wga, TOK0 + 512 * g, 512,
                                       lambda p: k.act("act", gaT[:, 512 * g:512 * (g + 1)], p, AF.Silu))
                        ycur = yT.get()
                        allseq = []
                        for d in (1, 0):
                            k.memset("pool", S32[d], 0.0)
                            k.memset("pool", Sbf[d].get(), 0.0)
                            seq = []
                            if d == 0:
                                seq += [(t, False, -1) for t in range(NCTX)]
                                seq += [(NCTX + t, True, t) for t in range(NOWN)]
                            else:
                                seq += [(t, False, -1) for t in reversed(range(NCTX))]
                                seq += [(NCTX + t, False, -1) for t in reversed(range(NOWN, NLOC))]
                                seq += [(NCTX + t, True, t) for t in reversed(range(NOWN))]
                            allseq += [(h, d, t, full, oi) for (t, full, oi) in seq]
                        run_units(gdn_unit, allseq, 4)
                        if h == 0:
                            dbg("KT", KT[:, 0:1024], [128, 1024])
                            dbg("vtok", vtok[:, 0:4, :], [128, 4, 128])
                            dbg("ya0", ycur, [128, NOWN * 128])
                            dbg("S32", S32[0], [128, 128])
                        k.dma("sp", V(ydr[0, h], ydr_res[0][h]), ycur)
                cut(7)

                with Scope():
                    vtokH = k.sbv([128, NT, 128], BF16, "vtokH")
                    qbT = k.sbv([128, NOWN * 128], BF16, "qbT")
                    gbT = k.sbv([128, NOWN * 128], BF16, "gbT")
                    oBT = k.sbv([128, NOWN * 128], BF16, "oBTh")
                    yT = k.ring(2, [128, NOWN * 128], BF16)
                    S32 = [k.sbv([128, 128], F32, "HS32_%d" % d) for d in range(2)]
                    Sbf = [k.ring(2, [128, 128], BF16) for d in range(2)]
                    vmall = k.sbv([128, NT, 4, 128], BF16, "vmall")
                    ebrs = [k.ring(1, [128, 4], F32) for _ in range(3)]
                    RST = C(C_RST)
                    PM = C(C_SUB)

                    def hgrn_unit(h, d, t, full, own_idx, wf, slot=0):
                        T1, B1 = t128s[slot], b128s[slot]
                        hd = d * 8 + h
                        HM = C(C_HMF if d == 0 else C_HMB)
                        lbc, omlc, nomlc = lb[:, hd:hd + 1], oml[:, hd:hd + 1], noml[:, hd:hd + 1]
                        hv = hxs(t * 128, 128)
                        pf = PSM.get()
                        for kt in range(8):
                            k.mm(pf, wf[:, kt, :], hv(kt), start=(kt == 0), stop=(kt == 7))
                        sig = T1.get()
                        k.act("act", sig, pf, AF.Exp, scale=-1.0)
                        yield
                        k.ts("dve", sig, sig, 1.0, ALU.add)
                        yield
                        k.op("dve", lambda en: en.reciprocal(out=sig.ap, in_=sig.ap), [sig], [sig])
                        yield
                        lf = T1.get()
                        k.act("act", lf, sig, AF.Ln, bias=lbc, scale=omlc)
                        yield
                        kb = T1.get()
                        k.ts("dve", kb, sig, nomlc, ALU.mult, omlc, ALU.add)
                        yield
                        bb = T1.get()
                        k.scan(bb, RST, lf, 0.0, ALU.mult, ALU.add)
                        yield
                        if d == 1:
                            b2 = T1.get()
                            k.stt("dve", b2, bb, -1.0, lf, ALU.mult, ALU.add)
                            yield
                            for c in range(4):
                                k.ts("dve", b2[:, 32 * c:32 * c + 32], b2[:, 32 * c:32 * c + 32],
                                     bb[:, 32 * c + 31:32 * c + 32], ALU.add)
                            bb = b2
                        lastcol = [(32 * c + 31) if d == 0 else (32 * c) for c in range(4)]
                        Dm = T1.get()
                        for c in range(4):
                            k.ts("dve", Dm[:, 32 * c:32 * c + 32], bb[:, 32 * c:32 * c + 32],
                                 bb[:, 32 * c + 15:32 * c + 16], ALU.subtract)
                        EK = T1.get()
                        k.act("act", EK, Dm, AF.Exp, scale=-1.0)
                        yield
                        Kp = B1.get()
                        k.tt("dve", Kp, kb, EK, ALU.mult)
                        yield
                        ED = T1.get()
                        eb = ebrs[slot].get()
                        for c in range(4):
                            lc = lastcol[c]
                            k.act("act", ED[:, 32 * c:32 * c + 32], bb[:, 32 * c:32 * c + 32], AF.Exp,
                                  bias=bb[:, lc:lc + 1], scale=-1.0)
                            yield
                        k.act("act", eb, bb[:, lastcol[0]:128:32], AF.Exp)
                        yield
                        kdT = B1.get()
                        k.tt("dve", kdT, kb, ED, ALU.mult)
                        yield
                        pb = PBF.get()
                        k.tr(pb, kdT, I_bf)
                        kdtok = B1.get()
                        k.cp("dve", kdtok, pb)
                        yield
                        pG = PBIG.get()
                        for c in range(4):
                            k.mm(pG[:, 128 * c:128 * c + 128], kdtok, vmall[:, t, c, :])
                        if full:
                            oc = slice(own_idx * 128, (own_idx + 1) * 128)
                            EQ = T1.get()
                            k.act("act", EQ, Dm, AF.Exp)
                            yield
                            Qp = B1.get()
                            k.tt("dve", Qp, qbT[:, oc], EQ, ALU.mult)
                            yield
                            pa = PSM.get()
                            k.mm(pa, Kp, Qp)
                            aT = B1.get()
                            k.tt("dve", aT, pa, HM, ALU.mult)
                            yield
                            EB = T1.get()
                            k.act("act", EB, bb, AF.Exp)
                            yield
                            Qb = B1.get()
                            k.tt("dve", Qb, qbT[:, oc], EB, ALU.mult)
                            yield
                            po = PSM.get()
                        yield "chain"
                        for c in (range(4) if d == 0 else reversed(range(4))):
                            Sb = Sbf[d].items[(Sbf[d].i - 1) % 2]
                            cs = slice(32 * c, 32 * c + 32)
                            if full:
                                k.mm(po[:, cs], Sb, Qb[:, cs], start=True, stop=False)
                                k.mm(po[:, cs], vtokH[:, t, :], aT[:, cs], start=False, stop=True)
                            k.stt("dve", S32[d], S32[d], eb[:, c:c + 1], pG[:, 128 * c:128 * c + 128], ALU.mult, ALU.add)
                            yield
                            Sn = Sbf[d].get()
                            k.cp("act", Sn, S32[d])
                            yield
                        if full:
                            if d == 1:
                                k.cp("act", oBT[:, oc], po)
                                yield
                            else:
                                o32 = T1.get()
                                k.tt("dve", o32, po, oBT[:, oc], ALU.add)
                                yield
                                sq = T1.get()
                                k.tt("dve", sq, o32, o32, ALU.mult)
                                yield
                                pss = PSM.get()
                                k.mm(pss, ONE_f, sq)
                                rn = T1.get()
                                rsqrt_le(rn, pss, 1.0 / 128)
                                k.tt("dve", o32, o32, rn, ALU.mult)
                                yield
                                k.stt("dve", ycurH[:, oc], o32, nw[:, 1:2], gbT[:, oc], ALU.mult, ALU.mult)
                                yield

                    for h in range(NH if stage >= 3 else 1):
                        wfF, wfB, wqb, wib, wgb = (load_w(O_F + h * 128), load_w(O_F + 1024 + h * 128),
                                                   load_w(O_QB + h * 128), load_w(O_IB + h * 128),
                                                   load_w(O_GB + h * 128))
                        for t in range(NT):
                            hv = hxs(t * 128, 128)
                            pv = PSM.get()
                            for kt in range(8):
                                k.mm(pv, hv(kt), wib[:, kt, :], start=(kt == 0), stop=(kt == 7))
                            k.cp("act", vtokH[:, t, :], pv)
                            for c in range(4):
                                k.ts("dve", vmall[:, t, c, :], vtokH[:, t, :], PM[:, c:c + 1], ALU.mult)
                        for g in range(4):
                            project_fm(wqb, TOK0 + 512 * g, 512,
                                       lambda p: k.act("act", qbT[:, 512 * g:512 * (g + 1)], p, AF.Silu))
                            project_fm(wgb, TOK0 + 512 * g, 512,
                                       lambda p: k.act("act", gbT[:, 512 * g:512 * (g + 1)], p, AF.Silu))
                        k.ts("dve", qbT, qbT, 128.0 ** -0.5, ALU.mult)
                        ycurH = yT.get()
                        allseq = []
                        for d in (1, 0):
                            k.memset("dve", S32[d], 0.0)
                            k.memset("dve", Sbf[d].get(), 0.0)
                            seq = []
                            if d == 0:
                                seq += [(t, False, -1) for t in range(NCTX)]
                                seq += [(NCTX + t, True, t) for t in range(NOWN)]
                            else:
                                seq += [(t, False, -1) for t in reversed(range(NCTX))]
                                seq += [(NCTX + t, False, -1) for t in reversed(range(NOWN, NLOC))]
                                seq += [(NCTX + t, True, t) for t in reversed(range(NOWN))]
                            allseq += [(h, d, t, full, oi, wfF if d == 0 else wfB) for (t, full, oi) in seq]
                        run_units(hgrn_unit, allseq, 3)
                        if h == 0:
                            dbg("yb0", ycurH, [128, NOWN * 128])
                            dbg("HS", S32[0], [128, 128])
                        k.dma("sp", V(ydr[1, h], ydr_res[1][h]), ycurH)
            cut(8)

            yres = [R() for _ in range(NOWN)]
            h2 = [V(hxT_t[:, :, TOK0 + (NOWN + t) * 128:TOK0 + (NOWN + t + 1) * 128], hx[NCTX + NOWN + t].res)
                  for t in range(NOWN)]

            def load_cols(dst, src_d, ncol_tiles):
                sv = src_d.rearrange("(kt p) n -> p kt n", p=128)
                for ct in range(ncol_tiles):
                    st_ = wst.get()
                    k.dma("sp", st_, sv[:, :, ct * 128:(ct + 1) * 128])
                    k.cp("dve", dst[:, :, ct * 128:(ct + 1) * 128], st_)

            with Scope():
                Wa = k.sbv([128, 8, D], BF16, "Wa")
                Wb = k.sbv([128, 8, D], BF16, "Wb")
                Wo = k.sbv([128, 8, D], BF16, "Wo")
                load_cols(Wa, wa_d, 8)
                load_cols(Wb, wb_d, 8)
                load_cols(Wo, wo_d, 8)
                yA = k.sbv([128, NH, 512], BF16, "yA")
                yB = k.sbv([128, NH, 512], BF16, "yB")
                mgd = k.sbv([128, 8, 512], BF16, "mgd")
                xr = k.ring(1, [128, D], F32)
                o2r = k.ring(1, [128, D], F32)
                sqr = k.ring(1, [128, D], F32)
                str_ = k.ring(4, [128, 2], F32)
                sg = k.ring(1, [128, 512], F32)
                m1r = k.ring(2, [128, 512], F32)
                for G in range(4):
                    gc = slice(512 * G, 512 * (G + 1))
                    for h in range(NH):
                        k.dma("sp", yA[:, h, :], V(ydr[0, h][:, gc], ydr_res[0][h]))
                        k.dma("sp", yB[:, h, :], V(ydr[1, h][:, gc], ydr_res[1][h]))
                    hvG = hxs(TOK0 + 512 * G, 512)
                    for dt in range(8):
                        wga_ = load_w(O_MG + dt * 128)
                        wgb_ = load_w(O_MG + 1024 + dt * 128)
                        dc = slice(dt * 128, (dt + 1) * 128)
                        pga = PBIG.get()
                        for kt in range(8):
                            k.mm(pga, wga_[:, kt, :], hvG(kt), start=(kt == 0), stop=(kt == 7))
                        sga = sg.get()
                        k.act("act", sga, pga, AF.Sigmoid)
                        pa_ = PBIG.get()
                        for h in range(NH):
                            k.mm(pa_, Wa[:, h, dc], yA[:, h, :], start=(h == 0), stop=(h == NH - 1))
                        m1 = m1r.get()
                        k.tt("dve", m1, pa_, sga, ALU.mult)
                        pgb = PBIG.get()
                        for kt in range(8):
                            k.mm(pgb, wgb_[:, kt, :], hvG(kt), start=(kt == 0), stop=(kt == 7))
                        sgb = sg.get()
                        k.act("act", sgb, pgb, AF.Sigmoid)
                        pb_ = PBIG.get()
                        for h in range(NH):
                            k.mm(pb_, Wb[:, h, dc], yB[:, h, :], start=(h == 0), stop=(h == NH - 1))
                        m2 = m1r.get()
                        k.tt("dve", m2, pb_, sgb, ALU.mult)
                        k.tt("dve", mgd[:, dt, :], m1, m2, ALU.add)
                    if G == 0:
                        dbg("yA", yA, [128, NH, 512])
                        dbg("yB", yB, [128, NH, 512])
                        dbg("mgd", mgd, [128, 8, 512])
                    for j in range(4):
                        t = 4 * G + j
                        o2 = o2r.get()
                        for half in range(2):
                            po_ = PBIG.get()
                            for dt in range(8):
                                k.mm(po_, mgd[:, dt, j * 128:(j + 1) * 128], Wo[:, dt, half * 512:(half + 1) * 512],
                                     start=(dt == 0), stop=(dt == 7))
                            k.cp("act", o2[:, half * 512:(half + 1) * 512], po_)
                        sq = sqr.get()
                        st = str_.get()
                        if t == 0:
                            dbg("o2raw", o2, [128, D])
                        k.tt("dve", sq, o2, o2, ALU.mult)
                        k.red("dve", st[:, 0:1], sq)
                        rsqrt(st[:, 1:2], st[:, 0:1], 1.0 / D)
                        if t == 0:
                            dbg("o2st", st, [128, 2])
                        k.act("act", o2, o2, AF.Copy, scale=st[:, 1:2])
                        k.tt("dve", o2, o2, grow[:, 0, :], ALU.mult)
                        if t == 0:
                            dbg("o2sc", o2, [128, D])
                        xt = xr.get()
                        k.dma("sp", xt, x_d[t * 128:(t + 1) * 128, :])
                        k.tt("dve", xt, xt, o2, ALU.add)
                        k.dma("sp", V(y_d[t * 128:(t + 1) * 128, :], yres[t]), xt)
                        if t == 0:
                            dbg("x1t0", xt, [128, D])
                        norm_T(xt, sqr.get(), str_.get(), h2[t], A2, B2, 0, t)
            cut(9)

            w1_v = w1_d.rearrange("(kt p) n -> p kt n", p=128)
            w2_v = w2_d.rearrange("(ft p) n -> p ft n", p=128)
            with Scope():
                hid = k.sbv([128, 32, 512], BF16, "hid")
                w2r = k.ring(3, [128, 512], BF16)
                w2s = k.ring(3, [128, 512], F32)
                rl = k.ring(2, [128, 512], F32)
                xr = k.ring(1, [128, D], F32)
                o3r = k.ring(4, [128, D], F32)
                sqr = k.ring(1, [128, D], F32)
                str_ = k.ring(4, [128, 2], F32)
                PACC = [V(b.ap, b.res, True) for b in _psm]
                for G in range(4):
                    hvG = lambda kt: V(hxT_t[:, kt, TOK0 + (NOWN + 4 * G) * 128:TOK0 + (NOWN + 4 * G + 4) * 128],
                                       tuple(h2[4 * G + j].res for j in range(4)))
                    for ft in range(32):
                        st_ = wst.get()
                        k.dma("sp", st_, w1_v[:, :, ft * 128:(ft + 1) * 128])
                        w1 = wring.get()
                        k.cp("dve", w1, st_)
                        ph = PBIG.get()
                        for kt in range(8):
                            k.mm(ph, w1[:, kt, :], hvG(kt), start=(kt == 0), stop=(kt == 7))
                        r_ = rl.get()
                        k.act("act", r_, ph, AF.Relu)
                        k.tt("dve", hid[:, ft, :], r_, r_, ALU.mult)
                    o3s = [o3r.get() for _ in range(4)]
                    for half in range(2):
                        for ft in range(32):
                            s2 = w2s.get()
                            k.dma("sp", s2, w2_v[:, ft, half * 512:(half + 1) * 512])
                            w2 = w2r.get()
                            k.cp("dve", w2, s2)
                            for j in range(4):
                                k.mm(PACC[j], hid[:, ft, j * 128:(j + 1) * 128], w2, start=(ft == 0), stop=(ft == 31))
                        for j in range(4):
                            k.cp("act", o3s[j][:, half * 512:(half + 1) * 512], PACC[j])
                    for j in range(4):
                        t = 4 * G + j
                        o3 = o3s[j]
                        sq = sqr.get()
                        st = str_.get()
                        k.tt("dve", sq, o3, o3, ALU.mult)
                        k.red("dve", st[:, 0:1], sq)
                        rsqrt(st[:, 1:2], st[:, 0:1], 1.0 / D)
                        k.act("act", o3, o3, AF.Copy, scale=st[:, 1:2])
                        k.tt("dve", o3, o3, grow[:, 1, :], ALU.mult)
                        xt = xr.get()
                        yv = V(y_d[t * 128:(t + 1) * 128, :], yres[t])
                        k.dma("sp", xt, yv)
                        k.tt("dve", xt, xt, o3, ALU.add)
                        k.dma("sp", yv, xt, is_out=True)
            k.finish()
            raise Done()


        try:
            _phase2plus()
            raise Cut()
        except Done:
            return nc, dbg_d
        except Cut:
            z = V(hxT_t[:, 0, 0:2048].bitcast(F32), tuple(h_.res for h_ in hx))
            k.memset("dve", z, 0.0)
            for t in range(NOWN):
                k.dma("sp", y_d[t * 128:(t + 1) * 128, :], z, is_out=True)
        k.finish()
    return nc, dbg_d


def host_prep(inputs):
    f = np.float32
    x = np.asarray(inputs["x"], f)
    c = np.asarray(inputs["c"], f)
    ctx = np.asarray(inputs["ctx"], f)
    c_ctx = np.asarray(inputs["c_ctx"], f)
    w_ada = np.ascontiguousarray(np.asarray(inputs["w_ada"], f)[0])
    b_ada = np.asarray(inputs["b_ada"], f)[0]
    norm_g = np.asarray(inputs["norm_g"], f)[0]
    w_in = np.asarray(inputs["w_in"], f)[0]
    conv_w = np.asarray(inputs["conv_w"], f)[0]
    a_log = np.asarray(inputs["gdn_a_log"], f)[0]
    dt_b = np.asarray(inputs["gdn_dt_bias"], f)[0]
    gnw = np.asarray(inputs["gdn_norm_w"], f)[0]
    lbl = np.asarray(inputs["hgrn_lb_logits"], f)
    hnw = np.asarray(inputs["hgrn_norm_w"], f)[0]

    i = np.arange(128)
    consts = np.zeros((NCONST, 128, 128), f)
    consts[C_I] = np.eye(128)
    consts[C_ONE] = 1.0
    consts[C_UF] = (i[:, None] <= i[None, :])
    consts[C_UB] = (i[:, None] >= i[None, :])
    consts[C_NSF] = np.where(i[None, :] > i[:, None], 0.0, NEG)
    consts[C_NSB] = np.where(i[None, :] < i[:, None], 0.0, NEG)
    consts[C_NIF] = np.where(i[None, :] >= i[:, None], 0.0, NEG)
    consts[C_NIB] = np.where(i[None, :] <= i[:, None], 0.0, NEG)
    same = (i[:, None] // 32) == (i[None, :] // 32)
    consts[C_HMF] = same & (i[:, None] <= i[None, :])
    consts[C_HMB] = same & (i[:, None] >= i[None, :])
    consts[C_RST] = np.broadcast_to((i % 32 != 0).astype(f)[None, :], (128, 128))
    consts[C_SUB][:, 0:4] = ((i[:, None] // 32) == np.arange(4)[None, :])
    consts = np.ascontiguousarray(consts.transpose(1, 0, 2).reshape(128, NCONST * 128))

    bada_fm = np.ascontiguousarray(b_ada.reshape(48, 128).T)
    bgt = np.ascontiguousarray(np.broadcast_to(
        np.stack([b_ada[2 * D:3 * D], b_ada[5 * D:6 * D]])[None], (128, 2, D)))
    g02 = np.ascontiguousarray(np.stack([norm_g[0].reshape(8, 128).T, norm_g[2].reshape(8, 128).T], axis=1))
    g13 = np.ascontiguousarray(np.broadcast_to(np.stack([norm_g[1], norm_g[3]])[None], (128, 2, D)))
    nwv = np.ascontiguousarray(np.stack([gnw, hnw], axis=1))

    def perm_cols(flip):
        cols = []
        cols.append(np.arange(0, 4096))
        for base in (4096, 4112):
            idx = base + np.arange(16).reshape(2, 8)
            cols.append((idx[::-1] if flip else idx).reshape(-1))
        idx = 4128 + np.arange(2048).reshape(2, 1024)
        cols.append((idx[::-1] if flip else idx).reshape(-1))
        cols.append(np.arange(6176, IN_W))
        return np.concatenate(cols)

    maps = []
    shared = dict(w_ada=w_ada, bada_fm=bada_fm, bgt=bgt, g02=g02, g13=g13, nw=nwv,
                  w_a=np.ascontiguousarray(np.asarray(inputs["w_branch_a"], f)[0]),
                  w_b=np.ascontiguousarray(np.asarray(inputs["w_branch_b"], f)[0]),
                  w_o=np.ascontiguousarray(np.asarray(inputs["w_out"], f)[0]),
                  w_1=np.ascontiguousarray(np.asarray(inputs["w_mlp_in"], f)[0]),
                  w_2=np.ascontiguousarray(np.asarray(inputs["w_mlp_out"], f)[0]),
                  consts=consts)
    per_flip = {}
    for flip in (0, 1):
        win_l = np.ascontiguousarray(w_in[:, perm_cols(flip)])
        cw = conv_w[::-1, ::-1, :] if flip else conv_w
        convw = np.ascontiguousarray(cw.reshape(9, 24, 128).transpose(2, 1, 0))
        al = a_log[::-1] if flip else a_log
        db = dt_b[::-1] if flip else dt_b
        abv = np.ascontiguousarray(np.broadcast_to(np.stack([al.reshape(-1), db.reshape(-1)])[None], (128, 2, 16)))
        ll = lbl[:, ::-1, :] if flip else lbl
        lblv = np.ascontiguousarray(ll.reshape(2, 2, 8, 128).transpose(3, 0, 1, 2).reshape(128, 2, 16))
        per_flip[flip] = dict(w_in=win_l, convw=convw, ab=abv, lbl=lblv)
    for core in range(8):
        b, s = core // 2, core % 2
        xb = x[b][::-1] if s else x[b]
        cb = ctx[b][::-1] if s else ctx[b]
        cfm = np.ascontiguousarray(np.stack([c[b], c_ctx], axis=1).reshape(8, 128, 2).transpose(1, 0, 2))
        m = dict(shared)
        m.update(per_flip[s])
        m.update(x=np.ascontiguousarray(xb), ctx=np.ascontiguousarray(cb), cfm=cfm)
        maps.append(m)
    return maps


_NC_CACHE = {}


def kernel(**inputs):
    maps = host_prep(inputs)
    if "nc" not in _NC_CACHE:
        import os
        _NC_CACHE["nc"] = build(cutn=int(os.environ.get("KCUT", "0")))[0]
    nc = _NC_CACHE["nc"]
    res = run_bass_kernel_spmd(nc, maps, core_ids=list(range(8)))
    out = np.zeros((4, 4096, D), np.float32)
    for core in range(8):
        b, s = core // 2, core % 2
        y = np.asarray(res.results[core]["y"])
        if s:
            out[b, 2048:] = y[::-1]
        else:
            out[b, :2048] = y
    return out
```
